# Optimizing a Trainium2 kernel written in Bass

```python
import math
import jax
import jax.numpy as jnp
from jax import lax
import numpy as np


D_MODEL = 2048
BATCH = 8
SEQ = 4096
DEPTH = 2

GRID_W = 64
EPS = 1e-6
A_HEADS = 8
A_DK = 128
A_DV = 128
A_CHUNK = 64
CONV_W = 5
B_HEADS = 8
B_KV_HEADS = 2
B_HD = 128
Q_BLOCK = 128
ROPE_THETA = 10000.0
C_WIDTH = 2048
C_GROUPS = 16
C_CHUNK = 128

A_W = A_HEADS * A_DK
B_W = B_HEADS * B_HD
B_KV_W = B_KV_HEADS * B_HD
IN0_SIZES = (A_W, A_W, A_HEADS * A_DV, A_HEADS * A_DV, 4 * A_HEADS, B_W, B_KV_W, B_KV_W, B_W)

kernel_name = "hybrid_deltanet_gqa_gmlp_encoder"


def rms_norm(x, g):
    xf = x.astype(jnp.float32)
    y = xf * lax.rsqrt(jnp.mean(xf * xf, axis=-1, keepdims=True) + EPS)
    return (y * g.astype(jnp.float32)).astype(x.dtype)


def layer_norm(x, g, b):
    xf = x.astype(jnp.float32)
    mu = jnp.mean(xf, axis=-1, keepdims=True)
    var = jnp.mean(jnp.square(xf - mu), axis=-1, keepdims=True)
    y = (xf - mu) * lax.rsqrt(var + EPS) * g.astype(jnp.float32) + b.astype(jnp.float32)
    return y.astype(x.dtype)


def l2norm(t):
    tf = t.astype(jnp.float32)
    return tf * lax.rsqrt(jnp.sum(tf * tf, axis=-1, keepdims=True) + EPS)


def split_cols(t, sizes):
    offs = []
    acc = 0
    for s in sizes[:-1]:
        acc += s
        offs.append(acc)
    return jnp.split(t, offs, axis=-1)


def centred_dwconv(x, w):
    pad = w.shape[0] // 2
    return lax.conv_general_dilated(
        x, w[:, None, :].astype(x.dtype), window_strides=(1,), padding=[(pad, pad)],
        dimension_numbers=("NWC", "WIO", "NWC"), feature_group_count=x.shape[-1])


def gated_delta_chunked(q, k, v, g, beta):
    f32 = jnp.float32
    bsz, nh, L, dk = q.shape
    dv = v.shape[-1]
    C = A_CHUNK
    n = L // C
    q = (q.astype(f32) * (dk ** -0.5)).reshape(bsz, nh, n, C, dk)
    k = k.astype(f32).reshape(bsz, nh, n, C, dk)
    v = v.astype(f32).reshape(bsz, nh, n, C, dv)
    g = jnp.cumsum(g.astype(f32).reshape(bsz, nh, n, C), axis=-1)
    beta = beta.astype(f32).reshape(bsz, nh, n, C)
    idx = jnp.arange(C)
    incl = idx[:, None] >= idx[None, :]
    strict = idx[:, None] > idx[None, :]
    decay = jnp.exp(jnp.where(incl, g[..., :, None] - g[..., None, :], -jnp.inf))
    k_beta = k * beta[..., None]
    m = jnp.where(strict, jnp.einsum('bhnid,bhnjd->bhnij', k_beta, k) * decay, 0.0)
    eye = jnp.eye(C, dtype=f32)
    t_inv = lax.linalg.triangular_solve(eye + m, jnp.broadcast_to(eye, m.shape),
                                        left_side=True, lower=True, unit_diagonal=True)
    u = jnp.einsum('bhnij,bhnje->bhnie', t_inv, v * beta[..., None])
    w = jnp.einsum('bhnij,bhnjd->bhnid', t_inv, k_beta * jnp.exp(g)[..., None])
    attn = jnp.einsum('bhnid,bhnjd->bhnij', q, k) * decay
    q_dec = q * jnp.exp(g)[..., None]
    k_dec = k * jnp.exp(g[..., -1:] - g)[..., None]
    g_last = jnp.exp(g[..., -1])
    xs = tuple(jnp.moveaxis(t, 2, 0) for t in (q_dec, k_dec, u, w, attn, g_last))

    def step(S, inp):
        q_c, k_c, u_c, w_c, a_c, gl_c = inp
        v_new = u_c - jnp.einsum('bhcd,bhde->bhce', w_c, S)
        o_c = jnp.einsum('bhcd,bhde->bhce', q_c, S) + jnp.einsum('bhij,bhje->bhie', a_c, v_new)
        S = S * gl_c[..., None, None] + jnp.einsum('bhcd,bhce->bhde', k_c, v_new)
        return S, o_c

    S0 = jnp.zeros((bsz, nh, dk, dv), f32)
    _, o = lax.scan(step, S0, xs)
    return jnp.moveaxis(o, 0, 2).reshape(bsz, nh, L, dv)


def mixer_a(aq, ak, av, agate, alog_in, conv_w, a_log, dt_bias, onorm_g):
    bsz, L, _ = aq.shape
    qkv = jax.nn.silu(centred_dwconv(jnp.concatenate([aq, ak, av], axis=-1), conv_w))
    q, k, v = split_cols(qkv, (A_W, A_W, A_HEADS * A_DV))
    q = l2norm(jnp.transpose(q.reshape(bsz, L, A_HEADS, A_DK), (0, 2, 1, 3)))
    k = l2norm(jnp.transpose(k.reshape(bsz, L, A_HEADS, A_DK), (0, 2, 1, 3)))
    v = jnp.transpose(v.reshape(bsz, L, A_HEADS, A_DV), (0, 2, 1, 3))
    a_f, a_b, b_f, b_b = jnp.split(alog_in.astype(jnp.float32), 4, axis=-1)

    def log_decay(a, d):
        gd = -jnp.exp(a_log[d].astype(jnp.float32)) * jax.nn.softplus(a + dt_bias[d].astype(jnp.float32))
        return jnp.transpose(gd, (0, 2, 1))

    def flip(t):
        return jnp.flip(t, axis=2)

    o_f = gated_delta_chunked(q, k, v, log_decay(a_f, 0), jnp.transpose(jax.nn.sigmoid(b_f), (0, 2, 1)))
    o_b = flip(gated_delta_chunked(flip(q), flip(k), flip(v), flip(log_decay(a_b, 1)),
                                   flip(jnp.transpose(jax.nn.sigmoid(b_b), (0, 2, 1)))))
    o = jnp.transpose(o_f + o_b, (0, 2, 1, 3))
    o = rms_norm(o, onorm_g).reshape(bsz, L, A_HEADS * A_DV)
    return (o * jax.nn.silu(agate.astype(jnp.float32))).astype(aq.dtype)


def axial_rope_tables(L):
    rows = L // GRID_W
    t = jnp.arange(rows * GRID_W)
    row = (t // GRID_W).astype(jnp.float32)
    col = (t % GRID_W).astype(jnp.float32)
    n_freq = B_HD // 4
    inv = ROPE_THETA ** (-jnp.arange(n_freq, dtype=jnp.float32) / n_freq)
    ang = jnp.concatenate([row[:, None] * inv, col[:, None] * inv], axis=-1)
    return jnp.cos(ang), jnp.sin(ang)


def apply_axial_rope(x, cos, sin):
    xf = x.astype(jnp.float32)
    nf = B_HD // 4
    half = B_HD // 2
    c = cos[None, :, None, :]
    s = sin[None, :, None, :]

    def rot(xh, cc, ss):
        x1 = xh[..., :nf]
        x2 = xh[..., nf:]
        return jnp.concatenate([x1 * cc - x2 * ss, x2 * cc + x1 * ss], axis=-1)

    out = jnp.concatenate([rot(xf[..., :half], c[..., :nf], s[..., :nf]),
                           rot(xf[..., half:], c[..., nf:], s[..., nf:])], axis=-1)
    return out.astype(x.dtype)


def mixer_b(bq, bk, bv, bgate, qn_g, kn_g):
    bsz, L, _ = bq.shape
    grp = B_HEADS // B_KV_HEADS
    q = rms_norm(bq.reshape(bsz, L, B_HEADS, B_HD), qn_g)
    k = rms_norm(bk.reshape(bsz, L, B_KV_HEADS, B_HD), kn_g)
    v = bv.reshape(bsz, L, B_KV_HEADS, B_HD)
    cos, sin = axial_rope_tables(L)
    q = apply_axial_rope(q, cos, sin)
    k = apply_axial_rope(k, cos, sin)
    q = jnp.moveaxis(q.reshape(bsz, L // Q_BLOCK, Q_BLOCK, B_KV_HEADS, grp, B_HD), 1, 0)
    scale = B_HD ** -0.5

    def attend(qb):
        s = jnp.einsum('bqkgd,bskd->bkgqs', qb, k, preferred_element_type=jnp.float32) * scale
        p = jax.nn.softmax(s, axis=-1)
        return jnp.einsum('bkgqs,bskd->bqkgd', p.astype(v.dtype), v)

    o = lax.map(attend, q)
    o = jnp.moveaxis(o, 0, 1).reshape(bsz, L, B_W)
    return o * jax.nn.silu(bgate)


def hybrid_layer(x, norm_g, w_in, conv_w, a_log, dt_bias, a_onorm_g, qn_g, kn_g, w_out):
    h = rms_norm(x, norm_g)
    aq, ak, av, agate, alog_in, bq, bk, bv, bgate = split_cols(h @ w_in, IN0_SIZES)
    ya = mixer_a(aq, ak, av, agate, alog_in, conv_w, a_log, dt_bias, a_onorm_g)
    yb = mixer_b(bq, bk, bv, bgate, qn_g, kn_g)
    return x + jnp.concatenate([ya, yb.astype(ya.dtype)], axis=-1) @ w_out


def gmlp_layer(x, norm_g, w_in, ln_g, ln_b, w_s, b_s, w_out):
    bsz, L, _ = x.shape
    h = rms_norm(x, norm_g)
    u, v, gate = jnp.split(h @ w_in, 3, axis=-1)
    u = jax.nn.gelu(u, approximate=False)
    v = layer_norm(jax.nn.gelu(v, approximate=False), ln_g, ln_b)
    n = L // C_CHUNK
    cg = C_WIDTH // C_GROUPS
    vc = v.reshape(bsz, n, C_CHUNK, C_GROUPS, cg)
    s = jnp.einsum('gts,bnsgc->bntgc', w_s, vc) + jnp.transpose(b_s)[None, None, :, :, None]
    z = u * s.reshape(bsz, L, C_WIDTH) * jax.nn.silu(gate)
    return x + z @ w_out


def setup_inputs(seed: int = 0) -> dict:
    key = jax.random.key(seed)
    ks = jax.random.split(key, 20)
    ne = (DEPTH + 1) // 2
    no = DEPTH // 2
    f32 = jnp.float32

    def nrm(k, shape, scale):
        return jax.random.normal(k, shape, f32) * scale

    in0 = sum(IN0_SIZES)
    x = nrm(ks[0], (BATCH, SEQ, D_MODEL), 1.0)
    norm0_g = 1.0 + nrm(ks[1], (ne, D_MODEL), 0.02)
    w_in0 = nrm(ks[2], (ne, D_MODEL, in0), D_MODEL ** -0.5)
    conv0_w = nrm(ks[3], (ne, CONV_W, 3 * A_W), CONV_W ** -0.5)
    a_log0 = jnp.log(jax.random.uniform(ks[4], (ne, 2, A_HEADS), f32, 1.0, 16.0))
    dt = jnp.exp(jax.random.uniform(ks[5], (ne, 2, A_HEADS), f32, math.log(1e-3), math.log(1e-1)))
    dt_bias0 = dt + jnp.log(-jnp.expm1(-dt))
    a_onorm_g0 = 1.0 + nrm(ks[6], (ne, A_DV), 0.02)
    b_qnorm_g0 = 1.0 + nrm(ks[7], (ne, B_HD), 0.02)
    b_knorm_g0 = 1.0 + nrm(ks[8], (ne, B_HD), 0.02)
    w_out0 = nrm(ks[9], (ne, A_HEADS * A_DV + B_W, D_MODEL), (A_HEADS * A_DV + B_W) ** -0.5)
    norm1_g = 1.0 + nrm(ks[10], (no, D_MODEL), 0.02)
    w_in1 = nrm(ks[11], (no, D_MODEL, 3 * C_WIDTH), D_MODEL ** -0.5)
    c_ln_g1 = 1.0 + nrm(ks[12], (no, C_WIDTH), 0.02)
    c_ln_b1 = nrm(ks[13], (no, C_WIDTH), 0.02)
    c_ws1 = nrm(ks[14], (no, C_GROUPS, C_CHUNK, C_CHUNK), C_CHUNK ** -0.5)
    c_bs1 = 1.0 + nrm(ks[15], (no, C_GROUPS, C_CHUNK), 0.01)
    w_out1 = nrm(ks[16], (no, C_WIDTH, D_MODEL), C_WIDTH ** -0.5)
    return {"x": x, "norm0_g": norm0_g, "w_in0": w_in0, "conv0_w": conv0_w, "a_log0": a_log0,
            "dt_bias0": dt_bias0, "a_onorm_g0": a_onorm_g0, "b_qnorm_g0": b_qnorm_g0,
            "b_knorm_g0": b_knorm_g0, "w_out0": w_out0, "norm1_g": norm1_g, "w_in1": w_in1,
            "c_ln_g1": c_ln_g1, "c_ln_b1": c_ln_b1, "c_ws1": c_ws1, "c_bs1": c_bs1, "w_out1": w_out1}


def reference(x, norm0_g, w_in0, conv0_w, a_log0, dt_bias0, a_onorm_g0, b_qnorm_g0, b_knorm_g0, w_out0,
              norm1_g, w_in1, c_ln_g1, c_ln_b1, c_ws1, c_bs1, w_out1):
    for layer in range(DEPTH):
        i = layer // 2
        if layer % 2 == 0:
            x = hybrid_layer(x, norm0_g[i], w_in0[i], conv0_w[i], a_log0[i], dt_bias0[i],
                             a_onorm_g0[i], b_qnorm_g0[i], b_knorm_g0[i], w_out0[i])
        else:
            x = gmlp_layer(x, norm1_g[i], w_in1[i], c_ln_g1[i], c_ln_b1[i], c_ws1[i], c_bs1[i], w_out1[i])
    return x
```

```python
import numpy as np
import concourse.bass as bass
import concourse.mybir as mybir

F32 = mybir.dt.float32
BF16 = mybir.dt.bfloat16
AF = mybir.ActivationFunctionType
ALU = mybir.AluOpType
AX = mybir.AxisListType

ENGS = ["pe", "act", "dve", "pool", "sp"]
SEM_CAP = 16000
N_DMA_SEMS = 24


class Buf:
    __slots__ = ("ap", "key")

    def __init__(self, ap, key):
        self.ap = ap
        self.key = key


class Prog:
    def __init__(self, nc):
        self.nc = nc
        self.ops = {e: [] for e in ENGS}
        self.cnt = {e: 0 for e in ENGS}
        self.waited = {e: {} for e in ENGS}
        self.lastw = {}
        self.rd = {}
        self.sems = {}
        self.dma_rr = {"hw": 0, "sw": 0}
        self.dma_cnt = {}
        self.nwaits = 0
        self.arena_words = 53000
        self.arena = nc.alloc_sbuf_tensor("arena", [128, self.arena_words], F32)
        self.top = 0
        self.psum = nc.alloc_psum_tensor("psum", [128, 4096], F32)
        self.uid = 0

    def mark(self):
        return self.top

    def release(self, m):
        self.barrier()
        self.top = m

    def sb(self, name, free_elems, dtype, parts=128):
        esz = 4 if dtype == F32 else 2
        nbytes = free_elems * esz
        nbytes = (nbytes + 63) // 64 * 64
        off = self.top
        self.top += nbytes
        assert self.top <= self.arena_words * 4, f"SBUF overflow at {name}: {self.top}"
        ap = self.arena[0:parts, off // 4:(off + nbytes) // 4]
        if dtype != F32:
            ap = ap.bitcast(dtype)
        ap = ap[:, 0:free_elems]
        self.uid += 1
        return Buf(ap, (name, self.uid))

    def bank(self, b, dtype=F32):
        ap = self.psum[:, b * 512:(b + 1) * 512]
        if dtype != F32:
            ap = ap.bitcast(dtype)
        return Buf(ap, ("psum", b))

    def _sem(self, key):
        if key not in self.sems:
            self.sems[key] = self.nc.alloc_semaphore("s_%s_%s" % key)
        return self.sems[key]

    def _deps(self, eng, reads, writes):
        deps = {}

        def add(tok):
            if tok is None:
                return
            k, v = tok
            if eng == "pe" and k[0] == "pe":
                return
            if deps.get(k, 0) < v:
                deps[k] = v

        for b in reads:
            add(self.lastw.get(b))
        for b in writes:
            add(self.lastw.get(b))
            for k, v in self.rd.get(b, {}).items():
                add((k, v))
        out = []
        w = self.waited[eng]
        for k, v in deps.items():
            if w.get(k, 0) < v:
                w[k] = v
                out.append((k, v))
        self.nwaits += len(out)
        return out

    def _commit(self, tok, reads, writes):
        k, v = tok
        for b in reads:
            d = self.rd.setdefault(b, {})
            if d.get(k, 0) < v:
                d[k] = v
        for b in writes:
            self.lastw[b] = tok
            self.rd[b] = {}

    @staticmethod
    def _keys(bufs):
        out = []
        for b in bufs:
            if b is None:
                continue
            if isinstance(b, Buf):
                out.append(b.key)
            else:
                out.append(b)
        return out

    def op(self, eng, fn, reads=(), writes=()):
        reads = self._keys(reads)
        writes = self._keys(writes)
        waits = self._deps(eng, reads, writes)
        n = self.cnt[eng]
        self.cnt[eng] = n + 1
        tok = ((eng, n // SEM_CAP), n % SEM_CAP + 1)
        self.ops[eng].append((waits, fn, (tok[0], 1)))
        self._commit(tok, reads, writes)
        return tok

    def dma(self, q, out, in_, reads=(), writes=()):
        reads = self._keys(reads)
        writes = self._keys(writes)
        cls = "sw" if q == "pool" else "hw"
        nsem = 8 if cls == "sw" else N_DMA_SEMS
        s = self.dma_rr[cls]
        self.dma_rr[cls] = (s + 1) % nsem
        key = ("dma" + cls, s)
        waits = self._deps(q, reads, writes)
        prev = self.dma_cnt.get(key, 0)
        if prev and self.waited[q].get(key, 0) < prev * 16:
            self.waited[q][key] = prev * 16
            waits.append((key, prev * 16))
        self.dma_cnt[key] = prev + 1
        tok = (key, (prev + 1) * 16)
        if q == "pool":
            fn = lambda e, o=out, i=in_: e.dma_start(out=o, in_=i)
        else:
            fn = lambda e, o=out, i=in_: e.dma_start(out=o, in_=i)
        self.ops[q].append((waits, fn, (key, 16)))
        self._commit(tok, reads, writes)
        return tok

    def barrier(self):
        toks = []
        for e in ENGS:
            n = self.cnt[e]
            if n:
                toks.append(((e, (n - 1) // SEM_CAP), (n - 1) % SEM_CAP + 1))
        for key, cnt in self.dma_cnt.items():
            toks.append((key, cnt * 16))
        for e in ENGS:
            waits = []
            for k, v in toks:
                if k[0] == e and e != "sp":
                    pass
                if self.waited[e].get(k, 0) < v:
                    self.waited[e][k] = v
                    waits.append((k, v))
            if waits:
                self.ops[e].append((waits, None, None))

    def emit(self):
        nc = self.nc
        self.barrier()

        def run(e, eng):
            for waits, fn, inc in self.ops[e]:
                for k, v in waits:
                    eng.wait_ge(self._sem(k), v)
                if fn is None:
                    continue
                ins = fn(eng)
                if inc is not None:
                    ins.then_inc(self._sem(inc[0]), inc[1])

        for e in ENGS:
            for waits, fn, inc in self.ops[e]:
                for k, v in waits:
                    self._sem(k)
                if inc is not None:
                    self._sem(inc[0])

        with nc.Block() as block:
            @block.tensor
            def _(eng):
                run("pe", eng)

            @block.scalar
            def _(eng):
                run("act", eng)

            @block.vector
            def _(eng):
                run("dve", eng)

            @block.gpsimd
            def _(eng):
                run("pool", eng)

            @block.sync
            def _(eng):
                run("sp", eng)


EPS = 1e-6
L = 4096
D = 2048
NT = 32
NB = 8
KC = 16


class Ctx:
    pass


def setup_consts(p, cst_dram):
    c = Ctx()
    c.ident = p.sb("ident", 128, BF16)
    c.identf = p.sb("identf", 128, F32)
    c.ones = p.sb("ones", 128, BF16)
    c.onesm = p.sb("onesm", 128, BF16)
    c.perm = p.sb("perm", 128, BF16)
    p.dma("pool", c.ident.ap, cst_dram[:, 0:128], writes=[c.ident])
    p.dma("sp", c.identf.ap, cst_dram[:, 0:128], writes=[c.identf])
    p.dma("pool", c.perm.ap, cst_dram[:, 128:256], writes=[c.perm])
    p.op("pool", lambda e: e.memset(c.ones.ap, 1.0), writes=[c.ones])
    p.op("pool", lambda e: e.memset(c.onesm.ap, 1.0 / 128), writes=[c.onesm])
    c.identbf = c.ident
    c.onesf = p.sb("onesf", 128, F32)
    p.op("pool", lambda e: e.memset(c.onesf.ap, 1.0), writes=[c.onesf])
    return c


def setup_gdn_consts(p, c, cst_dram):
    c.nmiT4 = [p.sb("nmiT4_%d" % d, 512, BF16) for d in range(2)]
    c.nms4 = [p.sb("nms4_%d" % d, 512, BF16) for d in range(2)]
    c.bd4 = [p.sb("bd4_%d" % i, 512, BF16) for i in range(4)]
    c.ident4 = p.sb("ident4", 512, BF16)
    for r in range(4):
        rs = slice(r * 128, (r + 1) * 128)
        p.dma("pool", c.ident4.ap[:, rs], cst_dram[:, 0:128], writes=[c.ident4])
        for d in range(2):
            p.dma("pool", c.nmiT4[d].ap[:, rs], cst_dram[:, 256 + d * 256:384 + d * 256], writes=[c.nmiT4[d]])
            p.dma("pool", c.nms4[d].ap[:, rs], cst_dram[:, 384 + d * 256:512 + d * 256], writes=[c.nms4[d]])
        for i in range(4):
            p.dma("pool", c.bd4[i].ap[:, rs], cst_dram[:, 896 + i * 128:896 + (i + 1) * 128], writes=[c.bd4[i]])


def stage_norm_T(p, c, x_dram, g_dram, hT, after_block=None):
    m = p.mark()
    gB = p.sb("gB", D, F32)
    p.dma("sp", gB.ap, g_dram.partition_broadcast(128), writes=[gB])
    xb = [p.sb("xt%d" % i, D, F32) for i in range(2)]
    xn = [p.sb("xn%d" % i, D, BF16) for i in range(2)]
    junk = p.sb("junk", D, BF16)
    ss = [p.sb("ss%d" % i, 1, F32) for i in range(2)]
    rs = [p.sb("rs%d" % i, 1, F32) for i in range(2)]
    hT3 = hT.ap.rearrange("p (k t) -> p k t", k=KC)
    for t in range(NT):
        x, n_, s, r = xb[t % 2], xn[t % 2], ss[t % 2], rs[t % 2]
        p.dma("sp", x.ap, x_dram[t * 128:(t + 1) * 128, :], writes=[x])
        p.op("act", lambda e, x=x, s=s: e.activation(out=junk.ap, in_=x.ap, func=AF.Square, accum_out=s.ap),
             reads=[x], writes=[junk, s])
        p.op("act", lambda e, s=s, r=r: e.activation(out=r.ap, in_=s.ap, func=AF.Ln, bias=EPS, scale=1.0 / D),
             reads=[s], writes=[r])
        p.op("act", lambda e, r=r: e.activation(out=r.ap, in_=r.ap, func=AF.Exp, scale=-0.5), reads=[r], writes=[r])
        p.op("dve", lambda e, x=x, r=r, n_=n_: e.scalar_tensor_tensor(out=n_.ap, in0=x.ap, scalar=r.ap[:, 0:1],
                                                                       in1=gB.ap, op0=ALU.mult, op1=ALU.mult),
             reads=[x, r, gB], writes=[n_])
        for half in range(2):
            bk = p.bank((2 * t + half) % 8, BF16)
            for j in range(8):
                kc = half * 8 + j
                p.op("pe", lambda e, bk=bk, j=j, kc=kc, n_=n_: e.transpose(
                    out=bk.ap[:, j * 128:(j + 1) * 128], in_=n_.ap[:, kc * 128:(kc + 1) * 128], identity=c.ident.ap),
                    reads=[n_, c.ident], writes=[bk])
            dst = hT3[:, half * 8:(half + 1) * 8, t * 128:(t + 1) * 128]
            src = bk.ap.rearrange("p (j c) -> p j c", c=128)
            eng = "act" if half == 0 else "dve"
            if eng == "act":
                p.op("act", lambda e, dst=dst, src=src: e.copy(out=dst, in_=src), reads=[bk], writes=[(hT.key, t // 4)])
            else:
                p.op("dve", lambda e, dst=dst, src=src: e.tensor_copy(out=dst, in_=src), reads=[bk],
                     writes=[(hT.key, t // 4)])
        if after_block is not None and t % 4 == 3 and t >= 7:
            after_block(t // 4 - 1)
    if after_block is not None:
        after_block(NB - 1)
    p.release(m)


def proj_load_w(p, w, w_view, job):
    n = job["n"]
    w3 = w.ap[:, 0:KC * n].rearrange("p (k c) -> p k c", k=KC)
    p.dma("pool", w3, w_view[:, :, job["c0"]:job["c0"] + n], writes=[w])
    return w3


def proj_fm_block(p, hT, w, w3, job, sub, tb, bank_ctr):
    hT3 = hT.ap.rearrange("p (k t) -> p k t", k=KC)
    mcols = min(128, job["n"] - sub * 128)
    b = bank_ctr[0] % 8
    bank_ctr[0] += 1
    bk = p.bank(b)
    for kc in range(KC):
        p.op("pe", lambda e, bk=bk, kc=kc: e.matmul(
            bk.ap[0:mcols, :], lhsT=w3[:, kc, sub * 128:sub * 128 + mcols],
            rhs=hT3[:, kc, tb * 512:(tb + 1) * 512], start=(kc == 0), stop=(kc == KC - 1)),
            reads=[w, (hT.key, tb)], writes=[bk])
    job["sink"](p, bk, job, sub, tb)


def stage_proj(p, hT, w_view, jobs, bank_ctr=[0], wb=None, skip_first=False):
    m = p.mark()
    if wb is None:
        wb = [p.sb("w%d" % i, KC * 512, BF16) for i in range(2)]
    hT3 = hT.ap.rearrange("p (k t) -> p k t", k=KC)
    for ji, job in enumerate(jobs):
        if skip_first and ji == 0:
            continue
        w = wb[ji % 2]
        n = job["n"]
        w3 = proj_load_w(p, w, w_view, job)
        if job["mode"] == "fm":
            nsub = (n + 127) // 128
            for sub in range(nsub):
                for tb in range(NB):
                    proj_fm_block(p, hT, w, w3, job, sub, tb, bank_ctr)
        else:
            for t in range(NT):
                b = bank_ctr[0] % 8
                bank_ctr[0] += 1
                bk = p.bank(b)
                for kc in range(KC):
                    p.op("pe", lambda e, bk=bk, kc=kc, t=t, w3=w3, n=n: e.matmul(
                        bk.ap[:, 0:n], lhsT=hT3[:, kc, t * 128:(t + 1) * 128], rhs=w3[:, kc, :],
                        start=(kc == 0), stop=(kc == KC - 1)),
                        reads=[w, (hT.key, t // 4)], writes=[bk])
                job["sink"](p, bk, job, 0, t)
    p.release(m)


def make_fm_direct_sink(p, dst, nbuf=4):
    stg = [p.sb("dstg%d" % i, 512, BF16) for i in range(nbuf)]
    st = {"i": 0}
    alt = Alt(["act", "dve"])

    def sink(p, bk, job, sub, tb):
        s_ = stg[st["i"] % nbuf]
        st["i"] += 1
        f_ = job.get("func")
        if f_ is not None:
            ACTV(p, s_.ap, bk.ap, f_, [bk], [s_])
        else:
            CP(p, alt(), s_.ap, bk.ap, [bk], [s_])
        p.dma("sp", dst(job, sub)[:, tb * 512:(tb + 1) * 512], s_.ap, reads=[s_])

    return sink


def TT(p, eng, out, in0, in1, op, reads, writes):
    return p.op(eng, lambda e: e.tensor_tensor(out=out, in0=in0, in1=in1, op=op), reads, writes)


def TS(p, eng, out, in0, s1, s2, op0, op1, reads, writes):
    if s2 is None:
        return p.op(eng, lambda e: e.tensor_scalar(out=out, in0=in0, scalar1=s1, scalar2=None, op0=op0), reads, writes)
    return p.op(eng, lambda e: e.tensor_scalar(out=out, in0=in0, scalar1=s1, scalar2=s2, op0=op0, op1=op1), reads, writes)


def STT(p, eng, out, in0, scalar, in1, op0, op1, reads, writes):
    return p.op(eng, lambda e: e.scalar_tensor_tensor(out=out, in0=in0, scalar=scalar, in1=in1, op0=op0, op1=op1),
                reads, writes)


def ACTV(p, out, in_, func, reads, writes, bias=None, scale=None, accum=None):
    kw = {}
    if bias is not None:
        kw["bias"] = bias
    if scale is not None:
        kw["scale"] = scale
    if accum is not None:
        kw["accum_out"] = accum
    return p.op("act", lambda e: e.activation(out=out, in_=in_, func=func, **kw), reads, writes)


def MM(p, out, lhsT, rhs, reads, writes, start=True, stop=True):
    return p.op("pe", lambda e: e.matmul(out, lhsT=lhsT, rhs=rhs, start=start, stop=stop, skip_group_check=True),
                reads, writes)


def TR(p, out, in_, ident, reads, writes):
    return p.op("pe", lambda e: e.transpose(out=out, in_=in_, identity=ident), reads, writes)


def CP(p, eng, out, in_, reads, writes):
    if eng == "act":
        return p.op("act", lambda e: e.copy(out=out, in_=in_), reads, writes)
    return p.op(eng, lambda e: e.tensor_copy(out=out, in_=in_), reads, writes)


def MSET(p, eng, ap, val, writes):
    return p.op(eng, lambda e: e.memset(ap, val), (), writes)


class Alt:
    def __init__(self, engs):
        self.engs = engs
        self.i = 0

    def __call__(self):
        e = self.engs[self.i % len(self.engs)]
        self.i += 1
        return e


def make_fm_sink(p, dst, func=None, dtype=BF16, parts=128, nbuf=2):
    stg = [p.sb("stg%d" % i, L, dtype) for i in range(nbuf)]
    st = {"i": 0}
    alt = Alt(["act", "dve"])

    def sink(p, bk, job, sub, tb):
        s = stg[st["i"] % nbuf]
        mcols = min(128, job["n"] - sub * 128)
        o = s.ap[0:mcols, tb * 512:(tb + 1) * 512]
        i = bk.ap[0:mcols, :]
        f_ = job.get("func", func)
        if f_ is not None:
            ACTV(p, o, i, f_, [bk], [(s.key, tb)])
        else:
            CP(p, alt(), o, i, [bk], [(s.key, tb)])
        if tb == NB - 1:
            p.dma("sp", dst(job, sub), s.ap[0:mcols, :], reads=[(s.key, i) for i in range(NB)])
            st["i"] += 1

    return sink


def make_tm_sink(p, dst, func=None, dtype=BF16):
    stg = [p.sb("tstg%d" % i, 512, dtype) for i in range(3)]
    alt = Alt(["act", "dve"])
    st = {"i": 0}

    def sink(p, bk, job, sub, t):
        s = stg[st["i"] % 3]
        st["i"] += 1
        n = job["n"]
        if func is not None:
            ACTV(p, s.ap[:, 0:n], bk.ap[:, 0:n], func, [bk], [s])
        else:
            CP(p, alt(), s.ap[:, 0:n], bk.ap[:, 0:n], [bk], [s])
        p.dma("sp", dst(job, t), s.ap[:, 0:n], reads=[s])

    return sink


def stage_outproj(p, yT_dram, w_dram, xres_dram, out_dram):
    m = p.mark()
    w = p.sb("wout", KC * D, BF16)
    w3 = w.ap.rearrange("p (k c) -> p k c", k=KC)
    wv = w_dram.rearrange("(k p) c -> p k c", p=128)
    for cg in range(4):
        p.dma("pool", w3[:, :, cg * 512:(cg + 1) * 512], wv[:, :, cg * 512:(cg + 1) * 512], writes=[(w.key, cg)])
    yb = [p.sb("yb%d" % i, KC * 512, BF16) for i in range(2)]
    xb = [p.sb("xr%d" % i, D, F32) for i in range(2)]
    ob = [p.sb("ob%d" % i, D, F32) for i in range(2)]
    yv = yT_dram.rearrange("(k p) t -> p k t", p=128)
    bc = 0

    def load_y(tb):
        y = yb[tb % 2]
        p.dma("sp", y.ap.rearrange("p (k t) -> p k t", k=KC), yv[:, :, tb * 512:(tb + 1) * 512], writes=[y])

    def load_x(t):
        p.dma("sp", xb[t % 2].ap, xres_dram[t * 128:(t + 1) * 128, :], writes=[xb[t % 2]])

    load_y(0)
    load_x(0)
    for tb in range(NB):
        y = yb[tb % 2]
        y3 = y.ap.rearrange("p (k t) -> p k t", k=KC)
        if tb + 1 < NB:
            load_y(tb + 1)
        for ti in range(4):
            t = tb * 4 + ti
            xr, o = xb[t % 2], ob[t % 2]
            if t + 1 < NT:
                load_x(t + 1)
            for cg in range(4):
                bk = p.bank(bc % 8)
                bc += 1
                for kc in range(KC):
                    MM(p, bk.ap, y3[:, kc, ti * 128:(ti + 1) * 128], w3[:, kc, cg * 512:(cg + 1) * 512],
                       [y, (w.key, cg)], [bk], start=(kc == 0), stop=(kc == KC - 1))
                TT(p, "dve", o.ap[:, cg * 512:(cg + 1) * 512], bk.ap, xr.ap[:, cg * 512:(cg + 1) * 512], ALU.add,
                   [bk, xr], [(o.key, cg)])
            p.dma("sp", out_dram[t * 128:(t + 1) * 128, :], o.ap, reads=[(o.key, i) for i in range(4)])
    p.release(m)


def stage_qkprep(p, c, s_bq, s_bk, qg_dram, kg_dram, ropec, ropes, s_qk):
    m = p.mark()
    Ct = p.sb("ropeC", L, F32)
    St = p.sb("ropeS", L, F32)
    p.dma("sp", Ct.ap, ropec, writes=[Ct])
    p.dma("sp", St.ap, ropes, writes=[St])
    gq = p.sb("gq", 1, F32)
    gk = p.sb("gk", 1, F32)
    p.dma("sp", gq.ap, qg_dram.rearrange("o d -> d o"), writes=[gq])
    p.dma("sp", gk.ap, kg_dram.rearrange("o d -> d o"), writes=[gk])
    xin = [p.sb("qkin%d" % i, L, BF16) for i in range(2)]
    stg = [p.sb("qkst%d" % i, L, BF16) for i in range(2)]
    sq = [p.sb("qksq%d" % i, 512, BF16) for i in range(2)]
    xg = [p.sb("qkxg%d" % i, 512, BF16) for i in range(2)]
    lnb = [p.sb("qkln%d" % i, 512, F32) for i in range(2)]
    rs = [p.sb("qkrs%d" % i, 512, F32) for i in range(2)]
    t1 = [p.sb("qkt1%d" % i, 512, F32) for i in range(2)]
    t2 = [p.sb("qkt2%d" % i, 512, F32) for i in range(2)]
    t3 = [p.sb("qkt3%d" % i, 512, F32) for i in range(2)]
    it = 0

    def load_in(ht):
        src = s_bq[ht * 128:(ht + 1) * 128, :] if ht < 8 else s_bk[(ht - 8) * 128:(ht - 7) * 128, :]
        p.dma("sp", xin[ht % 2].ap, src, writes=[xin[ht % 2]])

    load_in(0)
    for ht in range(10):
        g = gq if ht < 8 else gk
        xi, so = xin[ht % 2], stg[ht % 2]
        if ht + 1 < 10:
            load_in(ht + 1)
        for tb in range(NB):
            i = it % 2
            it += 1
            blk = slice(tb * 512, (tb + 1) * 512)
            bA = p.bank((2 * it) % 8)
            bB = p.bank((2 * it + 1) % 8)
            ACTV(p, sq[i].ap, xi.ap[:, blk], AF.Square, [xi], [sq[i]])
            MM(p, bA.ap, c.onesm.ap, sq[i].ap, [c.onesm, sq[i]], [bA])
            TS(p, "dve", xg[i].ap, xi.ap[:, blk], g.ap[:, 0:1], None, ALU.mult, None, [xi, g], [xg[i]])
            MM(p, bB.ap, c.perm.ap, xg[i].ap, [c.perm, xg[i]], [bB])
            ACTV(p, lnb[i].ap, bA.ap, AF.Ln, [bA], [lnb[i]], bias=EPS)
            ACTV(p, rs[i].ap, lnb[i].ap, AF.Exp, [lnb[i]], [rs[i]], scale=-0.5)
            TT(p, "dve", t1[i].ap, xg[i].ap, Ct.ap[:, blk], ALU.mult, [xg[i], Ct], [t1[i]])
            TT(p, "dve", t2[i].ap, bB.ap, St.ap[:, blk], ALU.mult, [bB, St], [t2[i]])
            TT(p, "pool", t3[i].ap, t1[i].ap, t2[i].ap, ALU.add, [t1[i], t2[i]], [t3[i]])
            TT(p, "pool", so.ap[:, blk], t3[i].ap, rs[i].ap, ALU.mult, [t3[i], rs[i]], [(so.key, tb)])
        p.dma("sp", s_qk[ht * 128:(ht + 1) * 128, :], so.ap, reads=[(so.key, i) for i in range(NB)])
    p.release(m)


def stage_attn(p, c, s_qk, s_bv, s_bgate, s_y):
    m = p.mark()
    KTs = [p.sb("KT%d" % i, L, BF16) for i in range(2)]
    Vs = [p.sb("Vtm%d" % i, L, BF16) for i in range(2)]
    QT = [p.sb("QT%d" % i, L, BF16) for i in range(2)]
    GT = [p.sb("GT%d" % i, L, BF16) for i in range(2)]
    ST = [p.sb("yst%d" % i, L, BF16) for i in range(2)]
    PT = [p.sb("PT%d" % i, 1024, BF16) for i in range(4)]
    acc = [p.sb("acc%d" % i, 512, F32) for i in range(2)]
    atmp = [p.sb("atmp%d" % i, 512, F32) for i in range(2)]
    ocp = [p.sb("ocp%d" % i, 512, F32) for i in range(2)]
    scp = [p.sb("scp%d" % i, 512, F32) for i in range(2)]
    rinv = [p.sb("rinv%d" % i, 512, F32) for i in range(2)]
    yf = [p.sb("yf%d" % i, 512, F32) for i in range(2)]
    scale = 128 ** -0.5
    pc = 0
    qbc = 0
    NP = NT // 2

    def load_kv(j):
        p.dma("sp", KTs[j].ap, s_qk[(8 + j) * 128:(9 + j) * 128, :], writes=[KTs[j]])
        p.dma("sp", Vs[j].ap.rearrange("p (t d) -> p t d", d=128),
              s_bv[:, j * 128:(j + 1) * 128].rearrange("(t p) d -> p t d", p=128), writes=[Vs[j]])

    def load_q(h):
        p.dma("sp", QT[h % 2].ap, s_qk[h * 128:(h + 1) * 128, :], writes=[QT[h % 2]])
        p.dma("sp", GT[h % 2].ap, s_bgate[h * 128:(h + 1) * 128, :], writes=[GT[h % 2]])

    load_kv(0)
    load_q(0)
    load_kv(1)
    pending = [None]
    for j in range(2):
        KT, V = KTs[j], Vs[j]
        V3 = V.ap.rearrange("p (t d) -> p t d", d=128)
        for r in range(4):
            h = 4 * j + r
            q, gt, so = QT[h % 2], GT[h % 2], ST[h % 2]
            if pending[0] is not None:
                pending[0]()
                pending[0] = None
            if h + 1 < 8:
                load_q(h + 1)
            for qb in range(NB):
                blk = slice(qb * 512, (qb + 1) * 512)
                bO = p.bank(6)
                bS = p.bank(7)
                oc, sc = ocp[qbc % 2], scp[qbc % 2]
                ac = acc[qbc % 2]
                qbc += 1
                sb = {}

                def issue_s(pr):
                    b0 = 2 * ((pc + pr) % 3)
                    k0, k1 = ("psum", b0), ("psum", b0 + 1)
                    for u in range(2):
                        kt = 2 * pr + u
                        MM(p, p.psum[:, (b0 + u) * 512:(b0 + u + 1) * 512], KT.ap[:, kt * 128:(kt + 1) * 128], q.ap[:, blk],
                           [KT, q], [k0, k1])
                    sb[pr] = (p.psum[:, b0 * 512:(b0 + 2) * 512], [k0, k1])

                issue_s(0)
                issue_s(1)
                if pending[0] is not None:
                    pending[0]()
                    pending[0] = None
                for pr in range(NP):
                    if pr + 2 < NP:
                        issue_s(pr + 2)
                    pt = PT[(pc + pr) % 4]
                    sap, sk = sb[pr]
                    ACTV(p, pt.ap, sap, AF.Exp, sk, [pt], scale=scale)
                    for u in range(2):
                        kt = 2 * pr + u
                        MM(p, bO.ap, V3[:, kt, :], pt.ap[:, u * 512:(u + 1) * 512], [V, pt], [bO], start=(kt == 0),
                           stop=(kt == NT - 1))
                        if pr % 2 == 1:
                            MM(p, bS.ap, c.ones.ap, pt.ap[:, u * 512:(u + 1) * 512], [c.ones, pt], [bS], start=(kt == 2),
                               stop=False)
                    if pr % 2 == 0:
                        if pr == 0:
                            TT(p, "dve", ac.ap, pt.ap[:, 0:512], pt.ap[:, 512:1024], ALU.add, [pt], [ac])
                        else:
                            tm_ = atmp[(pr // 2) % 2]
                            TT(p, "dve", tm_.ap, pt.ap[:, 0:512], pt.ap[:, 512:1024], ALU.add, [pt], [tm_])
                            TT(p, "dve", ac.ap, ac.ap, tm_.ap, ALU.add, [ac, tm_], [ac])
                pc += NP
                CP(p, "dve", oc.ap, bO.ap, [bO], [oc])

                def epilogue(bS=bS, oc=oc, sc=sc, ac=ac, qb=qb, so=so, gt=gt, blk=blk, h=h):
                    MM(p, bS.ap, c.onesf.ap, ac.ap, [c.onesf, ac], [bS], start=False, stop=True)
                    CP(p, "dve", sc.ap, bS.ap, [bS], [sc])
                    ri, y = rinv[qb % 2], yf[qb % 2]
                    p.op("dve", lambda e, ri=ri, sc=sc: e.reciprocal(out=ri.ap, in_=sc.ap), [sc], [ri])
                    TT(p, "dve", y.ap, oc.ap, ri.ap, ALU.mult, [oc, ri], [y])
                    TT(p, "dve", so.ap[:, blk], y.ap, gt.ap[:, blk], ALU.mult, [y, gt], [(so.key, qb)])
                    if qb == NB - 1:
                        p.dma("sp", s_y[1024 + h * 128:1024 + (h + 1) * 128, :], so.ap, reads=[(so.key, i) for i in range(NB)])

                pending[0] = epilogue
    pending[0]()
    p.release(m)


def stage_gdn_scalars(p, c, s_alog, alog_dram, dtb_dram, s_rows, s_eg, s_gl, s_ts, s_rows16=None):
    m = p.mark()
    h16 = p.sb("gsh16", L, BF16, parts=8)
    l16 = p.sb("gsl16", L, BF16, parts=8)
    buf = [p.sb("gs%d" % i, L, F32, parts=8) for i in range(5)]
    A_ = p.sb("gsA", 1, F32, parts=8)
    negA = p.sb("gsnA", 1, F32, parts=8)
    dtb = p.sb("gsdtb", 1, F32, parts=8)
    glc = p.sb("gsglc", 32, F32, parts=8)
    tsb = p.sb("gstsb", 1024, F32)
    for d in range(2):
        b0, b1, b2, b3, b4 = buf
        p.dma("sp", A_.ap, alog_dram[d:d + 1, :].rearrange("o h -> h o"), writes=[A_])
        p.dma("sp", dtb.ap, dtb_dram[d:d + 1, :].rearrange("o h -> h o"), writes=[dtb])
        p.dma("sp", b0.ap, s_alog[d * 8:(d + 1) * 8, :], writes=[b0])
        p.dma("sp", b1.ap, s_alog[16 + d * 8:16 + (d + 1) * 8, :], writes=[b1])
        ACTV(p, negA.ap, A_.ap, AF.Exp, [A_], [negA])
        TS(p, "dve", negA.ap, negA.ap, -1.0, None, ALU.mult, None, [negA], [negA])
        TS(p, "dve", b0.ap, b0.ap, dtb.ap[:, 0:1], None, ALU.add, None, [b0, dtb], [b0])
        ACTV(p, b2.ap, b0.ap, AF.Abs, [b0], [b2])
        ACTV(p, b2.ap, b2.ap, AF.Exp, [b2], [b2], scale=-1.0)
        ACTV(p, b2.ap, b2.ap, AF.Ln, [b2], [b2], bias=1.0)
        TS(p, "dve", b0.ap, b0.ap, 0.0, None, ALU.max, None, [b0], [b0])
        TT(p, "dve", b0.ap, b0.ap, b2.ap, ALU.add, [b0, b2], [b0])
        TS(p, "dve", b0.ap, b0.ap, negA.ap[:, 0:1], None, ALU.mult, None, [b0, negA], [b0])
        src, dst = b0, b2
        s = 1
        while s < 128:
            s3 = src.ap.rearrange("p (n c) -> p n c", c=128)
            d3 = dst.ap.rearrange("p (n c) -> p n c", c=128)
            if d == 0:
                CP(p, "pool", d3[:, :, 0:s], s3[:, :, 0:s], [src], [dst])
                TT(p, "dve", d3[:, :, s:128], s3[:, :, s:128], s3[:, :, 0:128 - s], ALU.add, [src], [dst])
            else:
                CP(p, "pool", d3[:, :, 128 - s:128], s3[:, :, 128 - s:128], [src], [dst])
                TT(p, "dve", d3[:, :, 0:128 - s], s3[:, :, 0:128 - s], s3[:, :, s:128], ALU.add, [src], [dst])
            src, dst = dst, src
            s *= 2
        G = src
        assert G is b2
        G3 = G.ap.rearrange("p (n c) -> p n c", c=128)
        ACTV(p, b1.ap, b1.ap, AF.Sigmoid, [b1], [b1])
        ACTV(p, b3.ap, b1.ap, AF.Ln, [b1], [b3])
        TS(p, "dve", b3.ap, b3.ap, -80.0, None, ALU.max, None, [b3], [b3])
        TS(p, "dve", b0.ap, G.ap, -1.0, None, ALU.mult, None, [G], [b0])
        p.dma("sp", s_rows[d, 0], G.ap, reads=[G])
        p.dma("sp", s_rows[d, 1], b0.ap, reads=[b0])
        TT(p, "dve", b3.ap, b0.ap, b3.ap, ALU.add, [b0, b3], [b3])
        p.dma("sp", s_rows[d, 2], b3.ap, reads=[b3])
        for ri, src_ in enumerate((G, b0, b3)):
            CP(p, "dve", h16.ap, src_.ap, [src_], [h16])
            CP(p, "dve", b4.ap, h16.ap, [h16], [b4])
            TT(p, "dve", b4.ap, src_.ap, b4.ap, ALU.subtract, [src_, b4], [b4])
            CP(p, "dve", l16.ap, b4.ap, [b4], [l16])
            p.dma("sp", s_rows16[d, ri, 0], h16.ap, reads=[h16])
            p.dma("sp", s_rows16[d, ri, 1], l16.ap, reads=[l16])
        ACTV(p, b4.ap, G.ap, AF.Exp, [G], [b4])
        p.dma("sp", s_eg[d], b4.ap, reads=[b4])
        gi = 127 if d == 0 else 0
        CP(p, "dve", glc.ap, G3[:, :, gi:gi + 1].rearrange("p n o -> p (n o)"), [G], [glc])
        b33 = b3.ap.rearrange("p (n c) -> p n c", c=128)
        TT(p, "dve", b33, G3[:, :, gi:gi + 1].broadcast_to([8, 32, 128]), G3, ALU.subtract, [G], [b3])
        ACTV(p, b3.ap, b3.ap, AF.Exp, [b3], [b3])
        ACTV(p, glc.ap, glc.ap, AF.Exp, [glc], [glc])
        p.dma("sp", s_gl[d], glc.ap, reads=[glc])
        TS(p, "dve", b0.ap, b1.ap, -1.0, None, ALU.mult, None, [b1], [b0])
        quants = [b1, b0, b4, b3]
        ts4 = tsb.ap.rearrange("p (t q h) -> p t q h", t=32, q=4)
        for t in range(NT):
            bk = p.bank(t % 8)
            for qi, qb in enumerate(quants):
                TR(p, bk.ap[:, qi * 8:(qi + 1) * 8], qb.ap[:, t * 128:(t + 1) * 128], c.identf.ap[0:8, 0:8],
                   [qb, c.identf], [bk])
            CP(p, "act", tsb.ap[:, t * 32:(t + 1) * 32], bk.ap[:, 0:32], [bk], [tsb])
        p.dma("sp", s_ts[d], tsb.ap, reads=[tsb])
    p.release(m)


def stage_gdn_head(p, c, h, s_aqkv, s_agate, convw, onorm, s_rows, s_eg, s_gl, s_ts, s_y, dbg=None):
    import math
    m = p.mark()
    qT = p.sb("qT", L, BF16)
    kT = p.sb("kT", L, BF16)
    ktm = p.sb("ktm", L, BF16)
    vtm = p.sb("vtm", L, BF16)
    ktm3 = ktm.ap.rearrange("p (t d) -> p t d", d=128)
    vtm3 = vtm.ap.rearrange("p (t d) -> p t d", d=128)
    m2 = p.mark()
    xp = [p.sb("xp%d" % i, L + 4, BF16) for i in range(2)]
    acc = [p.sb("acc%d" % i, L, F32) for i in range(2)]
    xc = [p.sb("xc%d" % i, L, F32) for i in range(2)]
    vT = p.sb("vT", L, BF16)
    sq = [p.sb("sq%d" % i, 512, BF16) for i in range(2)]
    lnb = [p.sb("lnb%d" % i, 512, F32) for i in range(2)]
    rsb = [p.sb("rsb%d" % i, 512, F32) for i in range(2)]
    cw3 = convw.ap.rearrange("p (b j) -> p b j", j=5)
    bc = 0
    for idx in range(3):
        blk = idx * 8 + h
        x_, a_, xc_ = xp[idx % 2], acc[idx % 2], xc[idx % 2]
        eng = "dve"
        MSET(p, "pool", x_.ap[:, 0:2], 0.0, [x_])
        MSET(p, "pool", x_.ap[:, L + 2:L + 4], 0.0, [x_])
        p.dma("sp", x_.ap[:, 2:L + 2], s_aqkv[blk * 128:(blk + 1) * 128, :], writes=[x_])
        TS(p, eng, a_.ap, x_.ap[:, 0:L], cw3[:, blk, 0:1], None, ALU.mult, None, [x_, convw], [a_])
        for j in range(1, 5):
            STT(p, eng, a_.ap, x_.ap[:, j:j + L], cw3[:, blk, j:j + 1], a_.ap, ALU.mult, ALU.add, [x_, convw, a_], [a_])
        if idx == 2:
            ACTV(p, vT.ap, a_.ap, AF.Silu, [a_], [vT])
            continue
        ACTV(p, xc_.ap, a_.ap, AF.Silu, [a_], [xc_])
        dstT = qT if idx == 0 else kT
        for tb in range(NB):
            i = bc % 2
            bk = p.bank(bc % 8)
            bc += 1
            sl = slice(tb * 512, (tb + 1) * 512)
            ACTV(p, sq[i].ap, xc_.ap[:, sl], AF.Square, [xc_], [sq[i]])
            MM(p, bk.ap, c.ones.ap, sq[i].ap, [c.ones, sq[i]], [bk])
            ACTV(p, lnb[i].ap, bk.ap, AF.Ln, [bk], [lnb[i]], bias=EPS)
            if idx == 0:
                ACTV(p, rsb[i].ap, lnb[i].ap, AF.Exp, [lnb[i]], [rsb[i]], scale=-0.5, bias=math.log(128 ** -0.5))
            else:
                ACTV(p, rsb[i].ap, lnb[i].ap, AF.Exp, [lnb[i]], [rsb[i]], scale=-0.5)
            TT(p, "dve", dstT.ap[:, sl], xc_.ap[:, sl], rsb[i].ap, ALU.mult, [xc_, rsb[i]], [(dstT.key, tb)])
    alt = Alt(["act", "dve"])
    for src, dst3, dst in ((kT, ktm3, ktm), (vT, vtm3, vtm)):
        for g8 in range(4):
            bk = p.bank(bc % 8, BF16)
            bc += 1
            for j in range(8):
                t = g8 * 8 + j
                rk = [(src.key, t // 4)] if src is kT else [src]
                TR(p, bk.ap[:, j * 128:(j + 1) * 128], src.ap[:, t * 128:(t + 1) * 128], c.ident.ap, rk + [c.ident], [bk])
            CP(p, alt(), dst3[:, g8 * 8:(g8 + 1) * 8, :], bk.ap.rearrange("p (j d) -> p j d", d=128), [bk], [dst])
    p.release(m2)
    qdec = [p.sb("qdec%d" % d, L, BF16) for d in range(2)]
    wT = [p.sb("wT%d" % d, L, BF16) for d in range(2)]
    aT = [p.sb("aT%d" % d, L, BF16) for d in range(2)]
    ub = [p.sb("ub%d" % d, L, BF16) for d in range(2)]
    kdc = [[p.sb("kdc%d_%d" % (d, i), 128, BF16) for i in range(2)] for d in range(2)]
    of = p.sb("of", L, F32)
    tsd = [p.sb("tsd%d" % d, 1024, F32) for d in range(2)]
    glB = [p.sb("glB%d" % d, 32, F32) for d in range(2)]
    sg = p.sb("sg", L, BF16)
    yst = p.sb("ystA", L, BF16)
    RA = [p.sb("RA%d" % i, 512, F32, parts=2) for i in range(2)]
    RB = [p.sb("RB%d" % i, 512, F32, parts=2) for i in range(2)]
    RB2 = [p.sb("RB2%d" % i, 512, F32, parts=2) for i in range(2)]
    egb = [p.sb("egb%d" % i, 512, F32) for i in range(2)]
    G4 = 4
    t12 = [p.sb("t12_%d" % i, 256, F32) for i in range(G4)]
    e12 = [p.sb("e12_%d" % i, 256, F32) for i in range(G4)]
    p0t = [p.sb("p0t_%d" % i, 128, F32) for i in range(G4)]
    pq = [[p.sb("pq_%d_%d" % (i, k), 256, BF16) for k in range(2)] for i in range(G4)]
    xy = [[p.sb("xy_%d_%d" % (i, k), 256, BF16) for k in range(2)] for i in range(G4)]
    qpf = [p.sb("qpf_%d" % i, 256, BF16) for i in range(G4)]
    qpo = [p.sb("qpo_%d" % i, 256, BF16) for i in range(G4)]
    wv = [p.sb("wv_%d" % i, 256, BF16) for i in range(G4)]
    xs = [p.sb("xs_%d" % i, 128, BF16) for i in range(G4)]
    identb = c.ident
    for i in range(2):
        MSET(p, "pool", RA[i].ap, 1.0, [RA[i]])
        MSET(p, "pool", RB[i].ap, 1.0, [RB[i]])
        MSET(p, "pool", RB2[i].ap, 1.0, [RB2[i]])
    p.dma("sp", sg.ap, s_agate[h * 128:(h + 1) * 128, :], writes=[sg])
    masks = c.masks
    of3 = of.ap.rearrange("p (t d) -> p t d", d=128)
    pcs = 0
    for d in range(2):
        p.dma("sp", tsd[d].ap, s_ts[d], writes=[tsd[d]])
        p.dma("sp", glB[d].ap, s_gl[d, h:h + 1, :].partition_broadcast(128), writes=[glB[d]])
        ts4 = tsd[d].ap.rearrange("p (t q h) -> p t q h", t=32, q=4)
        aT3 = aT[d].ap.rearrange("p (t i) -> p t i", i=128)
        ub3 = ub[d].ap.rearrange("p (t i) -> p t i", i=128)
        for pc in range(8):
            ra, rb, rb2, eg = RA[pcs % 2], RB[pcs % 2], RB2[pcs % 2], egb[pcs % 2]
            pcs += 1
            psl = slice(pc * 512, (pc + 1) * 512)
            p.dma("sp", ra.ap[0:1, :], s_rows[d, 0, h:h + 1, psl], writes=[ra])
            p.dma("sp", rb.ap[1:2, :], s_rows[d, 1, h:h + 1, psl], writes=[rb])
            p.dma("sp", rb2.ap[1:2, :], s_rows[d, 2, h:h + 1, psl], writes=[rb2])
            p.dma("sp", eg.ap, s_eg[d, h:h + 1, psl].partition_broadcast(128), writes=[eg])
            TT(p, "pool", qdec[d].ap[:, psl], qT.ap[:, psl], eg.ap, ALU.mult, [(qT.key, pc), eg],
               [(qdec[d].key, pc)])
            for g0 in range(0, 4, G4):
                chunks = [pc * 4 + g0 + i for i in range(G4)]
                st = {}
                bks = {}

                def nb(sl):
                    st[sl] = st.get(sl, 0) + 1
                    return p.bank(2 * sl + st[sl] % 2)

                for sl, n in enumerate(chunks):
                    cs = slice(n * 128, (n + 1) * 128)
                    ls = slice((n % 4) * 128, (n % 4 + 1) * 128)
                    bk = nb(sl)
                    bks[sl] = bk
                    kr = [(kT.key, n // 4)]
                    MM(p, bk.ap[:, 0:128], kT.ap[:, cs], kT.ap[:, cs], kr, [bk])
                    MM(p, bk.ap[:, 128:256], kT.ap[:, cs], qT.ap[:, cs], kr + [(qT.key, n // 4)], [bk])
                    MM(p, bk.ap[:, 256:384], rb.ap[0:2, ls], ra.ap[0:2, ls], [ra, rb], [bk])
                    MM(p, bk.ap[:, 384:512], ra.ap[0:2, ls], rb2.ap[0:2, ls], [ra, rb2], [bk])
                for sl, n in enumerate(chunks):
                    bk = bks[sl]
                    TT(p, "dve", t12[sl].ap, bk.ap[:, 256:512], masks[d].ap, ALU.add, [bk, masks[d]], [t12[sl]])
                    ACTV(p, e12[sl].ap, t12[sl].ap, AF.Exp, [t12[sl]], [e12[sl]])
                for sl, n in enumerate(chunks):
                    bk = bks[sl]
                    TT(p, "dve", aT3[:, n, :], bk.ap[:, 128:256], e12[sl].ap[:, 0:128], ALU.mult, [bk, e12[sl]],
                       [(aT[d].key, n)])
                    STT(p, "dve", p0t[sl].ap, bk.ap[:, 0:128], ts4[:, n, 1, h:h + 1], e12[sl].ap[:, 0:128], ALU.mult, ALU.mult,
                        [bk, tsd[d], e12[sl]], [p0t[sl]])
                    STT(p, "dve", qpf[sl].ap[:, 0:128], bk.ap[:, 0:128], -1.0, e12[sl].ap[:, 128:256], ALU.mult, ALU.mult,
                        [bk, e12[sl]], [qpf[sl]])
                    TT(p, "pool", qpf[sl].ap[:, 128:256], p0t[sl].ap, c.offd.ap, ALU.mult, [p0t[sl], c.offd, qpf[sl]],
                       [qpf[sl]])
                for sl, n in enumerate(chunks):
                    TT(p, "pool", pq[sl][0].ap, qpf[sl].ap, c.bdm[0].ap, ALU.mult, [qpf[sl], c.bdm[0]], [pq[sl][0]])
                    TT(p, "pool", xy[sl][0].ap, pq[sl][0].ap, c.ident2.ap, ALU.add, [pq[sl][0], c.ident2], [xy[sl][0]])
                step_i = 0
                for k in range(3):
                    cur, nxt = step_i % 2, (step_i + 1) % 2
                    step_i += 1
                    for sl, n in enumerate(chunks):
                        bk = nb(sl)
                        bks[sl] = bk
                        Qk = pq[sl][k % 2].ap[:, 0:128]
                        Pk = pq[sl][k % 2].ap[:, 128:256]
                        MM(p, bk.ap[:, 0:128], Pk, Qk, [pq[sl][k % 2]], [bk])
                        MM(p, bk.ap[:, 128:256], Qk, Pk, [pq[sl][k % 2]], [bk])
                    for sl, n in enumerate(chunks):
                        CP(p, "act", pq[sl][(k + 1) % 2].ap, bks[sl].ap[:, 0:256], [bks[sl]], [pq[sl][(k + 1) % 2]])
                    for sl, n in enumerate(chunks):
                        bk = bks[sl]
                        Qn = pq[sl][(k + 1) % 2].ap[:, 0:128]
                        Pn = pq[sl][(k + 1) % 2].ap[:, 128:256]
                        MM(p, bk.ap[:, 256:384], Pn, xy[sl][cur].ap[:, 0:128], [pq[sl][(k + 1) % 2], xy[sl][cur]], [bk])
                        MM(p, bk.ap[:, 384:512], Qn, xy[sl][cur].ap[:, 128:256], [pq[sl][(k + 1) % 2], xy[sl][cur]], [bk])
                    for sl, n in enumerate(chunks):
                        TT(p, "dve", xy[sl][nxt].ap, bks[sl].ap[:, 256:512], xy[sl][cur].ap, ALU.add, [bks[sl], xy[sl][cur]],
                           [xy[sl][nxt]])
                for lv in range(3):
                    cur, nxt = step_i % 2, (step_i + 1) % 2
                    step_i += 1
                    last = lv == 2
                    for sl, n in enumerate(chunks):
                        TT(p, "pool", qpo[sl].ap, qpf[sl].ap, c.bdm[lv + 1].ap, ALU.mult, [qpf[sl], c.bdm[lv + 1]], [qpo[sl]])
                    for sl, n in enumerate(chunks):
                        bk = nb(sl)
                        bks[sl] = bk
                        X_ = xy[sl][cur].ap[:, 0:128]
                        Y_ = xy[sl][cur].ap[:, 128:256]
                        if not last:
                            MM(p, bk.ap[:, 0:128], qpo[sl].ap[:, 128:256], X_, [qpo[sl], xy[sl][cur]], [bk])
                        MM(p, bk.ap[:, 128:256], qpo[sl].ap[:, 0:128], Y_, [qpo[sl], xy[sl][cur]], [bk])
                    for sl, n in enumerate(chunks):
                        lo = 128 if last else 0
                        CP(p, "act", wv[sl].ap[:, lo:256], bks[sl].ap[:, lo:256], [bks[sl]], [wv[sl]])
                    for sl, n in enumerate(chunks):
                        bk = bks[sl]
                        X_ = xy[sl][cur].ap[:, 0:128]
                        Y_ = xy[sl][cur].ap[:, 128:256]
                        if not last:
                            MM(p, bk.ap[:, 256:384], Y_, wv[sl].ap[:, 0:128], [wv[sl], xy[sl][cur]], [bk])
                        MM(p, bk.ap[:, 384:512], X_, wv[sl].ap[:, 128:256], [wv[sl], xy[sl][cur]], [bk])
                    for sl, n in enumerate(chunks):
                        lo = 128 if last else 0
                        TT(p, "dve", xy[sl][nxt].ap[:, lo:256], bks[sl].ap[:, 256 + lo:512], xy[sl][cur].ap[:, lo:256], ALU.add,
                           [bks[sl], xy[sl][cur]], [xy[sl][nxt]])
                assert step_i % 2 == 0
                for sl, n in enumerate(chunks):
                    TS(p, "pool", xs[sl].ap, ktm3[:, n, :], ts4[:, n, 2, h:h + 1], None, ALU.mult, None, [ktm, tsd[d]], [xs[sl]])
                bks2 = {}
                for sl, n in enumerate(chunks):
                    bk = nb(sl)
                    bks[sl] = bk
                    bk2 = nb(sl)
                    bks2[sl] = bk2
                    T_ = xy[sl][0]
                    Tap = T_.ap[:, 128:256]
                    MM(p, bk.ap[:, 0:128], Tap, vtm3[:, n, :], [T_, vtm], [bk])
                    MM(p, bk2.ap[:, 128:256], xs[sl].ap, Tap, [xs[sl], T_], [bk2])
                for sl, n in enumerate(chunks):
                    bk = bks[sl]
                    bk2 = bks2[sl]
                    TS(p, "dve", ub3[:, n, :], bk.ap[:, 0:128], ts4[:, n, 0, h:h + 1], None, ALU.mult, None, [bk, tsd[d]],
                       [(ub[d].key, n)])
                    CP(p, "act", wT[d].ap[:, n * 128:(n + 1) * 128], bk2.ap[:, 128:256], [bk2], [(wT[d].key, n)])
    S32 = [p.sb("S32_%d" % d, 128, F32) for d in range(2)]
    Sbf = [p.sb("Sbf_%d" % d, 128, BF16) for d in range(2)]
    vn = [[p.sb("vn_%d_%d" % (d, i), 128, BF16) for i in range(2)] for d in range(2)]
    osum = [p.sb("osum%d" % i, 128, F32) for i in range(4)]
    ojunk = p.sb("ojunk", 128, BF16)
    oss = [p.sb("oss%d" % i, 1, F32) for i in range(4)]
    ors = [p.sb("ors%d" % i, 1, F32) for i in range(4)]
    on = [p.sb("on%d" % i, 128, BF16) for i in range(4)]
    for d in range(2):
        MSET(p, "pool", S32[d].ap, 0.0, [S32[d]])
        MSET(p, "pool", Sbf[d].ap, 0.0, [Sbf[d]])
    for step in range(NT):
        for d in range(2):
            n = step if d == 0 else NT - 1 - step
            ts4 = tsd[d].ap.rearrange("p (t q h) -> p t q h", t=32, q=4)
            aT3 = aT[d].ap.rearrange("p (t i) -> p t i", i=128)
            ub3 = ub[d].ap.rearrange("p (t i) -> p t i", i=128)
            kd_ = kdc[d][step % 2]
            TS(p, "pool", kd_.ap, ktm3[:, n, :], ts4[:, n, 3, h:h + 1], None, ALU.mult, None, [ktm, tsd[d]], [kd_])
            cs = slice(n * 128, (n + 1) * 128)
            b1, b2, b3, b4 = [p.bank(4 * d + i) for i in range(4)]
            v_ = vn[d][step % 2]
            MM(p, b1.ap[:, 0:128], wT[d].ap[:, cs], Sbf[d].ap, [(wT[d].key, n), Sbf[d]], [b1])
            STT(p, "dve", v_.ap, b1.ap[:, 0:128], ts4[:, n, 1, h:h + 1], ub3[:, n, :], ALU.mult, ALU.add,
                [b1, tsd[d], (ub[d].key, n)], [v_])
            MM(p, b2.ap[:, 0:128], qdec[d].ap[:, cs], Sbf[d].ap, [(qdec[d].key, n // 4), Sbf[d]], [b2], start=True, stop=False)
            MM(p, b2.ap[:, 0:128], aT3[:, n, :], v_.ap, [(aT[d].key, n), v_], [b2], start=False, stop=True)
            MM(p, b3.ap[:, 0:128], kd_.ap, v_.ap, [kd_, v_], [b3])
            STT(p, "dve", S32[d].ap, S32[d].ap, glB[d].ap[:, n:n + 1], b3.ap[:, 0:128], ALU.mult, ALU.add,
                [S32[d], glB[d], b3], [S32[d]])
            CP(p, "act", Sbf[d].ap, S32[d].ap, [S32[d]], [Sbf[d]])
            if step < NT // 2:
                CP(p, "act", of3[:, n, :], b2.ap[:, 0:128], [b2], [(of.key, n)])
            else:
                i = (2 * step + d) % 4
                TT(p, "dve", osum[i].ap, b2.ap[:, 0:128], of3[:, n, :], ALU.add, [b2, (of.key, n)], [osum[i]])
                ACTV(p, ojunk.ap, osum[i].ap, AF.Square, [osum[i]], [ojunk, oss[i]], accum=oss[i].ap)
                ACTV(p, ors[i].ap, oss[i].ap, AF.Ln, [oss[i]], [ors[i]], bias=EPS, scale=1.0 / 128)
                ACTV(p, ors[i].ap, ors[i].ap, AF.Exp, [ors[i]], [ors[i]], scale=-0.5)
                TS(p, "pool", on[i].ap, osum[i].ap, ors[i].ap[:, 0:1], None, ALU.mult, None, [osum[i], ors[i]], [on[i]])
                b4b = p.bank(4 * d + 3, BF16)
                TR(p, b4b.ap[:, 0:128], on[i].ap, c.ident.ap, [on[i], c.ident], [b4b])
                STT(p, "dve", yst.ap[:, cs], b4b.ap[:, 0:128], onorm.ap[:, 0:1], sg.ap[:, cs], ALU.mult, ALU.mult,
                    [b4b, onorm, sg], [(yst.key, n)])
    p.dma("sp", s_y[h * 128:(h + 1) * 128, :], yst.ap, reads=[(yst.key, n) for n in range(NT)])
    if dbg is not None:
        p.barrier()
        for i, b in enumerate([qT, kT, ktm, vtm, qdec[0], qdec[1], wT[0], wT[1], aT[0], aT[1], ub[0], ub[1]]):
            p.dma("sp", dbg["bf"][i], b.ap)
        p.dma("sp", dbg["of"], of.ap)
    p.release(m)


def stage_gmlp_mix(p, c, s_gv, s_u, s_g1, lng_dram, lnb_dram, wsT_dram, bs_dram, s_z):
    m = p.mark()
    bsB = p.sb("bsB", D, F32)
    wsT = p.sb("wsT", D, BF16)
    lngT = p.sb("lngT", 16, F32)
    lnbrow = p.sb("lnbrow", D, F32, parts=1)
    rwrow = p.sb("rwrow", D, F32, parts=1)
    p.dma("sp", bsB.ap, bs_dram.partition_broadcast(128), writes=[bsB])
    p.dma("pool", wsT.ap.rearrange("p (g t) -> p g t", g=16), wsT_dram.rearrange("g s t -> s g t"), writes=[wsT])
    for gi in range(16):
        p.dma("sp", lngT.ap[:, gi:gi + 1], lng_dram[0:1, gi * 128:(gi + 1) * 128].rearrange("o d -> d o"), writes=[lngT])
    p.dma("sp", lnbrow.ap, lnb_dram, writes=[lnbrow])
    wsT3 = wsT.ap.rearrange("p (g t) -> p g t", g=16)
    for g4 in range(4):
        bk = p.bank(g4)
        MM(p, bk.ap[0:1, :], c.ones.ap[:, 0:1], wsT.ap[:, g4 * 512:(g4 + 1) * 512], [c.ones, wsT], [bk])
        CP(p, "act", rwrow.ap[:, g4 * 512:(g4 + 1) * 512], bk.ap[0:1, :], [bk], [rwrow])
    ub = [p.sb("mu%d" % i, KC * 512, BF16) for i in range(2)]
    gb = [p.sb("mg%d" % i, KC * 512, BF16) for i in range(2)]
    zb = [p.sb("mz%d" % i, KC * 512, BF16) for i in range(2)]
    gv4 = [[p.sb("gv%d_%d" % (k, i), D, BF16) for i in range(4)] for k in range(2)]
    vl4 = [[p.sb("vl%d_%d" % (k, i), D, BF16) for i in range(4)] for k in range(2)]
    junk = p.sb("mjunk", D, BF16)
    s1 = [p.sb("ms1%d" % i, 4, F32) for i in range(2)]
    s2 = [p.sb("ms2%d" % i, 4, F32) for i in range(2)]
    mean = [p.sb("mmean%d" % i, 4, F32) for i in range(2)]
    msq = [p.sb("mmsq%d" % i, 4, F32) for i in range(2)]
    var = [p.sb("mvar%d" % i, 4, F32) for i in range(2)]
    rstd = [p.sb("mrstd%d" % i, 4, F32) for i in range(2)]
    nmr = [p.sb("mnmr%d" % i, 4, F32) for i in range(2)]
    z1 = [p.sb("mz1%d" % i, 512, F32) for i in range(2)]
    uv = s_u.rearrange("(k p) t -> p k t", p=128)
    gvw = s_g1.rearrange("(k p) t -> p k t", p=128)
    zv = s_z.rearrange("(k p) t -> p k t", p=128)

    def emit_stats(tb):
        k4 = tb % 2
        for ti in range(4):
            n = tb * 4 + ti
            gvi = gv4[k4][ti]
            p.dma("sp", gvi.ap, s_gv[n * 128:(n + 1) * 128, :], writes=[gvi])
            ACTV(p, junk.ap, gvi.ap, AF.Identity, [gvi], [junk, (s1[k4].key, ti)], accum=s1[k4].ap[:, ti:ti + 1])
            ACTV(p, junk.ap, gvi.ap, AF.Square, [gvi], [junk, (s2[k4].key, ti)], accum=s2[k4].ap[:, ti:ti + 1])
        s1k = [(s1[k4].key, ti) for ti in range(4)]
        s2k = [(s2[k4].key, ti) for ti in range(4)]
        TS(p, "dve", mean[k4].ap, s1[k4].ap, 1.0 / D, None, ALU.mult, None, s1k, [mean[k4]])
        TT(p, "dve", msq[k4].ap, mean[k4].ap, mean[k4].ap, ALU.mult, [mean[k4]], [msq[k4]])
        STT(p, "dve", var[k4].ap, s2[k4].ap, 1.0 / D, msq[k4].ap, ALU.mult, ALU.subtract, s2k + [msq[k4]], [var[k4]])
        ACTV(p, rstd[k4].ap, var[k4].ap, AF.Ln, [var[k4]], [rstd[k4]], bias=EPS)
        ACTV(p, rstd[k4].ap, rstd[k4].ap, AF.Exp, [rstd[k4]], [rstd[k4]], scale=-0.5)
        STT(p, "dve", nmr[k4].ap, mean[k4].ap, -1.0, rstd[k4].ap, ALU.mult, ALU.mult, [mean[k4], rstd[k4]], [nmr[k4]])

    def load_ug(tb):
        p.dma("sp", ub[tb % 2].ap.rearrange("p (k t) -> p k t", k=KC), uv[:, :, tb * 512:(tb + 1) * 512], writes=[ub[tb % 2]])
        p.dma("sp", gb[tb % 2].ap.rearrange("p (k t) -> p k t", k=KC), gvw[:, :, tb * 512:(tb + 1) * 512], writes=[gb[tb % 2]])

    emit_stats(0)
    load_ug(0)
    bc = 4
    zc = 0
    for tb in range(NB):
        u, g, z = ub[tb % 2], gb[tb % 2], zb[tb % 2]
        u3 = u.ap.rearrange("p (k t) -> p k t", k=KC)
        z3 = z.ap.rearrange("p (k t) -> p k t", k=KC)
        k4 = tb % 2
        if tb + 1 < NB:
            load_ug(tb + 1)
            emit_stats(tb + 1)
        TT(p, "dve", u.ap, u.ap, g.ap, ALU.mult, [u, g], [u])
        for ti in range(4):
            ACTV(p, vl4[k4][ti].ap, gv4[k4][ti].ap, AF.Identity, [gv4[k4][ti], rstd[k4], nmr[k4]], [vl4[k4][ti]],
                 scale=rstd[k4].ap[:, ti:ti + 1], bias=nmr[k4].ap[:, ti:ti + 1])
        for gi in range(16):
            bk = p.bank(bc % 8)
            bc += 1
            gs = slice(gi * 128, (gi + 1) * 128)
            for ti in range(4):
                MM(p, bk.ap[:, ti * 128:(ti + 1) * 128], vl4[k4][ti].ap[:, gs], wsT3[:, gi, :], [vl4[k4][ti], wsT], [bk],
                   start=(ti == 0), stop=True)
            zz = z1[zc % 2]
            zc += 1
            STT(p, "dve", zz.ap.rearrange("p (i t) -> p i t", i=4), bk.ap.rearrange("p (i t) -> p i t", i=4),
                lngT.ap[:, gi:gi + 1], bsB.ap[:, gs].rearrange("p (o t) -> p o t", o=1).broadcast_to([128, 4, 128]),
                ALU.mult, ALU.add, [bk, lngT, bsB], [zz])
            bk2 = p.bank(bc % 8)
            bc += 1
            MM(p, bk2.ap, lnbrow.ap[0:1, gs], rwrow.ap[0:1, gs].rearrange("p (o t) -> p o t", o=1).broadcast_to([1, 4, 128]),
               [lnbrow, rwrow], [bk2])
            TT(p, "dve", zz.ap, zz.ap, bk2.ap, ALU.add, [zz, bk2], [zz])
            TT(p, "dve", z3[:, gi, :], zz.ap, u3[:, gi, :], ALU.mult, [zz, u], [z])
        p.dma("sp", zv[:, :, tb * 512:(tb + 1) * 512], z3, reads=[z])
    p.release(m)


def stage_gdn_head2(p, c, h, s_aqkv, s_agate, convw, onorm, s_rows, s_eg, s_gl, s_ts, s_y, dbg=None, s_rows16=None):
    import math
    m = p.mark()
    qT = p.sb("qT", L, BF16)
    kT = p.sb("kT", L, BF16)
    ktm = p.sb("ktm", L, BF16)
    vtm = p.sb("vtm", L, BF16)
    ktm3 = ktm.ap.rearrange("p (t d) -> p t d", d=128)
    vtm3 = vtm.ap.rearrange("p (t d) -> p t d", d=128)
    m2 = p.mark()
    xp = [p.sb("xp%d" % i, L + 4, BF16) for i in range(3)]
    xc = [p.sb("xc%d" % i, L, F32) for i in range(2)]
    vT = p.sb("vT", L, BF16)
    dg = [[p.sb("dg%d_%d" % (i, j), 128, BF16) for j in range(5)] for i in range(3)]
    sq = [p.sb("sq%d" % i, 512, BF16) for i in range(2)]
    lnb = [p.sb("lnb%d" % i, 512, F32) for i in range(2)]
    rsb = [p.sb("rsb%d" % i, 512, F32) for i in range(2)]
    cw3 = convw.ap.rearrange("p (b j) -> p b j", j=5)
    bc = 0
    for idx in range(3):
        blk = idx * 8 + h
        x_ = xp[idx]
        MSET(p, "pool", x_.ap[:, 0:2], 0.0, [x_])
        MSET(p, "pool", x_.ap[:, L + 2:L + 4], 0.0, [x_])
        p.dma("sp", x_.ap[:, 2:L + 2], s_aqkv[blk * 128:(blk + 1) * 128, :], writes=[x_])
        for j in range(5):
            TS(p, "dve", dg[idx][j].ap, c.ident.ap, cw3[:, blk, j:j + 1], None, ALU.mult, None, [c.ident, convw], [dg[idx][j]])
    for idx in range(3):
        x_ = xp[idx]
        xc_ = xc[idx % 2]
        for tb in range(NB):
            bk = p.bank(bc % 8)
            bc += 1
            sl = slice(tb * 512, (tb + 1) * 512)
            for j in range(5):
                MM(p, bk.ap, dg[idx][j].ap, x_.ap[:, tb * 512 + j:tb * 512 + j + 512], [dg[idx][j], x_], [bk],
                   start=(j == 0), stop=(j == 4))
            if idx == 2:
                ACTV(p, vT.ap[:, sl], bk.ap, AF.Silu, [bk], [(vT.key, tb)])
            else:
                ACTV(p, xc_.ap[:, sl], bk.ap, AF.Silu, [bk], [(xc_.key, tb)])
        if idx == 2:
            continue
        dstT = qT if idx == 0 else kT
        for tb in range(NB):
            i = bc % 2
            bk = p.bank(bc % 8)
            bc += 1
            sl = slice(tb * 512, (tb + 1) * 512)
            ACTV(p, sq[i].ap, xc_.ap[:, sl], AF.Square, [(xc_.key, tb)], [sq[i]])
            MM(p, bk.ap, c.ones.ap, sq[i].ap, [c.ones, sq[i]], [bk])
            ACTV(p, lnb[i].ap, bk.ap, AF.Ln, [bk], [lnb[i]], bias=EPS)
            if idx == 0:
                ACTV(p, rsb[i].ap, lnb[i].ap, AF.Exp, [lnb[i]], [rsb[i]], scale=-0.5, bias=math.log(128 ** -0.5))
            else:
                ACTV(p, rsb[i].ap, lnb[i].ap, AF.Exp, [lnb[i]], [rsb[i]], scale=-0.5)
            TT(p, "dve", dstT.ap[:, sl], xc_.ap[:, sl], rsb[i].ap, ALU.mult, [(xc_.key, tb), rsb[i]], [(dstT.key, tb)])
    alt = Alt(["act", "dve"])
    for src, dst3, dst in ((kT, ktm3, ktm), (vT, vtm3, vtm)):
        for g8 in range(4):
            bk = p.bank(bc % 8, BF16)
            bc += 1
            for j in range(8):
                t = g8 * 8 + j
                TR(p, bk.ap[:, j * 128:(j + 1) * 128], src.ap[:, t * 128:(t + 1) * 128], c.ident.ap,
                   [(src.key, t // 4), c.ident], [bk])
            CP(p, alt(), dst3[:, g8 * 8:(g8 + 1) * 8, :], bk.ap.rearrange("p (j d) -> p j d", d=128), [bk], [dst])
    p.release(m2)
    qdec = [p.sb("qdec%d" % d, L, BF16) for d in range(2)]
    wT = [p.sb("wT%d" % d, L, BF16) for d in range(2)]
    aT = [p.sb("aT%d" % d, L, BF16) for d in range(2)]
    ub = [p.sb("ub%d" % d, L, BF16) for d in range(2)]
    of = p.sb("of", L, F32)
    tsd = [p.sb("tsd%d" % d, 1024, F32) for d in range(2)]
    glB = [p.sb("glB%d" % d, 32, F32) for d in range(2)]
    NG = 2
    m3 = p.mark()
    RA = [[p.sb("RA%d_%d" % (g, i), 512, BF16, parts=4) for i in range(2)] for g in range(NG)]
    RB = [[p.sb("RB%d_%d" % (g, i), 512, BF16, parts=4) for i in range(2)] for g in range(NG)]
    RB2 = [[p.sb("RB2%d_%d" % (g, i), 512, BF16, parts=4) for i in range(2)] for g in range(NG)]
    egb = [[p.sb("egb%d_%d" % (g, i), 512, F32) for i in range(2)] for g in range(NG)]
    eb = [[p.sb("eb%d_%d" % (g, i), 512, F32) for i in range(2)] for g in range(NG)]
    Qf = [p.sb("Qf%d" % g, 512, BF16) for g in range(NG)]
    Pf = [p.sb("Pf%d" % g, 512, BF16) for g in range(NG)]
    pqQ = [[p.sb("pqQ%d_%d" % (g, i), 512, BF16) for i in range(2)] for g in range(NG)]
    pqP = [[p.sb("pqP%d_%d" % (g, i), 512, BF16) for i in range(2)] for g in range(NG)]
    XX = [[p.sb("XX%d_%d" % (g, i), 512, BF16) for i in range(2)] for g in range(NG)]
    YY = [[p.sb("YY%d_%d" % (g, i), 512, BF16) for i in range(2)] for g in range(NG)]
    Vm = [p.sb("Vm%d" % g, 512, BF16) for g in range(NG)]
    Wm = [p.sb("Wm%d" % g, 512, BF16) for g in range(NG)]
    xs = [p.sb("xs%d" % g, 512, BF16) for g in range(NG)]
    for g in range(NG):
        for i in range(2):
            MSET(p, "pool", RA[g][i].ap, 1.0, [RA[g][i]])
            MSET(p, "pool", RB[g][i].ap, 1.0, [RB[g][i]])
            MSET(p, "pool", RB2[g][i].ap, 1.0, [RB2[g][i]])
    for d in range(2):
        p.dma("sp", tsd[d].ap, s_ts[d], writes=[tsd[d]])
        p.dma("sp", glB[d].ap, s_gl[d, h:h + 1, :].partition_broadcast(128), writes=[glB[d]])
    of3 = of.ap.rearrange("p (t d) -> p t d", d=128)
    ts4 = [tsd[d].ap.rearrange("p (t q h) -> p t q h", t=32, q=4) for d in range(2)]
    aT3 = [aT[d].ap.rearrange("p (t i) -> p t i", i=128) for d in range(2)]
    ub3 = [ub[d].ap.rearrange("p (t i) -> p t i", i=128) for d in range(2)]
    C4 = lambda ap: ap.rearrange("p (c i) -> p c i", c=4)
    cs4 = lambda c_: slice(c_ * 128, (c_ + 1) * 128)

    def mm4(bk, lhs_fn, rhs_fn, reads, first=True):
        for c_ in range(4):
            MM(p, bk.ap[:, cs4(c_)], lhs_fn(c_), rhs_fn(c_), reads, [bk], start=(first and c_ == 0), stop=True)

    for pc in range(8):
        psl = slice(pc * 512, (pc + 1) * 512)
        n0 = pc * 4
        G = []
        for d in range(2):
            g = d
            i = pc % 2
            st = dict(d=d, g=g, ra=RA[g][i], rb=RB[g][i], rb2=RB2[g][i], eg=egb[g][i], eb=eb[g],
                      bA=p.bank(4 * g), bB=p.bank(4 * g + 1), bC=p.bank(4 * g + 2), bD=p.bank(4 * g + 3))
            G.append(st)
            p.dma("sp", st["ra"].ap[0:2, :], s_rows16[d, 0, :, h, psl], writes=[st["ra"]])
            p.dma("sp", st["rb"].ap[2:4, :], s_rows16[d, 1, :, h, psl], writes=[st["rb"]])
            p.dma("sp", st["rb2"].ap[2:4, :], s_rows16[d, 2, :, h, psl], writes=[st["rb2"]])
            p.dma("sp", st["eg"].ap, s_eg[d, h:h + 1, psl].partition_broadcast(128), writes=[st["eg"]])
            TT(p, "pool", qdec[d].ap[:, psl], qT.ap[:, psl], st["eg"].ap, ALU.mult, [(qT.key, pc), st["eg"]], [(qdec[d].key, pc)])
        kr = [(kT.key, pc)]
        kc = lambda c_: kT.ap[:, (n0 + c_) * 128:(n0 + c_ + 1) * 128]
        qc = lambda c_: qT.ap[:, (n0 + c_) * 128:(n0 + c_ + 1) * 128]
        for st in G:
            d, g = st["d"], st["g"]
            ra, rb, rb2 = st["ra"], st["rb"], st["rb2"]
            mm4(st["bA"], kc, kc, kr)
            mm4(st["bB"], lambda c_: ra.ap[0:4, cs4(c_)], lambda c_: rb2.ap[0:4, cs4(c_)], [ra, rb2])
            MM(p, st["bB"].ap, c.ident.ap, c.nms4[d].ap, [c.ident, c.nms4[d]], [st["bB"]], start=False, stop=True)
            mm4(st["bC"], lambda c_: rb2.ap[0:4, cs4(c_)], lambda c_: ra.ap[0:4, cs4(c_)], [ra, rb2])
            MM(p, st["bC"].ap, c.ident.ap, c.nms4[1 - d].ap, [c.ident, c.nms4[1 - d]], [st["bC"]], start=False, stop=True)
            mm4(st["bD"], kc, qc, kr + [(qT.key, pc)])
        for st in G:
            ACTV(p, st["eb"][0].ap, st["bB"].ap, AF.Exp, [st["bB"]], [st["eb"][0]])
            ACTV(p, st["eb"][1].ap, st["bC"].ap, AF.Exp, [st["bC"]], [st["eb"][1]])
        for st in G:
            g = st["g"]
            STT(p, "dve", Qf[g].ap, st["bA"].ap, -1.0, st["eb"][0].ap, ALU.mult, ALU.mult, [st["bA"], st["eb"][0]], [Qf[g]])
            STT(p, "dve", Pf[g].ap, st["bA"].ap, -1.0, st["eb"][1].ap, ALU.mult, ALU.mult, [st["bA"], st["eb"][1]], [Pf[g]])
        for st in G:
            d = st["d"]
            ra, rb = st["ra"], st["rb"]
            mm4(st["bB"], lambda c_: rb.ap[0:4, cs4(c_)], lambda c_: ra.ap[0:4, cs4(c_)], [ra, rb])
            MM(p, st["bB"].ap, c.ident.ap, c.nmiT4[d].ap, [c.ident, c.nmiT4[d]], [st["bB"]], start=False, stop=True)
        for st in G:
            ACTV(p, st["eb"][0].ap, st["bB"].ap, AF.Exp, [st["bB"]], [st["eb"][0]])
        for st in G:
            d, g = st["d"], st["g"]
            TT(p, "dve", aT3[d][:, n0:n0 + 4, :], C4(st["bD"].ap), C4(st["eb"][0].ap), ALU.mult, [st["bD"], st["eb"][0]],
               [(aT[d].key, pc)])
            TT(p, "dve", pqQ[g][0].ap, Qf[g].ap, c.bd4[0].ap, ALU.mult, [Qf[g], c.bd4[0]], [pqQ[g][0]])
            TT(p, "dve", pqP[g][0].ap, Pf[g].ap, c.bd4[0].ap, ALU.mult, [Pf[g], c.bd4[0]], [pqP[g][0]])
            TT(p, "pool", XX[g][0].ap, pqQ[g][0].ap, c.ident4.ap, ALU.add, [pqQ[g][0], c.ident4], [XX[g][0]])
            TT(p, "pool", YY[g][0].ap, pqP[g][0].ap, c.ident4.ap, ALU.add, [pqP[g][0], c.ident4], [YY[g][0]])
        si = 0
        for k in range(3):
            cur, nxt = si % 2, (si + 1) % 2
            si += 1
            kq, kn = k % 2, (k + 1) % 2
            for st in G:
                g = st["g"]
                Q_, P_ = pqQ[g][kq], pqP[g][kq]
                mm4(st["bA"], lambda c_: P_.ap[:, cs4(c_)], lambda c_: Q_.ap[:, cs4(c_)], [P_, Q_])
                mm4(st["bB"], lambda c_: Q_.ap[:, cs4(c_)], lambda c_: P_.ap[:, cs4(c_)], [P_, Q_])
            for st in G:
                g = st["g"]
                CP(p, "dve", pqQ[g][kn].ap, st["bA"].ap, [st["bA"]], [pqQ[g][kn]])
                CP(p, "act", pqP[g][kn].ap, st["bB"].ap, [st["bB"]], [pqP[g][kn]])
            for st in G:
                g = st["g"]
                Qn, Pn = pqQ[g][kn], pqP[g][kn]
                X_, Y_ = XX[g][cur], YY[g][cur]
                mm4(st["bC"], lambda c_: Pn.ap[:, cs4(c_)], lambda c_: X_.ap[:, cs4(c_)], [Pn, X_])
                MM(p, st["bC"].ap, c.ident.ap, X_.ap, [c.ident, X_], [st["bC"]], start=False, stop=True)
                mm4(st["bD"], lambda c_: Qn.ap[:, cs4(c_)], lambda c_: Y_.ap[:, cs4(c_)], [Qn, Y_])
                MM(p, st["bD"].ap, c.ident.ap, Y_.ap, [c.ident, Y_], [st["bD"]], start=False, stop=True)
            for st in G:
                g = st["g"]
                CP(p, "act", XX[g][nxt].ap, st["bC"].ap, [st["bC"]], [XX[g][nxt]])
                CP(p, "dve", YY[g][nxt].ap, st["bD"].ap, [st["bD"]], [YY[g][nxt]])
        for lv in range(3):
            cur, nxt = si % 2, (si + 1) % 2
            si += 1
            last = lv == 2
            for st in G:
                g = st["g"]
                X_, Y_ = XX[g][cur], YY[g][cur]
                if not last:
                    mm4(st["bA"], lambda c_: Pf[g].ap[:, cs4(c_)], lambda c_: X_.ap[:, cs4(c_)], [Pf[g], X_])
                mm4(st["bB"], lambda c_: Qf[g].ap[:, cs4(c_)], lambda c_: Y_.ap[:, cs4(c_)], [Qf[g], Y_])
            for st in G:
                g = st["g"]
                if not last:
                    TT(p, "dve", Vm[g].ap, st["bA"].ap, c.bd4[lv + 1].ap, ALU.mult, [st["bA"], c.bd4[lv + 1]], [Vm[g]])
                TT(p, "dve", Wm[g].ap, st["bB"].ap, c.bd4[lv + 1].ap, ALU.mult, [st["bB"], c.bd4[lv + 1]], [Wm[g]])
            for st in G:
                g = st["g"]
                X_, Y_ = XX[g][cur], YY[g][cur]
                if not last:
                    mm4(st["bC"], lambda c_: Y_.ap[:, cs4(c_)], lambda c_: Vm[g].ap[:, cs4(c_)], [Y_, Vm[g]])
                    MM(p, st["bC"].ap, c.ident.ap, X_.ap, [c.ident, X_], [st["bC"]], start=False, stop=True)
                mm4(st["bD"], lambda c_: X_.ap[:, cs4(c_)], lambda c_: Wm[g].ap[:, cs4(c_)], [X_, Wm[g]])
                MM(p, st["bD"].ap, c.ident.ap, Y_.ap, [c.ident, Y_], [st["bD"]], start=False, stop=True)
            for st in G:
                g = st["g"]
                if not last:
                    CP(p, "act", XX[g][nxt].ap, st["bC"].ap, [st["bC"]], [XX[g][nxt]])
                CP(p, "act", YY[g][nxt].ap, st["bD"].ap, [st["bD"]], [YY[g][nxt]])
        assert si % 2 == 0
        for st in G:
            d, g = st["d"], st["g"]
            TT(p, "pool", C4(xs[g].ap), ktm3[:, n0:n0 + 4, :], ts4[d][:, n0:n0 + 4, 2, h:h + 1].broadcast_to([128, 4, 128]),
               ALU.mult, [ktm, tsd[d]], [xs[g]])
        for st in G:
            d, g = st["d"], st["g"]
            Y_ = YY[g][0]
            mm4(st["bA"], lambda c_: Y_.ap[:, cs4(c_)], lambda c_: vtm3[:, n0 + c_, :], [Y_, vtm])
            mm4(st["bB"], lambda c_: xs[g].ap[:, cs4(c_)], lambda c_: Y_.ap[:, cs4(c_)], [xs[g], Y_])
        for st in G:
            d, g = st["d"], st["g"]
            TT(p, "dve", ub3[d][:, n0:n0 + 4, :], C4(st["bA"].ap), ts4[d][:, n0:n0 + 4, 0, h:h + 1].broadcast_to([128, 4, 128]),
               ALU.mult, [st["bA"], tsd[d]], [(ub[d].key, pc)])
            CP(p, "act", wT[d].ap[:, psl], st["bB"].ap, [st["bB"]], [(wT[d].key, pc)])
    p.release(m3)
    S32 = [p.sb("S32_%d" % d, 128, F32) for d in range(2)]
    Sbf = [p.sb("Sbf_%d" % d, 128, BF16) for d in range(2)]
    vn = [[p.sb("vn_%d_%d" % (d, i), 128, BF16) for i in range(2)] for d in range(2)]
    kdc = [[p.sb("kdc%d_%d" % (d, i), 128, BF16) for i in range(4)] for d in range(2)]
    for d in range(2):
        MSET(p, "pool", S32[d].ap, 0.0, [S32[d]])
        MSET(p, "pool", Sbf[d].ap, 0.0, [Sbf[d]])
    def emit_kd(step):
        for d in range(2):
            n = step if d == 0 else NT - 1 - step
            kd_ = kdc[d][step % 4]
            ACTV(p, kd_.ap, ktm3[:, n, :], AF.Copy, [ktm, tsd[d]], [kd_], scale=ts4[d][:, n, 3, h:h + 1])

    emit_kd(0)
    emit_kd(1)
    for step in range(NT):
        if step + 2 < NT:
            emit_kd(step + 2)
        for d in range(2):
            n = step if d == 0 else NT - 1 - step
            cs = slice(n * 128, (n + 1) * 128)
            b1, b2, b3 = [p.bank(4 * d + i) for i in range(3)]
            v_ = vn[d][step % 2]
            kd_ = kdc[d][step % 4]
            MM(p, b1.ap[:, 0:128], wT[d].ap[:, cs], Sbf[d].ap, [(wT[d].key, n // 4), Sbf[d]], [b1])
            MM(p, b2.ap[:, 0:128], qdec[d].ap[:, cs], Sbf[d].ap, [(qdec[d].key, n // 4), Sbf[d]], [b2], start=True, stop=False)
            STT(p, "dve", v_.ap, b1.ap[:, 0:128], ts4[d][:, n, 1, h:h + 1], ub3[d][:, n, :], ALU.mult, ALU.add,
                [b1, tsd[d], (ub[d].key, n // 4)], [v_])
            MM(p, b3.ap[:, 0:128], kd_.ap, v_.ap, [kd_, v_], [b3])
            MM(p, b2.ap[:, 0:128], aT3[d][:, n, :], v_.ap, [(aT[d].key, n // 4), v_], [b2], start=False, stop=True)
            STT(p, "dve", Sbf[d].ap, S32[d].ap, glB[d].ap[:, n:n + 1], b3.ap[:, 0:128], ALU.mult, ALU.add,
                [S32[d], glB[d], b3], [Sbf[d]])
            STT(p, "dve", S32[d].ap, S32[d].ap, glB[d].ap[:, n:n + 1], b3.ap[:, 0:128], ALU.mult, ALU.add,
                [S32[d], glB[d], b3], [S32[d]])
            if step < NT // 2:
                CP(p, "act", of3[:, n, :], b2.ap[:, 0:128], [b2], [(of.key, n)])
            else:
                TT(p, "dve", of3[:, n, :], b2.ap[:, 0:128], of3[:, n, :], ALU.add, [b2, (of.key, n)], [(of.key, n)])
    sqt = [p.sb("sqt%d" % i, 1024, F32) for i in range(2)]
    ss = p.sb("oss", 32, F32)
    rstd = p.sb("ors", 32, F32)
    onb = p.sb("onb", L, BF16)
    sgp = [p.sb("sgp%d" % i, 1024, BF16) for i in range(2)]
    ysp = [p.sb("ysp%d" % i, 1024, BF16) for i in range(2)]
    for r in range(4):
        sl = slice(r * 1024, (r + 1) * 1024)
        ACTV(p, sqt[r % 2].ap, of.ap[:, sl], AF.Square, [(of.key, n) for n in range(r * 8, r * 8 + 8)], [sqt[r % 2]])
        p.op("dve", lambda e, r=r: e.tensor_reduce(out=ss.ap[:, r * 8:(r + 1) * 8],
                                                    in_=sqt[r % 2].ap.rearrange("p (c i) -> p c i", i=128),
                                                    axis=AX.X, op=ALU.add), [sqt[r % 2]], [ss])
    ACTV(p, rstd.ap, ss.ap, AF.Ln, [ss], [rstd], bias=EPS, scale=1.0 / 128)
    ACTV(p, rstd.ap, rstd.ap, AF.Exp, [rstd], [rstd], scale=-0.5)
    TT(p, "dve", onb.ap.rearrange("p (t i) -> p t i", i=128), of3,
       rstd.ap.rearrange("p (t o) -> p t o", o=1).broadcast_to([128, 32, 128]), ALU.mult,
       [(of.key, n) for n in range(NT)] + [rstd], [onb])
    for r in range(4):
        sl = slice(r * 1024, (r + 1) * 1024)
        bk = p.bank(r % 8, BF16)
        p.dma("sp", sgp[r % 2].ap, s_agate[h * 128:(h + 1) * 128, sl], writes=[sgp[r % 2]])
        for j in range(8):
            t = r * 8 + j
            TR(p, bk.ap[:, j * 128:(j + 1) * 128], onb.ap[:, t * 128:(t + 1) * 128], c.ident.ap, [onb, c.ident], [bk])
        STT(p, "dve", ysp[r % 2].ap, bk.ap, onorm.ap[:, 0:1], sgp[r % 2].ap, ALU.mult, ALU.mult, [bk, onorm, sgp[r % 2]],
            [ysp[r % 2]])
        p.dma("sp", s_y[h * 128:(h + 1) * 128, sl], ysp[r % 2].ap, reads=[ysp[r % 2]])
    p.release(m)


def make_consts():
    i = np.arange(128)
    ident = np.eye(128, dtype=np.float32)
    perm = np.zeros((128, 128), np.float32)
    for d in range(128):
        hh = d // 64
        r = d % 64
        perm[hh * 64 + (r + 32) % 64, d] = 1.0
    NEG = -30000.0
    nmiT_f = np.where(i[None, :] >= i[:, None], 0.0, NEG).astype(np.float32)
    nms_f = np.where(i[:, None] > i[None, :], 0.0, NEG).astype(np.float32)
    nmiT_b = np.where(i[None, :] <= i[:, None], 0.0, NEG).astype(np.float32)
    nms_b = np.where(i[:, None] < i[None, :], 0.0, NEG).astype(np.float32)
    offd = (1.0 - ident).astype(np.float32)
    def bd(sz):
        return ((i[:, None] // sz) == (i[None, :] // sz)).astype(np.float32)
    bd16, bd32, bd64 = bd(16), bd(32), bd(64)
    extra = [bd16, bd32 - bd16, bd64 - bd32, 1.0 - bd64]
    return np.ascontiguousarray(np.concatenate([ident, perm, nmiT_f, nms_f, nmiT_b, nms_b, offd] + extra, axis=1))


def rope_tables():
    t = np.arange(L)
    row = (t // 64).astype(np.float32)
    col = (t % 64).astype(np.float32)
    nf = 32
    inv = (10000.0 ** (-np.arange(nf, dtype=np.float32) / nf)).astype(np.float32)
    ar = (row[:, None] * inv).astype(np.float32)
    ac = (col[:, None] * inv).astype(np.float32)
    C = np.zeros((128, L), np.float32)
    S = np.zeros((128, L), np.float32)
    C[0:32] = np.cos(ar).T; C[32:64] = np.cos(ar).T; C[64:96] = np.cos(ac).T; C[96:128] = np.cos(ac).T
    S[0:32] = -np.sin(ar).T; S[32:64] = np.sin(ar).T; S[64:96] = -np.sin(ac).T; S[96:128] = np.sin(ac).T
    return C, S


def build_full(upto=99, heads=range(8), dbg_out=()):
    nc = bass.Bass("TRN2", target_bir_lowering=False)
    I = lambda n, s, d=F32: nc.dram_tensor(n, s, d, kind="ExternalInput").ap()

    def SC(n, s, d):
        kind = "ExternalOutput" if n in dbg_out else "Internal"
        return nc.dram_tensor(n, s, d, kind=kind).ap()

    x = I("x", [L, D]); norm0_g = I("norm0_g", [1, D]); w_in0 = I("w_in0", [D, 6688])
    convwT = I("convwT", [3072, 5]); a_log = I("a_log0", [2, 8]); dt_bias = I("dt_bias0", [2, 8])
    onorm_g = I("a_onorm_g0", [1, 128]); qn_g = I("b_qnorm_g0", [1, 128]); kn_g = I("b_knorm_g0", [1, 128])
    w_out0 = I("w_out0", [D, D]); norm1_g = I("norm1_g", [1, D]); w_in1 = I("w_in1", [D, 6144])
    ln_g = I("c_ln_g1", [1, D]); ln_b = I("c_ln_b1", [1, D]); wsT = I("c_wsT", [16, 128, 128]); bs = I("c_bs1", [1, D])
    w_out1 = I("w_out1", [D, D]); cst = I("cst", [128, 1408]); ropec = I("rope_c", [128, L]); ropes = I("rope_s", [128, L])
    out = nc.dram_tensor("out", [L, D], F32, kind="ExternalOutput").ap()

    s_aqkv = SC("s_aqkv", [3072, L], BF16); s_agate = SC("s_agate", [1024, L], BF16); s_alog = SC("s_alog", [32, L], F32)
    s_bq = SC("s_bq", [1024, L], BF16); s_bk = SC("s_bk", [256, L], BF16); s_bv = SC("s_bv", [L, 256], BF16)
    s_bgate = SC("s_bgate", [1024, L], BF16); s_qk = SC("s_qk", [1280, L], BF16); s_y = SC("s_y", [2048, L], BF16)
    s_x1 = SC("s_x1", [L, D], F32)
    s_rows = SC("s_rows", [2, 3, 8, L], F32); s_eg = SC("s_eg", [2, 8, L], F32); s_gl = SC("s_gl", [2, 8, 32], F32)
    s_ts = SC("s_ts", [2, 128, 1024], F32)
    s_rows16 = SC("s_rows16", [2, 3, 2, 8, L], BF16)
    s_u = SC("s_u", [D, L], BF16); s_g1 = SC("s_g1", [D, L], BF16); s_gv = SC("s_gv", [L, D], BF16); s_z = SC("s_z", [D, L], BF16)

    p = Prog(nc)
    c = setup_consts(p, cst)
    convw = p.sb("convw", 120, F32)
    p.dma("sp", convw.ap.rearrange("p (b j) -> p b j", j=5), convwT.rearrange("(b p) j -> p b j", p=128), writes=[convw])
    onorm = p.sb("onorm", 1, F32)
    p.dma("sp", onorm.ap, onorm_g.rearrange("o d -> d o"), writes=[onorm])

    m = p.mark()
    hT = p.sb("hT", KC * L, BF16)
    wb0 = [p.sb("w%d" % i, KC * 512, BF16) for i in range(2)]
    w0v = w_in0.rearrange("(k p) c -> p k c", p=128)
    bctr = [0]
    sk_dir = make_fm_direct_sink(p, lambda job, sub: job["dst"][job["r0"] + sub * 128:job["r0"] + sub * 128 + 128, :])
    job0 = dict(c0=0, n=512, mode="fm", sink=sk_dir, dst=s_aqkv, r0=0)
    w30 = proj_load_w(p, wb0[0], w0v, job0)

    def pre0(tb):
        for sub in range(4):
            proj_fm_block(p, hT, wb0[0], w30, job0, sub, tb, bctr)

    stage_norm_T(p, c, x, norm0_g, hT, after_block=pre0)
    sk_plain = make_fm_sink(p, lambda job, sub: job["dst"][job["r0"] + sub * 128:job["r0"] + sub * 128 + min(128, job["n"] - sub * 128), :])
    sk_log = make_fm_sink(p, lambda job, sub: job["dst"][0:32, :], dtype=F32, nbuf=1)
    sk_tm = make_tm_sink(p, lambda job, t: job["dst"][t * 128:(t + 1) * 128, 0:job["n"]])
    jobs = [job0]
    for i in range(1, 6):
        jobs.append(dict(c0=i * 512, n=512, mode="fm", sink=sk_plain, dst=s_aqkv, r0=i * 512))
    for i in range(2):
        jobs.append(dict(c0=3072 + i * 512, n=512, mode="fm", sink=sk_plain, dst=s_agate, r0=i * 512, func=AF.Silu))
    jobs.append(dict(c0=4096, n=32, mode="fm", sink=sk_log, dst=s_alog, r0=0))
    for i in range(2):
        jobs.append(dict(c0=4128 + i * 512, n=512, mode="fm", sink=sk_plain, dst=s_bq, r0=i * 512))
    jobs.append(dict(c0=5152, n=256, mode="fm", sink=sk_plain, dst=s_bk, r0=0))
    jobs.append(dict(c0=5408, n=256, mode="tm", sink=sk_tm, dst=s_bv))
    for i in range(2):
        jobs.append(dict(c0=5664 + i * 512, n=512, mode="fm", sink=sk_plain, dst=s_bgate, r0=i * 512, func=AF.Silu))
    if upto >= 1:
        stage_proj(p, hT, w0v, jobs, bank_ctr=bctr, wb=wb0, skip_first=True)
    p.release(m)
    if upto >= 2:
        stage_qkprep(p, c, s_bq, s_bk, qn_g, kn_g, ropec, ropes, s_qk)
        stage_attn(p, c, s_qk, s_bv, s_bgate, s_y)
    if upto >= 3:
        mg = p.mark()
        setup_gdn_consts(p, c, cst)
        stage_gdn_scalars(p, c, s_alog, a_log, dt_bias, s_rows, s_eg, s_gl, s_ts, s_rows16=s_rows16)
        for h in heads:
            stage_gdn_head2(p, c, h, s_aqkv, s_agate, convw, onorm, s_rows, s_eg, s_gl, s_ts, s_y, s_rows16=s_rows16)
        p.release(mg)
    if upto >= 4:
        stage_outproj(p, s_y, w_out0, x, s_x1)
    if upto >= 5:
        m = p.mark()
        hT = p.sb("hT1", KC * L, BF16)
        wb1 = [p.sb("w1_%d" % i, KC * 512, BF16) for i in range(2)]
        w1v = w_in1.rearrange("(k p) c -> p k c", p=128)
        sk_dir1 = make_fm_direct_sink(p, lambda job, sub: job["dst"][job["r0"] + sub * 128:job["r0"] + sub * 128 + 128, :])
        job10 = dict(c0=0, n=512, mode="fm", sink=sk_dir1, dst=s_u, r0=0, func=AF.Gelu)
        w310 = proj_load_w(p, wb1[0], w1v, job10)

        def pre1(tb):
            for sub in range(4):
                proj_fm_block(p, hT, wb1[0], w310, job10, sub, tb, bctr)

        stage_norm_T(p, c, s_x1, norm1_g, hT, after_block=pre1)
        sk_gelu = make_fm_sink(p, lambda job, sub: job["dst"][job["r0"] + sub * 128:job["r0"] + sub * 128 + 128, :], func=AF.Gelu)
        sk_silu1 = sk_gelu
        sk_tmg = make_tm_sink(p, lambda job, t: job["dst"][t * 128:(t + 1) * 128, job["cc"]:job["cc"] + 512], func=AF.Gelu)
        jobs = [job10]
        for i in range(1, 4):
            jobs.append(dict(c0=i * 512, n=512, mode="fm", sink=sk_gelu, dst=s_u, r0=i * 512))
        for i in range(4):
            jobs.append(dict(c0=2048 + i * 512, n=512, mode="tm", sink=sk_tmg, dst=s_gv, cc=i * 512))
        for i in range(4):
            jobs.append(dict(c0=4096 + i * 512, n=512, mode="fm", sink=sk_silu1, dst=s_g1, r0=i * 512, func=AF.Silu))
        stage_proj(p, hT, w1v, jobs, bank_ctr=bctr, wb=wb1, skip_first=True)
        p.release(m)
    if upto >= 6:
        stage_gmlp_mix(p, c, s_gv, s_u, s_g1, ln_g, ln_b, wsT, bs, s_z)
        stage_outproj(p, s_z, w_out1, s_x1, out)
    p.emit()
    print("ops:", {e: len(p.ops[e]) for e in ENGS}, "waits", p.nwaits, flush=True)
    return nc


def make_in_maps(inputs):
    C, S = rope_tables()
    shared = dict(
        norm0_g=np.ascontiguousarray(inputs["norm0_g"][0:1]), w_in0=np.ascontiguousarray(inputs["w_in0"][0]),
        convwT=np.ascontiguousarray(inputs["conv0_w"][0].T), a_log0=np.ascontiguousarray(inputs["a_log0"][0]),
        dt_bias0=np.ascontiguousarray(inputs["dt_bias0"][0]), a_onorm_g0=np.ascontiguousarray(inputs["a_onorm_g0"][0:1]),
        b_qnorm_g0=np.ascontiguousarray(inputs["b_qnorm_g0"][0:1]), b_knorm_g0=np.ascontiguousarray(inputs["b_knorm_g0"][0:1]),
        w_out0=np.ascontiguousarray(inputs["w_out0"][0]), norm1_g=np.ascontiguousarray(inputs["norm1_g"][0:1]),
        w_in1=np.ascontiguousarray(inputs["w_in1"][0]), c_ln_g1=np.ascontiguousarray(inputs["c_ln_g1"][0:1]),
        c_ln_b1=np.ascontiguousarray(inputs["c_ln_b1"][0:1]),
        c_wsT=np.ascontiguousarray(np.transpose(inputs["c_ws1"][0], (0, 2, 1))),
        c_bs1=np.ascontiguousarray(inputs["c_bs1"][0].reshape(1, D)), w_out1=np.ascontiguousarray(inputs["w_out1"][0]),
        cst=make_consts(), rope_c=C, rope_s=S)
    maps = []
    for b in range(inputs["x"].shape[0]):
        mp = dict(shared)
        mp["x"] = np.ascontiguousarray(inputs["x"][b])
        maps.append(mp)
    return maps


def kernel(**inputs):
    from concourse.bass_utils import run_bass_kernel_spmd
    inputs = {k: np.asarray(v, dtype=np.float32) for k, v in inputs.items()}
    nc = build_full()
    maps = make_in_maps(inputs)
    res = run_bass_kernel_spmd(nc, maps, core_ids=list(range(len(maps))))
    return np.stack([np.asarray(r["out"], dtype=np.float32) for r in res.results], axis=0)
```

```python
import numpy as np
import concourse.bass as bass
import concourse.mybir as mybir

F32 = mybir.dt.float32
BF16 = mybir.dt.bfloat16
AF = mybir.ActivationFunctionType
ALU = mybir.AluOpType
AX = mybir.AxisListType

ENGS = ["pe", "act", "dve", "pool", "sp"]
SEM_CAP = 16000
N_DMA_SEMS = 24


class Buf:
    __slots__ = ("ap", "key")

    def __init__(self, ap, key):
        self.ap = ap
        self.key = key


class Prog:
    def __init__(self, nc):
        self.nc = nc
        self.ops = {e: [] for e in ENGS}
        self.cnt = {e: 0 for e in ENGS}
        self.waited = {e: {} for e in ENGS}
        self.lastw = {}
        self.rd = {}
        self.sems = {}
        self.dma_rr = {"hw": 0, "sw": 0}
        self.dma_cnt = {}
        self.nwaits = 0
        self.arena_words = 53000
        self.arena = nc.alloc_sbuf_tensor("arena", [128, self.arena_words], F32)
        self.top = 0
        self.psum = nc.alloc_psum_tensor("psum", [128, 4096], F32)
        self.uid = 0

    def mark(self):
        return self.top

    def release(self, m):
        self.barrier()
        self.top = m

    def sb(self, name, free_elems, dtype, parts=128):
        esz = 4 if dtype == F32 else 2
        nbytes = free_elems * esz
        nbytes = (nbytes + 63) // 64 * 64
        off = self.top
        self.top += nbytes
        assert self.top <= self.arena_words * 4, f"SBUF overflow at {name}: {self.top}"
        ap = self.arena[0:parts, off // 4:(off + nbytes) // 4]
        if dtype != F32:
            ap = ap.bitcast(dtype)
        ap = ap[:, 0:free_elems]
        self.uid += 1
        return Buf(ap, (name, self.uid))

    def bank(self, b, dtype=F32):
        ap = self.psum[:, b * 512:(b + 1) * 512]
        if dtype != F32:
            ap = ap.bitcast(dtype)
        return Buf(ap, ("psum", b))

    def _sem(self, key):
        if key not in self.sems:
            self.sems[key] = self.nc.alloc_semaphore("s_%s_%s" % key)
        return self.sems[key]

    def _deps(self, eng, reads, writes):
        deps = {}

        def add(tok):
            if tok is None:
                return
            k, v = tok
            if eng == "pe" and k[0] == "pe":
                return
            if deps.get(k, 0) < v:
                deps[k] = v

        for b in reads:
            add(self.lastw.get(b))
        for b in writes:
            add(self.lastw.get(b))
            for k, v in self.rd.get(b, {}).items():
                add((k, v))
        out = []
        w = self.waited[eng]
        for k, v in deps.items():
            if w.get(k, 0) < v:
                w[k] = v
                out.append((k, v))
        self.nwaits += len(out)
        return out

    def _commit(self, tok, reads, writes):
        k, v = tok
        for b in reads:
            d = self.rd.setdefault(b, {})
            if d.get(k, 0) < v:
                d[k] = v
        for b in writes:
            self.lastw[b] = tok
            self.rd[b] = {}

    @staticmethod
    def _keys(bufs):
        out = []
        for b in bufs:
            if b is None:
                continue
            if isinstance(b, Buf):
                out.append(b.key)
            else:
                out.append(b)
        return out

    def op(self, eng, fn, reads=(), writes=()):
        reads = self._keys(reads)
        writes = self._keys(writes)
        waits = self._deps(eng, reads, writes)
        n = self.cnt[eng]
        self.cnt[eng] = n + 1
        tok = ((eng, n // SEM_CAP), n % SEM_CAP + 1)
        self.ops[eng].append((waits, fn, (tok[0], 1)))
        self._commit(tok, reads, writes)
        return tok

    def dma(self, q, out, in_, reads=(), writes=()):
        reads = self._keys(reads)
        writes = self._keys(writes)
        cls = "sw" if q == "pool" else "hw"
        nsem = 8 if cls == "sw" else N_DMA_SEMS
        s = self.dma_rr[cls]
        self.dma_rr[cls] = (s + 1) % nsem
        key = ("dma" + cls, s)
        waits = self._deps(q, reads, writes)
        prev = self.dma_cnt.get(key, 0)
        if prev and self.waited[q].get(key, 0) < prev * 16:
            self.waited[q][key] = prev * 16
            waits.append((key, prev * 16))
        self.dma_cnt[key] = prev + 1
        tok = (key, (prev + 1) * 16)
        if q == "pool":
            fn = lambda e, o=out, i=in_: e.dma_start(out=o, in_=i)
        else:
            fn = lambda e, o=out, i=in_: e.dma_start(out=o, in_=i)
        self.ops[q].append((waits, fn, (key, 16)))
        self._commit(tok, reads, writes)
        return tok

    def barrier(self):
        toks = []
        for e in ENGS:
            n = self.cnt[e]
            if n:
                toks.append(((e, (n - 1) // SEM_CAP), (n - 1) % SEM_CAP + 1))
        for key, cnt in self.dma_cnt.items():
            toks.append((key, cnt * 16))
        for e in ENGS:
            waits = []
            for k, v in toks:
                if k[0] == e and e != "sp":
                    pass
                if self.waited[e].get(k, 0) < v:
                    self.waited[e][k] = v
                    waits.append((k, v))
            if waits:
                self.ops[e].append((waits, None, None))

    def emit(self):
        nc = self.nc
        self.barrier()

        def run(e, eng):
            for waits, fn, inc in self.ops[e]:
                for k, v in waits:
                    eng.wait_ge(self._sem(k), v)
                if fn is None:
                    continue
                ins = fn(eng)
                if inc is not None:
                    ins.then_inc(self._sem(inc[0]), inc[1])

        for e in ENGS:
            for waits, fn, inc in self.ops[e]:
                for k, v in waits:
                    self._sem(k)
                if inc is not None:
                    self._sem(inc[0])

        with nc.Block() as block:
            @block.tensor
            def _(eng):
                run("pe", eng)

            @block.scalar
            def _(eng):
                run("act", eng)

            @block.vector
            def _(eng):
                run("dve", eng)

            @block.gpsimd
            def _(eng):
                run("pool", eng)

            @block.sync
            def _(eng):
                run("sp", eng)


EPS = 1e-6
L = 4096
D = 2048
NT = 32
NB = 8
KC = 16


class Ctx:
    pass


def setup_consts(p, cst_dram):
    c = Ctx()
    c.ident = p.sb("ident", 128, BF16)
    c.identf = p.sb("identf", 128, F32)
    c.ones = p.sb("ones", 128, BF16)
    c.onesm = p.sb("onesm", 128, BF16)
    c.perm = p.sb("perm", 128, BF16)
    p.dma("pool", c.ident.ap, cst_dram[:, 0:128], writes=[c.ident])
    p.dma("sp", c.identf.ap, cst_dram[:, 0:128], writes=[c.identf])
    p.dma("pool", c.perm.ap, cst_dram[:, 128:256], writes=[c.perm])
    p.op("pool", lambda e: e.memset(c.ones.ap, 1.0), writes=[c.ones])
    p.op("pool", lambda e: e.memset(c.onesm.ap, 1.0 / 128), writes=[c.onesm])
    c.identbf = c.ident
    c.onesf = p.sb("onesf", 128, F32)
    p.op("pool", lambda e: e.memset(c.onesf.ap, 1.0), writes=[c.onesf])
    return c


def setup_gdn_consts(p, c, cst_dram):
    c.nmiT4 = [p.sb("nmiT4_%d" % d, 512, BF16) for d in range(2)]
    c.nms4 = [p.sb("nms4_%d" % d, 512, BF16) for d in range(2)]
    c.bd4 = [p.sb("bd4_%d" % i, 512, BF16) for i in range(4)]
    c.ident4 = p.sb("ident4", 512, BF16)
    for r in range(4):
        rs = slice(r * 128, (r + 1) * 128)
        p.dma("pool", c.ident4.ap[:, rs], cst_dram[:, 0:128], writes=[c.ident4])
        for d in range(2):
            p.dma("pool", c.nmiT4[d].ap[:, rs], cst_dram[:, 256 + d * 256:384 + d * 256], writes=[c.nmiT4[d]])
            p.dma("pool", c.nms4[d].ap[:, rs], cst_dram[:, 384 + d * 256:512 + d * 256], writes=[c.nms4[d]])
        for i in range(4):
            p.dma("pool", c.bd4[i].ap[:, rs], cst_dram[:, 896 + i * 128:896 + (i + 1) * 128], writes=[c.bd4[i]])


def stage_norm_T(p, c, x_dram, g_dram, hT, after_block=None):
    m = p.mark()
    gB = p.sb("gB", D, F32)
    p.dma("sp", gB.ap, g_dram.partition_broadcast(128), writes=[gB])
    xb = [p.sb("xt%d" % i, D, F32) for i in range(2)]
    xn = [p.sb("xn%d" % i, D, BF16) for i in range(2)]
    junk = p.sb("junk", D, BF16)
    ss = [p.sb("ss%d" % i, 1, F32) for i in range(2)]
    rs = [p.sb("rs%d" % i, 1, F32) for i in range(2)]
    hT3 = hT.ap.rearrange("p (k t) -> p k t", k=KC)
    for t in range(NT):
        x, n_, s, r = xb[t % 2], xn[t % 2], ss[t % 2], rs[t % 2]
        p.dma("sp", x.ap, x_dram[t * 128:(t + 1) * 128, :], writes=[x])
        p.op("act", lambda e, x=x, s=s: e.activation(out=junk.ap, in_=x.ap, func=AF.Square, accum_out=s.ap),
             reads=[x], writes=[junk, s])
        p.op("act", lambda e, s=s, r=r: e.activation(out=r.ap, in_=s.ap, func=AF.Ln, bias=EPS, scale=1.0 / D),
             reads=[s], writes=[r])
        p.op("act", lambda e, r=r: e.activation(out=r.ap, in_=r.ap, func=AF.Exp, scale=-0.5), reads=[r], writes=[r])
        p.op("dve", lambda e, x=x, r=r, n_=n_: e.scalar_tensor_tensor(out=n_.ap, in0=x.ap, scalar=r.ap[:, 0:1],
                                                                       in1=gB.ap, op0=ALU.mult, op1=ALU.mult),
             reads=[x, r, gB], writes=[n_])
        for half in range(2):
            bk = p.bank((2 * t + half) % 8, BF16)
            for j in range(8):
                kc = half * 8 + j
                p.op("pe", lambda e, bk=bk, j=j, kc=kc, n_=n_: e.transpose(
                    out=bk.ap[:, j * 128:(j + 1) * 128], in_=n_.ap[:, kc * 128:(kc + 1) * 128], identity=c.ident.ap),
                    reads=[n_, c.ident], writes=[bk])
            dst = hT3[:, half * 8:(half + 1) * 8, t * 128:(t + 1) * 128]
            src = bk.ap.rearrange("p (j c) -> p j c", c=128)
            eng = "act" if half == 0 else "dve"
            if eng == "act":
                p.op("act", lambda e, dst=dst, src=src: e.copy(out=dst, in_=src), reads=[bk], writes=[(hT.key, t // 4)])
            else:
                p.op("dve", lambda e, dst=dst, src=src: e.tensor_copy(out=dst, in_=src), reads=[bk],
                     writes=[(hT.key, t // 4)])
        if after_block is not None and t % 4 == 3 and t >= 7:
            after_block(t // 4 - 1)
    if after_block is not None:
        after_block(NB - 1)
    p.release(m)


def proj_load_w(p, w, w_view, job):
    n = job["n"]
    w3 = w.ap[:, 0:KC * n].rearrange("p (k c) -> p k c", k=KC)
    p.dma("pool", w3, w_view[:, :, job["c0"]:job["c0"] + n], writes=[w])
    return w3


def proj_fm_block(p, hT, w, w3, job, sub, tb, bank_ctr):
    hT3 = hT.ap.rearrange("p (k t) -> p k t", k=KC)
    mcols = min(128, job["n"] - sub * 128)
    b = bank_ctr[0] % 8
    bank_ctr[0] += 1
    bk = p.bank(b)
    for kc in range(KC):
        p.op("pe", lambda e, bk=bk, kc=kc: e.matmul(
            bk.ap[0:mcols, :], lhsT=w3[:, kc, sub * 128:sub * 128 + mcols],
            rhs=hT3[:, kc, tb * 512:(tb + 1) * 512], start=(kc == 0), stop=(kc == KC - 1)),
            reads=[w, (hT.key, tb)], writes=[bk])
    job["sink"](p, bk, job, sub, tb)


def stage_proj(p, hT, w_view, jobs, bank_ctr=[0], wb=None, skip_first=False):
    m = p.mark()
    if wb is None:
        wb = [p.sb("w%d" % i, KC * 512, BF16) for i in range(2)]
    hT3 = hT.ap.rearrange("p (k t) -> p k t", k=KC)
    for ji, job in enumerate(jobs):
        if skip_first and ji == 0:
            continue
        w = wb[ji % 2]
        n = job["n"]
        w3 = proj_load_w(p, w, w_view, job)
        if job["mode"] == "fm":
            nsub = (n + 127) // 128
            for sub in range(nsub):
                for tb in range(NB):
                    proj_fm_block(p, hT, w, w3, job, sub, tb, bank_ctr)
        else:
            for t in range(NT):
                b = bank_ctr[0] % 8
                bank_ctr[0] += 1
                bk = p.bank(b)
                for kc in range(KC):
                    p.op("pe", lambda e, bk=bk, kc=kc, t=t, w3=w3, n=n: e.matmul(
                        bk.ap[:, 0:n], lhsT=hT3[:, kc, t * 128:(t + 1) * 128], rhs=w3[:, kc, :],
                        start=(kc == 0), stop=(kc == KC - 1)),
                        reads=[w, (hT.key, t // 4)], writes=[bk])
                job["sink"](p, bk, job, 0, t)
    p.release(m)


def make_fm_direct_sink(p, dst, nbuf=4):
    stg = [p.sb("dstg%d" % i, 512, BF16) for i in range(nbuf)]
    st = {"i": 0}
    alt = Alt(["act", "dve"])

    def sink(p, bk, job, sub, tb):
        s_ = stg[st["i"] % nbuf]
        st["i"] += 1
        f_ = job.get("func")
        if f_ is not None:
            ACTV(p, s_.ap, bk.ap, f_, [bk], [s_])
        else:
            CP(p, alt(), s_.ap, bk.ap, [bk], [s_])
        p.dma("sp", dst(job, sub)[:, tb * 512:(tb + 1) * 512], s_.ap, reads=[s_])

    return sink


def TT(p, eng, out, in0, in1, op, reads, writes):
    return p.op(eng, lambda e: e.tensor_tensor(out=out, in0=in0, in1=in1, op=op), reads, writes)


def TS(p, eng, out, in0, s1, s2, op0, op1, reads, writes):
    if s2 is None:
        return p.op(eng, lambda e: e.tensor_scalar(out=out, in0=in0, scalar1=s1, scalar2=None, op0=op0), reads, writes)
    return p.op(eng, lambda e: e.tensor_scalar(out=out, in0=in0, scalar1=s1, scalar2=s2, op0=op0, op1=op1), reads, writes)


def STT(p, eng, out, in0, scalar, in1, op0, op1, reads, writes):
    return p.op(eng, lambda e: e.scalar_tensor_tensor(out=out, in0=in0, scalar=scalar, in1=in1, op0=op0, op1=op1),
                reads, writes)


def ACTV(p, out, in_, func, reads, writes, bias=None, scale=None, accum=None):
    kw = {}
    if bias is not None:
        kw["bias"] = bias
    if scale is not None:
        kw["scale"] = scale
    if accum is not None:
        kw["accum_out"] = accum
    return p.op("act", lambda e: e.activation(out=out, in_=in_, func=func, **kw), reads, writes)


def MM(p, out, lhsT, rhs, reads, writes, start=True, stop=True):
    return p.op("pe", lambda e: e.matmul(out, lhsT=lhsT, rhs=rhs, start=start, stop=stop, skip_group_check=True),
                reads, writes)


def TR(p, out, in_, ident, reads, writes):
    return p.op("pe", lambda e: e.transpose(out=out, in_=in_, identity=ident), reads, writes)


def CP(p, eng, out, in_, reads, writes):
    if eng == "act":
        return p.op("act", lambda e: e.copy(out=out, in_=in_), reads, writes)
    return p.op(eng, lambda e: e.tensor_copy(out=out, in_=in_), reads, writes)


def MSET(p, eng, ap, val, writes):
    return p.op(eng, lambda e: e.memset(ap, val), (), writes)


class Alt:
    def __init__(self, engs):
        self.engs = engs
        self.i = 0

    def __call__(self):
        e = self.engs[self.i % len(self.engs)]
        self.i += 1
        return e


def make_fm_sink(p, dst, func=None, dtype=BF16, parts=128, nbuf=2):
    stg = [p.sb("stg%d" % i, L, dtype) for i in range(nbuf)]
    st = {"i": 0}
    alt = Alt(["act", "dve"])

    def sink(p, bk, job, sub, tb):
        s = stg[st["i"] % nbuf]
        mcols = min(128, job["n"] - sub * 128)
        o = s.ap[0:mcols, tb * 512:(tb + 1) * 512]
        i = bk.ap[0:mcols, :]
        f_ = job.get("func", func)
        if f_ is not None:
            ACTV(p, o, i, f_, [bk], [(s.key, tb)])
        else:
            CP(p, alt(), o, i, [bk], [(s.key, tb)])
        if tb == NB - 1:
            p.dma("sp", dst(job, sub), s.ap[0:mcols, :], reads=[(s.key, i) for i in range(NB)])
            st["i"] += 1

    return sink


def make_tm_sink(p, dst, func=None, dtype=BF16):
    stg = [p.sb("tstg%d" % i, 512, dtype) for i in range(3)]
    alt = Alt(["act", "dve"])
    st = {"i": 0}

    def sink(p, bk, job, sub, t):
        s = stg[st["i"] % 3]
        st["i"] += 1
        n = job["n"]
        if func is not None:
            ACTV(p, s.ap[:, 0:n], bk.ap[:, 0:n], func, [bk], [s])
        else:
            CP(p, alt(), s.ap[:, 0:n], bk.ap[:, 0:n], [bk], [s])
        p.dma("sp", dst(job, t), s.ap[:, 0:n], reads=[s])

    return sink


def stage_outproj(p, yT_dram, w_dram, xres_dram, out_dram):
    m = p.mark()
    w = p.sb("wout", KC * D, BF16)
    w3 = w.ap.rearrange("p (k c) -> p k c", k=KC)
    wv = w_dram.rearrange("(k p) c -> p k c", p=128)
    for cg in range(4):
        p.dma("pool", w3[:, :, cg * 512:(cg + 1) * 512], wv[:, :, cg * 512:(cg + 1) * 512], writes=[(w.key, cg)])
    yb = [p.sb("yb%d" % i, KC * 512, BF16) for i in range(2)]
    xb = [p.sb("xr%d" % i, D, F32) for i in range(2)]
    ob = [p.sb("ob%d" % i, D, F32) for i in range(2)]
    yv = yT_dram.rearrange("(k p) t -> p k t", p=128)
    bc = 0

    def load_y(tb):
        y = yb[tb % 2]
        p.dma("sp", y.ap.rearrange("p (k t) -> p k t", k=KC), yv[:, :, tb * 512:(tb + 1) * 512], writes=[y])

    def load_x(t):
        p.dma("sp", xb[t % 2].ap, xres_dram[t * 128:(t + 1) * 128, :], writes=[xb[t % 2]])

    load_y(0)
    load_x(0)
    for tb in range(NB):
        y = yb[tb % 2]
        y3 = y.ap.rearrange("p (k t) -> p k t", k=KC)
        if tb + 1 < NB:
            load_y(tb + 1)
        for ti in range(4):
            t = tb * 4 + ti
            xr, o = xb[t % 2], ob[t % 2]
            if t + 1 < NT:
                load_x(t + 1)
            for cg in range(4):
                bk = p.bank(bc % 8)
                bc += 1
                for kc in range(KC):
                    MM(p, bk.ap, y3[:, kc, ti * 128:(ti + 1) * 128], w3[:, kc, cg * 512:(cg + 1) * 512],
                       [y, (w.key, cg)], [bk], start=(kc == 0), stop=(kc == KC - 1))
                TT(p, "dve", o.ap[:, cg * 512:(cg + 1) * 512], bk.ap, xr.ap[:, cg * 512:(cg + 1) * 512], ALU.add,
                   [bk, xr], [(o.key, cg)])
            p.dma("sp", out_dram[t * 128:(t + 1) * 128, :], o.ap, reads=[(o.key, i) for i in range(4)])
    p.release(m)


def stage_qkprep(p, c, s_bq, s_bk, qg_dram, kg_dram, ropec, ropes, s_qk):
    m = p.mark()
    Ct = p.sb("ropeC", L, F32)
    St = p.sb("ropeS", L, F32)
    p.dma("sp", Ct.ap, ropec, writes=[Ct])
    p.dma("sp", St.ap, ropes, writes=[St])
    gq = p.sb("gq", 1, F32)
    gk = p.sb("gk", 1, F32)
    p.dma("sp", gq.ap, qg_dram.rearrange("o d -> d o"), writes=[gq])
    p.dma("sp", gk.ap, kg_dram.rearrange("o d -> d o"), writes=[gk])
    xin = [p.sb("qkin%d" % i, L, BF16) for i in range(2)]
    stg = [p.sb("qkst%d" % i, L, BF16) for i in range(2)]
    sq = [p.sb("qksq%d" % i, 512, BF16) for i in range(2)]
    xg = [p.sb("qkxg%d" % i, 512, BF16) for i in range(2)]
    lnb = [p.sb("qkln%d" % i, 512, F32) for i in range(2)]
    rs = [p.sb("qkrs%d" % i, 512, F32) for i in range(2)]
    t1 = [p.sb("qkt1%d" % i, 512, F32) for i in range(2)]
    t2 = [p.sb("qkt2%d" % i, 512, F32) for i in range(2)]
    t3 = [p.sb("qkt3%d" % i, 512, F32) for i in range(2)]
    it = 0

    def load_in(ht):
        src = s_bq[ht * 128:(ht + 1) * 128, :] if ht < 8 else s_bk[(ht - 8) * 128:(ht - 7) * 128, :]
        p.dma("sp", xin[ht % 2].ap, src, writes=[xin[ht % 2]])

    load_in(0)
    for ht in range(10):
        g = gq if ht < 8 else gk
        xi, so = xin[ht % 2], stg[ht % 2]
        if ht + 1 < 10:
            load_in(ht + 1)
        for tb in range(NB):
            i = it % 2
            it += 1
            blk = slice(tb * 512, (tb + 1) * 512)
            bA = p.bank((2 * it) % 8)
            bB = p.bank((2 * it + 1) % 8)
            ACTV(p, sq[i].ap, xi.ap[:, blk], AF.Square, [xi], [sq[i]])
            MM(p, bA.ap, c.onesm.ap, sq[i].ap, [c.onesm, sq[i]], [bA])
            TS(p, "dve", xg[i].ap, xi.ap[:, blk], g.ap[:, 0:1], None, ALU.mult, None, [xi, g], [xg[i]])
            MM(p, bB.ap, c.perm.ap, xg[i].ap, [c.perm, xg[i]], [bB])
            ACTV(p, lnb[i].ap, bA.ap, AF.Ln, [bA], [lnb[i]], bias=EPS)
            ACTV(p, rs[i].ap, lnb[i].ap, AF.Exp, [lnb[i]], [rs[i]], scale=-0.5)
            TT(p, "dve", t1[i].ap, xg[i].ap, Ct.ap[:, blk], ALU.mult, [xg[i], Ct], [t1[i]])
            TT(p, "dve", t2[i].ap, bB.ap, St.ap[:, blk], ALU.mult, [bB, St], [t2[i]])
            TT(p, "pool", t3[i].ap, t1[i].ap, t2[i].ap, ALU.add, [t1[i], t2[i]], [t3[i]])
            TT(p, "pool", so.ap[:, blk], t3[i].ap, rs[i].ap, ALU.mult, [t3[i], rs[i]], [(so.key, tb)])
        p.dma("sp", s_qk[ht * 128:(ht + 1) * 128, :], so.ap, reads=[(so.key, i) for i in range(NB)])
    p.release(m)


def stage_attn(p, c, s_qk, s_bv, s_bgate, s_y):
    m = p.mark()
    KTs = [p.sb("KT%d" % i, L, BF16) for i in range(2)]
    Vs = [p.sb("Vtm%d" % i, L, BF16) for i in range(2)]
    QT = [p.sb("QT%d" % i, L, BF16) for i in range(2)]
    GT = [p.sb("GT%d" % i, L, BF16) for i in range(2)]
    ST = [p.sb("yst%d" % i, L, BF16) for i in range(2)]
    PT = [p.sb("PT%d" % i, 1024, BF16) for i in range(4)]
    acc = [p.sb("acc%d" % i, 512, F32) for i in range(2)]
    atmp = [p.sb("atmp%d" % i, 512, F32) for i in range(2)]
    ocp = [p.sb("ocp%d" % i, 512, F32) for i in range(2)]
    scp = [p.sb("scp%d" % i, 512, F32) for i in range(2)]
    rinv = [p.sb("rinv%d" % i, 512, F32) for i in range(2)]
    yf = [p.sb("yf%d" % i, 512, F32) for i in range(2)]
    scale = 128 ** -0.5
    pc = 0
    qbc = 0
    NP = NT // 2

    def load_kv(j):
        p.dma("sp", KTs[j].ap, s_qk[(8 + j) * 128:(9 + j) * 128, :], writes=[KTs[j]])
        p.dma("sp", Vs[j].ap.rearrange("p (t d) -> p t d", d=128),
              s_bv[:, j * 128:(j + 1) * 128].rearrange("(t p) d -> p t d", p=128), writes=[Vs[j]])

    def load_q(h):
        p.dma("sp", QT[h % 2].ap, s_qk[h * 128:(h + 1) * 128, :], writes=[QT[h % 2]])
        p.dma("sp", GT[h % 2].ap, s_bgate[h * 128:(h + 1) * 128, :], writes=[GT[h % 2]])

    load_kv(0)
    load_q(0)
    load_kv(1)
    pending = [None]
    for j in range(2):
        KT, V = KTs[j], Vs[j]
        V3 = V.ap.rearrange("p (t d) -> p t d", d=128)
        for r in range(4):
            h = 4 * j + r
            q, gt, so = QT[h % 2], GT[h % 2], ST[h % 2]
            if pending[0] is not None:
                pending[0]()
                pending[0] = None
            if h + 1 < 8:
                load_q(h + 1)
            for qb in range(NB):
                blk = slice(qb * 512, (qb + 1) * 512)
                bO = p.bank(6)
                bS = p.bank(7)
                oc, sc = ocp[qbc % 2], scp[qbc % 2]
                ac = acc[qbc % 2]
                qbc += 1
                sb = {}

                def issue_s(pr):
                    b0 = 2 * ((pc + pr) % 3)
                    k0, k1 = ("psum", b0), ("psum", b0 + 1)
                    for u in range(2):
                        kt = 2 * pr + u
                        MM(p, p.psum[:, (b0 + u) * 512:(b0 + u + 1) * 512], KT.ap[:, kt * 128:(kt + 1) * 128], q.ap[:, blk],
                           [KT, q], [k0, k1])
                    sb[pr] = (p.psum[:, b0 * 512:(b0 + 2) * 512], [k0, k1])

                issue_s(0)
                issue_s(1)
                if pending[0] is not None:
                    pending[0]()
                    pending[0] = None
                for pr in range(NP):
                    if pr + 2 < NP:
                        issue_s(pr + 2)
                    pt = PT[(pc + pr) % 4]
                    sap, sk = sb[pr]
                    ACTV(p, pt.ap, sap, AF.Exp, sk, [pt], scale=scale)
                    for u in range(2):
                        kt = 2 * pr + u
                        MM(p, bO.ap, V3[:, kt, :], pt.ap[:, u * 512:(u + 1) * 512], [V, pt], [bO], start=(kt == 0),
                           stop=(kt == NT - 1))
                        if pr % 2 == 1:
                            MM(p, bS.ap, c.ones.ap, pt.ap[:, u * 512:(u + 1) * 512], [c.ones, pt], [bS], start=(kt == 2),
                               stop=False)
                    if pr % 2 == 0:
                        if pr == 0:
                            TT(p, "dve", ac.ap, pt.ap[:, 0:512], pt.ap[:, 512:1024], ALU.add, [pt], [ac])
                        else:
                            tm_ = atmp[(pr // 2) % 2]
                            TT(p, "dve", tm_.ap, pt.ap[:, 0:512], pt.ap[:, 512:1024], ALU.add, [pt], [tm_])
                            TT(p, "dve", ac.ap, ac.ap, tm_.ap, ALU.add, [ac, tm_], [ac])
                pc += NP
                CP(p, "dve", oc.ap, bO.ap, [bO], [oc])

                def epilogue(bS=bS, oc=oc, sc=sc, ac=ac, qb=qb, so=so, gt=gt, blk=blk, h=h):
                    MM(p, bS.ap, c.onesf.ap, ac.ap, [c.onesf, ac], [bS], start=False, stop=True)
                    CP(p, "dve", sc.ap, bS.ap, [bS], [sc])
                    ri, y = rinv[qb % 2], yf[qb % 2]
                    p.op("dve", lambda e, ri=ri, sc=sc: e.reciprocal(out=ri.ap, in_=sc.ap), [sc], [ri])
                    TT(p, "dve", y.ap, oc.ap, ri.ap, ALU.mult, [oc, ri], [y])
                    TT(p, "dve", so.ap[:, blk], y.ap, gt.ap[:, blk], ALU.mult, [y, gt], [(so.key, qb)])
                    if qb == NB - 1:
                        p.dma("sp", s_y[1024 + h * 128:1024 + (h + 1) * 128, :], so.ap, reads=[(so.key, i) for i in range(NB)])

                pending[0] = epilogue
    pending[0]()
    p.release(m)


def stage_gdn_scalars(p, c, s_alog, alog_dram, dtb_dram, s_rows, s_eg, s_gl, s_ts, s_rows16=None):
    m = p.mark()
    h16 = p.sb("gsh16", L, BF16, parts=8)
    l16 = p.sb("gsl16", L, BF16, parts=8)
    buf = [p.sb("gs%d" % i, L, F32, parts=8) for i in range(5)]
    A_ = p.sb("gsA", 1, F32, parts=8)
    negA = p.sb("gsnA", 1, F32, parts=8)
    dtb = p.sb("gsdtb", 1, F32, parts=8)
    glc = p.sb("gsglc", 32, F32, parts=8)
    tsb = p.sb("gstsb", 1024, F32)
    for d in range(2):
        b0, b1, b2, b3, b4 = buf
        p.dma("sp", A_.ap, alog_dram[d:d + 1, :].rearrange("o h -> h o"), writes=[A_])
        p.dma("sp", dtb.ap, dtb_dram[d:d + 1, :].rearrange("o h -> h o"), writes=[dtb])
        p.dma("sp", b0.ap, s_alog[d * 8:(d + 1) * 8, :], writes=[b0])
        p.dma("sp", b1.ap, s_alog[16 + d * 8:16 + (d + 1) * 8, :], writes=[b1])
        ACTV(p, negA.ap, A_.ap, AF.Exp, [A_], [negA])
        TS(p, "dve", negA.ap, negA.ap, -1.0, None, ALU.mult, None, [negA], [negA])
        TS(p, "dve", b0.ap, b0.ap, dtb.ap[:, 0:1], None, ALU.add, None, [b0, dtb], [b0])
        ACTV(p, b2.ap, b0.ap, AF.Abs, [b0], [b2])
        ACTV(p, b2.ap, b2.ap, AF.Exp, [b2], [b2], scale=-1.0)
        ACTV(p, b2.ap, b2.ap, AF.Ln, [b2], [b2], bias=1.0)
        TS(p, "dve", b0.ap, b0.ap, 0.0, None, ALU.max, None, [b0], [b0])
        TT(p, "dve", b0.ap, b0.ap, b2.ap, ALU.add, [b0, b2], [b0])
        TS(p, "dve", b0.ap, b0.ap, negA.ap[:, 0:1], None, ALU.mult, None, [b0, negA], [b0])
        src, dst = b0, b2
        s = 1
        while s < 128:
            s3 = src.ap.rearrange("p (n c) -> p n c", c=128)
            d3 = dst.ap.rearrange("p (n c) -> p n c", c=128)
            if d == 0:
                CP(p, "pool", d3[:, :, 0:s], s3[:, :, 0:s], [src], [dst])
                TT(p, "dve", d3[:, :, s:128], s3[:, :, s:128], s3[:, :, 0:128 - s], ALU.add, [src], [dst])
            else:
                CP(p, "pool", d3[:, :, 128 - s:128], s3[:, :, 128 - s:128], [src], [dst])
                TT(p, "dve", d3[:, :, 0:128 - s], s3[:, :, 0:128 - s], s3[:, :, s:128], ALU.add, [src], [dst])
            src, dst = dst, src
            s *= 2
        G = src
        assert G is b2
        G3 = G.ap.rearrange("p (n c) -> p n c", c=128)
        ACTV(p, b1.ap, b1.ap, AF.Sigmoid, [b1], [b1])
        ACTV(p, b3.ap, b1.ap, AF.Ln, [b1], [b3])
        TS(p, "dve", b3.ap, b3.ap, -80.0, None, ALU.max, None, [b3], [b3])
        TS(p, "dve", b0.ap, G.ap, -1.0, None, ALU.mult, None, [G], [b0])
        p.dma("sp", s_rows[d, 0], G.ap, reads=[G])
        p.dma("sp", s_rows[d, 1], b0.ap, reads=[b0])
        TT(p, "dve", b3.ap, b0.ap, b3.ap, ALU.add, [b0, b3], [b3])
        p.dma("sp", s_rows[d, 2], b3.ap, reads=[b3])
        for ri, src_ in enumerate((G, b0, b3)):
            CP(p, "dve", h16.ap, src_.ap, [src_], [h16])
            CP(p, "dve", b4.ap, h16.ap, [h16], [b4])
            TT(p, "dve", b4.ap, src_.ap, b4.ap, ALU.subtract, [src_, b4], [b4])
            CP(p, "dve", l16.ap, b4.ap, [b4], [l16])
            p.dma("sp", s_rows16[d, ri, 0], h16.ap, reads=[h16])
            p.dma("sp", s_rows16[d, ri, 1], l16.ap, reads=[l16])
        ACTV(p, b4.ap, G.ap, AF.Exp, [G], [b4])
        p.dma("sp", s_eg[d], b4.ap, reads=[b4])
        gi = 127 if d == 0 else 0
        CP(p, "dve", glc.ap, G3[:, :, gi:gi + 1].rearrange("p n o -> p (n o)"), [G], [glc])
        b33 = b3.ap.rearrange("p (n c) -> p n c", c=128)
        TT(p, "dve", b33, G3[:, :, gi:gi + 1].broadcast_to([8, 32, 128]), G3, ALU.subtract, [G], [b3])
        ACTV(p, b3.ap, b3.ap, AF.Exp, [b3], [b3])
        ACTV(p, glc.ap, glc.ap, AF.Exp, [glc], [glc])
        p.dma("sp", s_gl[d], glc.ap, reads=[glc])
        TS(p, "dve", b0.ap, b1.ap, -1.0, None, ALU.mult, None, [b1], [b0])
        quants = [b1, b0, b4, b3]
        ts4 = tsb.ap.rearrange("p (t q h) -> p t q h", t=32, q=4)
        for t in range(NT):
            bk = p.bank(t % 8)
            for qi, qb in enumerate(quants):
                TR(p, bk.ap[:, qi * 8:(qi + 1) * 8], qb.ap[:, t * 128:(t + 1) * 128], c.identf.ap[0:8, 0:8],
                   [qb, c.identf], [bk])
            CP(p, "act", tsb.ap[:, t * 32:(t + 1) * 32], bk.ap[:, 0:32], [bk], [tsb])
        p.dma("sp", s_ts[d], tsb.ap, reads=[tsb])
    p.release(m)


def stage_gdn_head(p, c, h, s_aqkv, s_agate, convw, onorm, s_rows, s_eg, s_gl, s_ts, s_y, dbg=None):
    import math
    m = p.mark()
    qT = p.sb("qT", L, BF16)
    kT = p.sb("kT", L, BF16)
    ktm = p.sb("ktm", L, BF16)
    vtm = p.sb("vtm", L, BF16)
    ktm3 = ktm.ap.rearrange("p (t d) -> p t d", d=128)
    vtm3 = vtm.ap.rearrange("p (t d) -> p t d", d=128)
    m2 = p.mark()
    xp = [p.sb("xp%d" % i, L + 4, BF16) for i in range(2)]
    acc = [p.sb("acc%d" % i, L, F32) for i in range(2)]
    xc = [p.sb("xc%d" % i, L, F32) for i in range(2)]
    vT = p.sb("vT", L, BF16)
    sq = [p.sb("sq%d" % i, 512, BF16) for i in range(2)]
    lnb = [p.sb("lnb%d" % i, 512, F32) for i in range(2)]
    rsb = [p.sb("rsb%d" % i, 512, F32) for i in range(2)]
    cw3 = convw.ap.rearrange("p (b j) -> p b j", j=5)
    bc = 0
    for idx in range(3):
        blk = idx * 8 + h
        x_, a_, xc_ = xp[idx % 2], acc[idx % 2], xc[idx % 2]
        eng = "dve"
        MSET(p, "pool", x_.ap[:, 0:2], 0.0, [x_])
        MSET(p, "pool", x_.ap[:, L + 2:L + 4], 0.0, [x_])
        p.dma("sp", x_.ap[:, 2:L + 2], s_aqkv[blk * 128:(blk + 1) * 128, :], writes=[x_])
        TS(p, eng, a_.ap, x_.ap[:, 0:L], cw3[:, blk, 0:1], None, ALU.mult, None, [x_, convw], [a_])
        for j in range(1, 5):
            STT(p, eng, a_.ap, x_.ap[:, j:j + L], cw3[:, blk, j:j + 1], a_.ap, ALU.mult, ALU.add, [x_, convw, a_], [a_])
        if idx == 2:
            ACTV(p, vT.ap, a_.ap, AF.Silu, [a_], [vT])
            continue
        ACTV(p, xc_.ap, a_.ap, AF.Silu, [a_], [xc_])
        dstT = qT if idx == 0 else kT
        for tb in range(NB):
            i = bc % 2
            bk = p.bank(bc % 8)
            bc += 1
            sl = slice(tb * 512, (tb + 1) * 512)
            ACTV(p, sq[i].ap, xc_.ap[:, sl], AF.Square, [xc_], [sq[i]])
            MM(p, bk.ap, c.ones.ap, sq[i].ap, [c.ones, sq[i]], [bk])
            ACTV(p, lnb[i].ap, bk.ap, AF.Ln, [bk], [lnb[i]], bias=EPS)
            if idx == 0:
                ACTV(p, rsb[i].ap, lnb[i].ap, AF.Exp, [lnb[i]], [rsb[i]], scale=-0.5, bias=math.log(128 ** -0.5))
            else:
                ACTV(p, rsb[i].ap, lnb[i].ap, AF.Exp, [lnb[i]], [rsb[i]], scale=-0.5)
            TT(p, "dve", dstT.ap[:, sl], xc_.ap[:, sl], rsb[i].ap, ALU.mult, [xc_, rsb[i]], [(dstT.key, tb)])
    alt = Alt(["act", "dve"])
    for src, dst3, dst in ((kT, ktm3, ktm), (vT, vtm3, vtm)):
        for g8 in range(4):
            bk = p.bank(bc % 8, BF16)
            bc += 1
            for j in range(8):
                t = g8 * 8 + j
                rk = [(src.key, t // 4)] if src is kT else [src]
                TR(p, bk.ap[:, j * 128:(j + 1) * 128], src.ap[:, t * 128:(t + 1) * 128], c.ident.ap, rk + [c.ident], [bk])
            CP(p, alt(), dst3[:, g8 * 8:(g8 + 1) * 8, :], bk.ap.rearrange("p (j d) -> p j d", d=128), [bk], [dst])
    p.release(m2)
    qdec = [p.sb("qdec%d" % d, L, BF16) for d in range(2)]
    wT = [p.sb("wT%d" % d, L, BF16) for d in range(2)]
    aT = [p.sb("aT%d" % d, L, BF16) for d in range(2)]
    ub = [p.sb("ub%d" % d, L, BF16) for d in range(2)]
    kdc = [[p.sb("kdc%d_%d" % (d, i), 128, BF16) for i in range(2)] for d in range(2)]
    of = p.sb("of", L, F32)
    tsd = [p.sb("tsd%d" % d, 1024, F32) for d in range(2)]
    glB = [p.sb("glB%d" % d, 32, F32) for d in range(2)]
    sg = p.sb("sg", L, BF16)
    yst = p.sb("ystA", L, BF16)
    RA = [p.sb("RA%d" % i, 512, F32, parts=2) for i in range(2)]
    RB = [p.sb("RB%d" % i, 512, F32, parts=2) for i in range(2)]
    RB2 = [p.sb("RB2%d" % i, 512, F32, parts=2) for i in range(2)]
    egb = [p.sb("egb%d" % i, 512, F32) for i in range(2)]
    G4 = 4
    t12 = [p.sb("t12_%d" % i, 256, F32) for i in range(G4)]
    e12 = [p.sb("e12_%d" % i, 256, F32) for i in range(G4)]
    p0t = [p.sb("p0t_%d" % i, 128, F32) for i in range(G4)]
    pq = [[p.sb("pq_%d_%d" % (i, k), 256, BF16) for k in range(2)] for i in range(G4)]
    xy = [[p.sb("xy_%d_%d" % (i, k), 256, BF16) for k in range(2)] for i in range(G4)]
    qpf = [p.sb("qpf_%d" % i, 256, BF16) for i in range(G4)]
    qpo = [p.sb("qpo_%d" % i, 256, BF16) for i in range(G4)]
    wv = [p.sb("wv_%d" % i, 256, BF16) for i in range(G4)]
    xs = [p.sb("xs_%d" % i, 128, BF16) for i in range(G4)]
    identb = c.ident
    for i in range(2):
        MSET(p, "pool", RA[i].ap, 1.0, [RA[i]])
        MSET(p, "pool", RB[i].ap, 1.0, [RB[i]])
        MSET(p, "pool", RB2[i].ap, 1.0, [RB2[i]])
    p.dma("sp", sg.ap, s_agate[h * 128:(h + 1) * 128, :], writes=[sg])
    masks = c.masks
    of3 = of.ap.rearrange("p (t d) -> p t d", d=128)
    pcs = 0
    for d in range(2):
        p.dma("sp", tsd[d].ap, s_ts[d], writes=[tsd[d]])
        p.dma("sp", glB[d].ap, s_gl[d, h:h + 1, :].partition_broadcast(128), writes=[glB[d]])
        ts4 = tsd[d].ap.rearrange("p (t q h) -> p t q h", t=32, q=4)
        aT3 = aT[d].ap.rearrange("p (t i) -> p t i", i=128)
        ub3 = ub[d].ap.rearrange("p (t i) -> p t i", i=128)
        for pc in range(8):
            ra, rb, rb2, eg = RA[pcs % 2], RB[pcs % 2], RB2[pcs % 2], egb[pcs % 2]
            pcs += 1
            psl = slice(pc * 512, (pc + 1) * 512)
            p.dma("sp", ra.ap[0:1, :], s_rows[d, 0, h:h + 1, psl], writes=[ra])
            p.dma("sp", rb.ap[1:2, :], s_rows[d, 1, h:h + 1, psl], writes=[rb])
            p.dma("sp", rb2.ap[1:2, :], s_rows[d, 2, h:h + 1, psl], writes=[rb2])
            p.dma("sp", eg.ap, s_eg[d, h:h + 1, psl].partition_broadcast(128), writes=[eg])
            TT(p, "pool", qdec[d].ap[:, psl], qT.ap[:, psl], eg.ap, ALU.mult, [(qT.key, pc), eg],
               [(qdec[d].key, pc)])
            for g0 in range(0, 4, G4):
                chunks = [pc * 4 + g0 + i for i in range(G4)]
                st = {}
                bks = {}

                def nb(sl):
                    st[sl] = st.get(sl, 0) + 1
                    return p.bank(2 * sl + st[sl] % 2)

                for sl, n in enumerate(chunks):
                    cs = slice(n * 128, (n + 1) * 128)
                    ls = slice((n % 4) * 128, (n % 4 + 1) * 128)
                    bk = nb(sl)
                    bks[sl] = bk
                    kr = [(kT.key, n // 4)]
                    MM(p, bk.ap[:, 0:128], kT.ap[:, cs], kT.ap[:, cs], kr, [bk])
                    MM(p, bk.ap[:, 128:256], kT.ap[:, cs], qT.ap[:, cs], kr + [(qT.key, n // 4)], [bk])
                    MM(p, bk.ap[:, 256:384], rb.ap[0:2, ls], ra.ap[0:2, ls], [ra, rb], [bk])
                    MM(p, bk.ap[:, 384:512], ra.ap[0:2, ls], rb2.ap[0:2, ls], [ra, rb2], [bk])
                for sl, n in enumerate(chunks):
                    bk = bks[sl]
                    TT(p, "dve", t12[sl].ap, bk.ap[:, 256:512], masks[d].ap, ALU.add, [bk, masks[d]], [t12[sl]])
                    ACTV(p, e12[sl].ap, t12[sl].ap, AF.Exp, [t12[sl]], [e12[sl]])
                for sl, n in enumerate(chunks):
                    bk = bks[sl]
                    TT(p, "dve", aT3[:, n, :], bk.ap[:, 128:256], e12[sl].ap[:, 0:128], ALU.mult, [bk, e12[sl]],
                       [(aT[d].key, n)])
                    STT(p, "dve", p0t[sl].ap, bk.ap[:, 0:128], ts4[:, n, 1, h:h + 1], e12[sl].ap[:, 0:128], ALU.mult, ALU.mult,
                        [bk, tsd[d], e12[sl]], [p0t[sl]])
                    STT(p, "dve", qpf[sl].ap[:, 0:128], bk.ap[:, 0:128], -1.0, e12[sl].ap[:, 128:256], ALU.mult, ALU.mult,
                        [bk, e12[sl]], [qpf[sl]])
                    TT(p, "pool", qpf[sl].ap[:, 128:256], p0t[sl].ap, c.offd.ap, ALU.mult, [p0t[sl], c.offd, qpf[sl]],
                       [qpf[sl]])
                for sl, n in enumerate(chunks):
                    TT(p, "pool", pq[sl][0].ap, qpf[sl].ap, c.bdm[0].ap, ALU.mult, [qpf[sl], c.bdm[0]], [pq[sl][0]])
                    TT(p, "pool", xy[sl][0].ap, pq[sl][0].ap, c.ident2.ap, ALU.add, [pq[sl][0], c.ident2], [xy[sl][0]])
                step_i = 0
                for k in range(3):
                    cur, nxt = step_i % 2, (step_i + 1) % 2
                    step_i += 1
                    for sl, n in enumerate(chunks):
                        bk = nb(sl)
                        bks[sl] = bk
                        Qk = pq[sl][k % 2].ap[:, 0:128]
                        Pk = pq[sl][k % 2].ap[:, 128:256]
                        MM(p, bk.ap[:, 0:128], Pk, Qk, [pq[sl][k % 2]], [bk])
                        MM(p, bk.ap[:, 128:256], Qk, Pk, [pq[sl][k % 2]], [bk])
                    for sl, n in enumerate(chunks):
                        CP(p, "act", pq[sl][(k + 1) % 2].ap, bks[sl].ap[:, 0:256], [bks[sl]], [pq[sl][(k + 1) % 2]])
                    for sl, n in enumerate(chunks):
                        bk = bks[sl]
                        Qn = pq[sl][(k + 1) % 2].ap[:, 0:128]
                        Pn = pq[sl][(k + 1) % 2].ap[:, 128:256]
                        MM(p, bk.ap[:, 256:384], Pn, xy[sl][cur].ap[:, 0:128], [pq[sl][(k + 1) % 2], xy[sl][cur]], [bk])
                        MM(p, bk.ap[:, 384:512], Qn, xy[sl][cur].ap[:, 128:256], [pq[sl][(k + 1) % 2], xy[sl][cur]], [bk])
                    for sl, n in enumerate(chunks):
                        TT(p, "dve", xy[sl][nxt].ap, bks[sl].ap[:, 256:512], xy[sl][cur].ap, ALU.add, [bks[sl], xy[sl][cur]],
                           [xy[sl][nxt]])
                for lv in range(3):
                    cur, nxt = step_i % 2, (step_i + 1) % 2
                    step_i += 1
                    last = lv == 2
                    for sl, n in enumerate(chunks):
                        TT(p, "pool", qpo[sl].ap, qpf[sl].ap, c.bdm[lv + 1].ap, ALU.mult, [qpf[sl], c.bdm[lv + 1]], [qpo[sl]])
                    for sl, n in enumerate(chunks):
                        bk = nb(sl)
                        bks[sl] = bk
                        X_ = xy[sl][cur].ap[:, 0:128]
                        Y_ = xy[sl][cur].ap[:, 128:256]
                        if not last:
                            MM(p, bk.ap[:, 0:128], qpo[sl].ap[:, 128:256], X_, [qpo[sl], xy[sl][cur]], [bk])
                        MM(p, bk.ap[:, 128:256], qpo[sl].ap[:, 0:128], Y_, [qpo[sl], xy[sl][cur]], [bk])
                    for sl, n in enumerate(chunks):
                        lo = 128 if last else 0
                        CP(p, "act", wv[sl].ap[:, lo:256], bks[sl].ap[:, lo:256], [bks[sl]], [wv[sl]])
                    for sl, n in enumerate(chunks):
                        bk = bks[sl]
                        X_ = xy[sl][cur].ap[:, 0:128]
                        Y_ = xy[sl][cur].ap[:, 128:256]
                        if not last:
                            MM(p, bk.ap[:, 256:384], Y_, wv[sl].ap[:, 0:128], [wv[sl], xy[sl][cur]], [bk])
                        MM(p, bk.ap[:, 384:512], X_, wv[sl].ap[:, 128:256], [wv[sl], xy[sl][cur]], [bk])
                    for sl, n in enumerate(chunks):
                        lo = 128 if last else 0
                        TT(p, "dve", xy[sl][nxt].ap[:, lo:256], bks[sl].ap[:, 256 + lo:512], xy[sl][cur].ap[:, lo:256], ALU.add,
                           [bks[sl], xy[sl][cur]], [xy[sl][nxt]])
                assert step_i % 2 == 0
                for sl, n in enumerate(chunks):
                    TS(p, "pool", xs[sl].ap, ktm3[:, n, :], ts4[:, n, 2, h:h + 1], None, ALU.mult, None, [ktm, tsd[d]], [xs[sl]])
                bks2 = {}
                for sl, n in enumerate(chunks):
                    bk = nb(sl)
                    bks[sl] = bk
                    bk2 = nb(sl)
                    bks2[sl] = bk2
                    T_ = xy[sl][0]
                    Tap = T_.ap[:, 128:256]
                    MM(p, bk.ap[:, 0:128], Tap, vtm3[:, n, :], [T_, vtm], [bk])
                    MM(p, bk2.ap[:, 128:256], xs[sl].ap, Tap, [xs[sl], T_], [bk2])
                for sl, n in enumerate(chunks):
                    bk = bks[sl]
                    bk2 = bks2[sl]
                    TS(p, "dve", ub3[:, n, :], bk.ap[:, 0:128], ts4[:, n, 0, h:h + 1], None, ALU.mult, None, [bk, tsd[d]],
                       [(ub[d].key, n)])
                    CP(p, "act", wT[d].ap[:, n * 128:(n + 1) * 128], bk2.ap[:, 128:256], [bk2], [(wT[d].key, n)])
    S32 = [p.sb("S32_%d" % d, 128, F32) for d in range(2)]
    Sbf = [p.sb("Sbf_%d" % d, 128, BF16) for d in range(2)]
    vn = [[p.sb("vn_%d_%d" % (d, i), 128, BF16) for i in range(2)] for d in range(2)]
    osum = [p.sb("osum%d" % i, 128, F32) for i in range(4)]
    ojunk = p.sb("ojunk", 128, BF16)
    oss = [p.sb("oss%d" % i, 1, F32) for i in range(4)]
    ors = [p.sb("ors%d" % i, 1, F32) for i in range(4)]
    on = [p.sb("on%d" % i, 128, BF16) for i in range(4)]
    for d in range(2):
        MSET(p, "pool", S32[d].ap, 0.0, [S32[d]])
        MSET(p, "pool", Sbf[d].ap, 0.0, [Sbf[d]])
    for step in range(NT):
        for d in range(2):
            n = step if d == 0 else NT - 1 - step
            ts4 = tsd[d].ap.rearrange("p (t q h) -> p t q h", t=32, q=4)
            aT3 = aT[d].ap.rearrange("p (t i) -> p t i", i=128)
            ub3 = ub[d].ap.rearrange("p (t i) -> p t i", i=128)
            kd_ = kdc[d][step % 2]
            TS(p, "pool", kd_.ap, ktm3[:, n, :], ts4[:, n, 3, h:h + 1], None, ALU.mult, None, [ktm, tsd[d]], [kd_])
            cs = slice(n * 128, (n + 1) * 128)
            b1, b2, b3, b4 = [p.bank(4 * d + i) for i in range(4)]
            v_ = vn[d][step % 2]
            MM(p, b1.ap[:, 0:128], wT[d].ap[:, cs], Sbf[d].ap, [(wT[d].key, n), Sbf[d]], [b1])
            STT(p, "dve", v_.ap, b1.ap[:, 0:128], ts4[:, n, 1, h:h + 1], ub3[:, n, :], ALU.mult, ALU.add,
                [b1, tsd[d], (ub[d].key, n)], [v_])
            MM(p, b2.ap[:, 0:128], qdec[d].ap[:, cs], Sbf[d].ap, [(qdec[d].key, n // 4), Sbf[d]], [b2], start=True, stop=False)
            MM(p, b2.ap[:, 0:128], aT3[:, n, :], v_.ap, [(aT[d].key, n), v_], [b2], start=False, stop=True)
            MM(p, b3.ap[:, 0:128], kd_.ap, v_.ap, [kd_, v_], [b3])
            STT(p, "dve", S32[d].ap, S32[d].ap, glB[d].ap[:, n:n + 1], b3.ap[:, 0:128], ALU.mult, ALU.add,
                [S32[d], glB[d], b3], [S32[d]])
            CP(p, "act", Sbf[d].ap, S32[d].ap, [S32[d]], [Sbf[d]])
            if step < NT // 2:
                CP(p, "act", of3[:, n, :], b2.ap[:, 0:128], [b2], [(of.key, n)])
            else:
                i = (2 * step + d) % 4
                TT(p, "dve", osum[i].ap, b2.ap[:, 0:128], of3[:, n, :], ALU.add, [b2, (of.key, n)], [osum[i]])
                ACTV(p, ojunk.ap, osum[i].ap, AF.Square, [osum[i]], [ojunk, oss[i]], accum=oss[i].ap)
                ACTV(p, ors[i].ap, oss[i].ap, AF.Ln, [oss[i]], [ors[i]], bias=EPS, scale=1.0 / 128)
                ACTV(p, ors[i].ap, ors[i].ap, AF.Exp, [ors[i]], [ors[i]], scale=-0.5)
                TS(p, "pool", on[i].ap, osum[i].ap, ors[i].ap[:, 0:1], None, ALU.mult, None, [osum[i], ors[i]], [on[i]])
                b4b = p.bank(4 * d + 3, BF16)
                TR(p, b4b.ap[:, 0:128], on[i].ap, c.ident.ap, [on[i], c.ident], [b4b])
                STT(p, "dve", yst.ap[:, cs], b4b.ap[:, 0:128], onorm.ap[:, 0:1], sg.ap[:, cs], ALU.mult, ALU.mult,
                    [b4b, onorm, sg], [(yst.key, n)])
    p.dma("sp", s_y[h * 128:(h + 1) * 128, :], yst.ap, reads=[(yst.key, n) for n in range(NT)])
    if dbg is not None:
        p.barrier()
        for i, b in enumerate([qT, kT, ktm, vtm, qdec[0], qdec[1], wT[0], wT[1], aT[0], aT[1], ub[0], ub[1]]):
            p.dma("sp", dbg["bf"][i], b.ap)
        p.dma("sp", dbg["of"], of.ap)
    p.release(m)


def stage_gmlp_mix(p, c, s_gv, s_u, s_g1, lng_dram, lnb_dram, wsT_dram, bs_dram, s_z):
    m = p.mark()
    bsB = p.sb("bsB", D, F32)
    wsT = p.sb("wsT", D, BF16)
    lngT = p.sb("lngT", 16, F32)
    lnbrow = p.sb("lnbrow", D, F32, parts=1)
    rwrow = p.sb("rwrow", D, F32, parts=1)
    p.dma("sp", bsB.ap, bs_dram.partition_broadcast(128), writes=[bsB])
    p.dma("pool", wsT.ap.rearrange("p (g t) -> p g t", g=16), wsT_dram.rearrange("g s t -> s g t"), writes=[wsT])
    for gi in range(16):
        p.dma("sp", lngT.ap[:, gi:gi + 1], lng_dram[0:1, gi * 128:(gi + 1) * 128].rearrange("o d -> d o"), writes=[lngT])
    p.dma("sp", lnbrow.ap, lnb_dram, writes=[lnbrow])
    wsT3 = wsT.ap.rearrange("p (g t) -> p g t", g=16)
    for g4 in range(4):
        bk = p.bank(g4)
        MM(p, bk.ap[0:1, :], c.ones.ap[:, 0:1], wsT.ap[:, g4 * 512:(g4 + 1) * 512], [c.ones, wsT], [bk])
        CP(p, "act", rwrow.ap[:, g4 * 512:(g4 + 1) * 512], bk.ap[0:1, :], [bk], [rwrow])
    ub = [p.sb("mu%d" % i, KC * 512, BF16) for i in range(2)]
    gb = [p.sb("mg%d" % i, KC * 512, BF16) for i in range(2)]
    zb = [p.sb("mz%d" % i, KC * 512, BF16) for i in range(2)]
    gv4 = [[p.sb("gv%d_%d" % (k, i), D, BF16) for i in range(4)] for k in range(2)]
    vl4 = [[p.sb("vl%d_%d" % (k, i), D, BF16) for i in range(4)] for k in range(2)]
    junk = p.sb("mjunk", D, BF16)
    s1 = [p.sb("ms1%d" % i, 4, F32) for i in range(2)]
    s2 = [p.sb("ms2%d" % i, 4, F32) for i in range(2)]
    mean = [p.sb("mmean%d" % i, 4, F32) for i in range(2)]
    msq = [p.sb("mmsq%d" % i, 4, F32) for i in range(2)]
    var = [p.sb("mvar%d" % i, 4, F32) for i in range(2)]
    rstd = [p.sb("mrstd%d" % i, 4, F32) for i in range(2)]
    nmr = [p.sb("mnmr%d" % i, 4, F32) for i in range(2)]
    z1 = [p.sb("mz1%d" % i, 512, F32) for i in range(2)]
    uv = s_u.rearrange("(k p) t -> p k t", p=128)
    gvw = s_g1.rearrange("(k p) t -> p k t", p=128)
    zv = s_z.rearrange("(k p) t -> p k t", p=128)

    def emit_stats(tb):
        k4 = tb % 2
        for ti in range(4):
            n = tb * 4 + ti
            gvi = gv4[k4][ti]
            p.dma("sp", gvi.ap, s_gv[n * 128:(n + 1) * 128, :], writes=[gvi])
            ACTV(p, junk.ap, gvi.ap, AF.Identity, [gvi], [junk, (s1[k4].key, ti)], accum=s1[k4].ap[:, ti:ti + 1])
            ACTV(p, junk.ap, gvi.ap, AF.Square, [gvi], [junk, (s2[k4].key, ti)], accum=s2[k4].ap[:, ti:ti + 1])
        s1k = [(s1[k4].key, ti) for ti in range(4)]
        s2k = [(s2[k4].key, ti) for ti in range(4)]
        TS(p, "dve", mean[k4].ap, s1[k4].ap, 1.0 / D, None, ALU.mult, None, s1k, [mean[k4]])
        TT(p, "dve", msq[k4].ap, mean[k4].ap, mean[k4].ap, ALU.mult, [mean[k4]], [msq[k4]])
        STT(p, "dve", var[k4].ap, s2[k4].ap, 1.0 / D, msq[k4].ap, ALU.mult, ALU.subtract, s2k + [msq[k4]], [var[k4]])
        ACTV(p, rstd[k4].ap, var[k4].ap, AF.Ln, [var[k4]], [rstd[k4]], bias=EPS)
        ACTV(p, rstd[k4].ap, rstd[k4].ap, AF.Exp, [rstd[k4]], [rstd[k4]], scale=-0.5)
        STT(p, "dve", nmr[k4].ap, mean[k4].ap, -1.0, rstd[k4].ap, ALU.mult, ALU.mult, [mean[k4], rstd[k4]], [nmr[k4]])

    def load_ug(tb):
        p.dma("sp", ub[tb % 2].ap.rearrange("p (k t) -> p k t", k=KC), uv[:, :, tb * 512:(tb + 1) * 512], writes=[ub[tb % 2]])
        p.dma("sp", gb[tb % 2].ap.rearrange("p (k t) -> p k t", k=KC), gvw[:, :, tb * 512:(tb + 1) * 512], writes=[gb[tb % 2]])

    emit_stats(0)
    load_ug(0)
    bc = 4
    zc = 0
    for tb in range(NB):
        u, g, z = ub[tb % 2], gb[tb % 2], zb[tb % 2]
        u3 = u.ap.rearrange("p (k t) -> p k t", k=KC)
        z3 = z.ap.rearrange("p (k t) -> p k t", k=KC)
        k4 = tb % 2
        if tb + 1 < NB:
            load_ug(tb + 1)
            emit_stats(tb + 1)
        TT(p, "dve", u.ap, u.ap, g.ap, ALU.mult, [u, g], [u])
        for ti in range(4):
            ACTV(p, vl4[k4][ti].ap, gv4[k4][ti].ap, AF.Identity, [gv4[k4][ti], rstd[k4], nmr[k4]], [vl4[k4][ti]],
                 scale=rstd[k4].ap[:, ti:ti + 1], bias=nmr[k4].ap[:, ti:ti + 1])
        for gi in range(16):
            bk = p.bank(bc % 8)
            bc += 1
            gs = slice(gi * 128, (gi + 1) * 128)
            for ti in range(4):
                MM(p, bk.ap[:, ti * 128:(ti + 1) * 128], vl4[k4][ti].ap[:, gs], wsT3[:, gi, :], [vl4[k4][ti], wsT], [bk],
                   start=(ti == 0), stop=True)
            zz = z1[zc % 2]
            zc += 1
            STT(p, "dve", zz.ap.rearrange("p (i t) -> p i t", i=4), bk.ap.rearrange("p (i t) -> p i t", i=4),
                lngT.ap[:, gi:gi + 1], bsB.ap[:, gs].rearrange("p (o t) -> p o t", o=1).broadcast_to([128, 4, 128]),
                ALU.mult, ALU.add, [bk, lngT, bsB], [zz])
            bk2 = p.bank(bc % 8)
            bc += 1
            MM(p, bk2.ap, lnbrow.ap[0:1, gs], rwrow.ap[0:1, gs].rearrange("p (o t) -> p o t", o=1).broadcast_to([1, 4, 128]),
               [lnbrow, rwrow], [bk2])
            TT(p, "dve", zz.ap, zz.ap, bk2.ap, ALU.add, [zz, bk2], [zz])
            TT(p, "dve", z3[:, gi, :], zz.ap, u3[:, gi, :], ALU.mult, [zz, u], [z])
        p.dma("sp", zv[:, :, tb * 512:(tb + 1) * 512], z3, reads=[z])
    p.release(m)


def stage_gdn_head2(p, c, h, s_aqkv, s_agate, convw, onorm, s_rows, s_eg, s_gl, s_ts, s_y, dbg=None, s_rows16=None):
    import math
    m = p.mark()
    qT = p.sb("qT", L, BF16)
    kT = p.sb("kT", L, BF16)
    ktm = p.sb("ktm", L, BF16)
    vtm = p.sb("vtm", L, BF16)
    ktm3 = ktm.ap.rearrange("p (t d) -> p t d", d=128)
    vtm3 = vtm.ap.rearrange("p (t d) -> p t d", d=128)
    m2 = p.mark()
    xp = [p.sb("xp%d" % i, L + 4, BF16) for i in range(3)]
    xc = [p.sb("xc%d" % i, L, F32) for i in range(2)]
    vT = p.sb("vT", L, BF16)
    dg = [[p.sb("dg%d_%d" % (i, j), 128, BF16) for j in range(5)] for i in range(3)]
    sq = [p.sb("sq%d" % i, 512, BF16) for i in range(2)]
    lnb = [p.sb("lnb%d" % i, 512, F32) for i in range(2)]
    rsb = [p.sb("rsb%d" % i, 512, F32) for i in range(2)]
    cw3 = convw.ap.rearrange("p (b j) -> p b j", j=5)
    bc = 0
    for idx in range(3):
        blk = idx * 8 + h
        x_ = xp[idx]
        MSET(p, "pool", x_.ap[:, 0:2], 0.0, [x_])
        MSET(p, "pool", x_.ap[:, L + 2:L + 4], 0.0, [x_])
        p.dma("sp", x_.ap[:, 2:L + 2], s_aqkv[blk * 128:(blk + 1) * 128, :], writes=[x_])
        for j in range(5):
            TS(p, "dve", dg[idx][j].ap, c.ident.ap, cw3[:, blk, j:j + 1], None, ALU.mult, None, [c.ident, convw], [dg[idx][j]])
    for idx in range(3):
        x_ = xp[idx]
        xc_ = xc[idx % 2]
        for tb in range(NB):
            bk = p.bank(bc % 8)
            bc += 1
            sl = slice(tb * 512, (tb + 1) * 512)
            for j in range(5):
                MM(p, bk.ap, dg[idx][j].ap, x_.ap[:, tb * 512 + j:tb * 512 + j + 512], [dg[idx][j], x_], [bk],
                   start=(j == 0), stop=(j == 4))
            if idx == 2:
                ACTV(p, vT.ap[:, sl], bk.ap, AF.Silu, [bk], [(vT.key, tb)])
            else:
                ACTV(p, xc_.ap[:, sl], bk.ap, AF.Silu, [bk], [(xc_.key, tb)])
        if idx == 2:
            continue
        dstT = qT if idx == 0 else kT
        for tb in range(NB):
            i = bc % 2
            bk = p.bank(bc % 8)
            bc += 1
            sl = slice(tb * 512, (tb + 1) * 512)
            ACTV(p, sq[i].ap, xc_.ap[:, sl], AF.Square, [(xc_.key, tb)], [sq[i]])
            MM(p, bk.ap, c.ones.ap, sq[i].ap, [c.ones, sq[i]], [bk])
            ACTV(p, lnb[i].ap, bk.ap, AF.Ln, [bk], [lnb[i]], bias=EPS)
            if idx == 0:
                ACTV(p, rsb[i].ap, lnb[i].ap, AF.Exp, [lnb[i]], [rsb[i]], scale=-0.5, bias=math.log(128 ** -0.5))
            else:
                ACTV(p, rsb[i].ap, lnb[i].ap, AF.Exp, [lnb[i]], [rsb[i]], scale=-0.5)
            TT(p, "dve", dstT.ap[:, sl], xc_.ap[:, sl], rsb[i].ap, ALU.mult, [(xc_.key, tb), rsb[i]], [(dstT.key, tb)])
    alt = Alt(["act", "dve"])
    for src, dst3, dst in ((kT, ktm3, ktm), (vT, vtm3, vtm)):
        for g8 in range(4):
            bk = p.bank(bc % 8, BF16)
            bc += 1
            for j in range(8):
                t = g8 * 8 + j
                TR(p, bk.ap[:, j * 128:(j + 1) * 128], src.ap[:, t * 128:(t + 1) * 128], c.ident.ap,
                   [(src.key, t // 4), c.ident], [bk])
            CP(p, alt(), dst3[:, g8 * 8:(g8 + 1) * 8, :], bk.ap.rearrange("p (j d) -> p j d", d=128), [bk], [dst])
    p.release(m2)
    qdec = [p.sb("qdec%d" % d, L, BF16) for d in range(2)]
    wT = [p.sb("wT%d" % d, L, BF16) for d in range(2)]
    aT = [p.sb("aT%d" % d, L, BF16) for d in range(2)]
    ub = [p.sb("ub%d" % d, L, BF16) for d in range(2)]
    of = p.sb("of", L, F32)
    tsd = [p.sb("tsd%d" % d, 1024, F32) for d in range(2)]
    glB = [p.sb("glB%d" % d, 32, F32) for d in range(2)]
    NG = 2
    m3 = p.mark()
    RA = [[p.sb("RA%d_%d" % (g, i), 512, BF16, parts=4) for i in range(2)] for g in range(NG)]
    RB = [[p.sb("RB%d_%d" % (g, i), 512, BF16, parts=4) for i in range(2)] for g in range(NG)]
    RB2 = [[p.sb("RB2%d_%d" % (g, i), 512, BF16, parts=4) for i in range(2)] for g in range(NG)]
    egb = [[p.sb("egb%d_%d" % (g, i), 512, F32) for i in range(2)] for g in range(NG)]
    eb = [[p.sb("eb%d_%d" % (g, i), 512, F32) for i in range(2)] for g in range(NG)]
    Qf = [p.sb("Qf%d" % g, 512, BF16) for g in range(NG)]
    Pf = [p.sb("Pf%d" % g, 512, BF16) for g in range(NG)]
    pqQ = [[p.sb("pqQ%d_%d" % (g, i), 512, BF16) for i in range(2)] for g in range(NG)]
    pqP = [[p.sb("pqP%d_%d" % (g, i), 512, BF16) for i in range(2)] for g in range(NG)]
    XX = [[p.sb("XX%d_%d" % (g, i), 512, BF16) for i in range(2)] for g in range(NG)]
    YY = [[p.sb("YY%d_%d" % (g, i), 512, BF16) for i in range(2)] for g in range(NG)]
    Vm = [p.sb("Vm%d" % g, 512, BF16) for g in range(NG)]
    Wm = [p.sb("Wm%d" % g, 512, BF16) for g in range(NG)]
    xs = [p.sb("xs%d" % g, 512, BF16) for g in range(NG)]
    for g in range(NG):
        for i in range(2):
            MSET(p, "pool", RA[g][i].ap, 1.0, [RA[g][i]])
            MSET(p, "pool", RB[g][i].ap, 1.0, [RB[g][i]])
            MSET(p, "pool", RB2[g][i].ap, 1.0, [RB2[g][i]])
    for d in range(2):
        p.dma("sp", tsd[d].ap, s_ts[d], writes=[tsd[d]])
        p.dma("sp", glB[d].ap, s_gl[d, h:h + 1, :].partition_broadcast(128), writes=[glB[d]])
    of3 = of.ap.rearrange("p (t d) -> p t d", d=128)
    ts4 = [tsd[d].ap.rearrange("p (t q h) -> p t q h", t=32, q=4) for d in range(2)]
    aT3 = [aT[d].ap.rearrange("p (t i) -> p t i", i=128) for d in range(2)]
    ub3 = [ub[d].ap.rearrange("p (t i) -> p t i", i=128) for d in range(2)]
    C4 = lambda ap: ap.rearrange("p (c i) -> p c i", c=4)
    cs4 = lambda c_: slice(c_ * 128, (c_ + 1) * 128)

    def mm4(bk, lhs_fn, rhs_fn, reads, first=True):
        for c_ in range(4):
            MM(p, bk.ap[:, cs4(c_)], lhs_fn(c_), rhs_fn(c_), reads, [bk], start=(first and c_ == 0), stop=True)

    for pc in range(8):
        psl = slice(pc * 512, (pc + 1) * 512)
        n0 = pc * 4
        G = []
        for d in range(2):
            g = d
            i = pc % 2
            st = dict(d=d, g=g, ra=RA[g][i], rb=RB[g][i], rb2=RB2[g][i], eg=egb[g][i], eb=eb[g],
                      bA=p.bank(4 * g), bB=p.bank(4 * g + 1), bC=p.bank(4 * g + 2), bD=p.bank(4 * g + 3))
            G.append(st)
            p.dma("sp", st["ra"].ap[0:2, :], s_rows16[d, 0, :, h, psl], writes=[st["ra"]])
            p.dma("sp", st["rb"].ap[2:4, :], s_rows16[d, 1, :, h, psl], writes=[st["rb"]])
            p.dma("sp", st["rb2"].ap[2:4, :], s_rows16[d, 2, :, h, psl], writes=[st["rb2"]])
            p.dma("sp", st["eg"].ap, s_eg[d, h:h + 1, psl].partition_broadcast(128), writes=[st["eg"]])
            TT(p, "pool", qdec[d].ap[:, psl], qT.ap[:, psl], st["eg"].ap, ALU.mult, [(qT.key, pc), st["eg"]], [(qdec[d].key, pc)])
        kr = [(kT.key, pc)]
        kc = lambda c_: kT.ap[:, (n0 + c_) * 128:(n0 + c_ + 1) * 128]
        qc = lambda c_: qT.ap[:, (n0 + c_) * 128:(n0 + c_ + 1) * 128]
        for st in G:
            d, g = st["d"], st["g"]
            ra, rb, rb2 = st["ra"], st["rb"], st["rb2"]
            mm4(st["bA"], kc, kc, kr)
            mm4(st["bB"], lambda c_: ra.ap[0:4, cs4(c_)], lambda c_: rb2.ap[0:4, cs4(c_)], [ra, rb2])
            MM(p, st["bB"].ap, c.ident.ap, c.nms4[d].ap, [c.ident, c.nms4[d]], [st["bB"]], start=False, stop=True)
            mm4(st["bC"], lambda c_: rb2.ap[0:4, cs4(c_)], lambda c_: ra.ap[0:4, cs4(c_)], [ra, rb2])
            MM(p, st["bC"].ap, c.ident.ap, c.nms4[1 - d].ap, [c.ident, c.nms4[1 - d]], [st["bC"]], start=False, stop=True)
            mm4(st["bD"], kc, qc, kr + [(qT.key, pc)])
        for st in G:
            ACTV(p, st["eb"][0].ap, st["bB"].ap, AF.Exp, [st["bB"]], [st["eb"][0]])
            ACTV(p, st["eb"][1].ap, st["bC"].ap, AF.Exp, [st["bC"]], [st["eb"][1]])
        for st in G:
            g = st["g"]
            STT(p, "dve", Qf[g].ap, st["bA"].ap, -1.0, st["eb"][0].ap, ALU.mult, ALU.mult, [st["bA"], st["eb"][0]], [Qf[g]])
            STT(p, "dve", Pf[g].ap, st["bA"].ap, -1.0, st["eb"][1].ap, ALU.mult, ALU.mult, [st["bA"], st["eb"][1]], [Pf[g]])
        for st in G:
            d = st["d"]
            ra, rb = st["ra"], st["rb"]
            mm4(st["bB"], lambda c_: rb.ap[0:4, cs4(c_)], lambda c_: ra.ap[0:4, cs4(c_)], [ra, rb])
            MM(p, st["bB"].ap, c.ident.ap, c.nmiT4[d].ap, [c.ident, c.nmiT4[d]], [st["bB"]], start=False, stop=True)
        for st in G:
            ACTV(p, st["eb"][0].ap, st["bB"].ap, AF.Exp, [st["bB"]], [st["eb"][0]])
        for st in G:
            d, g = st["d"], st["g"]
            TT(p, "dve", aT3[d][:, n0:n0 + 4, :], C4(st["bD"].ap), C4(st["eb"][0].ap), ALU.mult, [st["bD"], st["eb"][0]],
               [(aT[d].key, pc)])
            TT(p, "dve", pqQ[g][0].ap, Qf[g].ap, c.bd4[0].ap, ALU.mult, [Qf[g], c.bd4[0]], [pqQ[g][0]])
            TT(p, "dve", pqP[g][0].ap, Pf[g].ap, c.bd4[0].ap, ALU.mult, [Pf[g], c.bd4[0]], [pqP[g][0]])
            TT(p, "pool", XX[g][0].ap, pqQ[g][0].ap, c.ident4.ap, ALU.add, [pqQ[g][0], c.ident4], [XX[g][0]])
            TT(p, "pool", YY[g][0].ap, pqP[g][0].ap, c.ident4.ap, ALU.add, [pqP[g][0], c.ident4], [YY[g][0]])
        si = 0
        for k in range(3):
            cur, nxt = si % 2, (si + 1) % 2
            si += 1
            kq, kn = k % 2, (k + 1) % 2
            for st in G:
                g = st["g"]
                Q_, P_ = pqQ[g][kq], pqP[g][kq]
                mm4(st["bA"], lambda c_: P_.ap[:, cs4(c_)], lambda c_: Q_.ap[:, cs4(c_)], [P_, Q_])
                mm4(st["bB"], lambda c_: Q_.ap[:, cs4(c_)], lambda c_: P_.ap[:, cs4(c_)], [P_, Q_])
            for st in G:
                g = st["g"]
                CP(p, "dve", pqQ[g][kn].ap, st["bA"].ap, [st["bA"]], [pqQ[g][kn]])
                CP(p, "act", pqP[g][kn].ap, st["bB"].ap, [st["bB"]], [pqP[g][kn]])
            for st in G:
                g = st["g"]
                Qn, Pn = pqQ[g][kn], pqP[g][kn]
                X_, Y_ = XX[g][cur], YY[g][cur]
                mm4(st["bC"], lambda c_: Pn.ap[:, cs4(c_)], lambda c_: X_.ap[:, cs4(c_)], [Pn, X_])
                MM(p, st["bC"].ap, c.ident.ap, X_.ap, [c.ident, X_], [st["bC"]], start=False, stop=True)
                mm4(st["bD"], lambda c_: Qn.ap[:, cs4(c_)], lambda c_: Y_.ap[:, cs4(c_)], [Qn, Y_])
                MM(p, st["bD"].ap, c.ident.ap, Y_.ap, [c.ident, Y_], [st["bD"]], start=False, stop=True)
            for st in G:
                g = st["g"]
                CP(p, "act", XX[g][nxt].ap, st["bC"].ap, [st["bC"]], [XX[g][nxt]])
                CP(p, "dve", YY[g][nxt].ap, st["bD"].ap, [st["bD"]], [YY[g][nxt]])
        for lv in range(3):
            cur, nxt = si % 2, (si + 1) % 2
            si += 1
            last = lv == 2
            for st in G:
                g = st["g"]
                X_, Y_ = XX[g][cur], YY[g][cur]
                if not last:
                    mm4(st["bA"], lambda c_: Pf[g].ap[:, cs4(c_)], lambda c_: X_.ap[:, cs4(c_)], [Pf[g], X_])
                mm4(st["bB"], lambda c_: Qf[g].ap[:, cs4(c_)], lambda c_: Y_.ap[:, cs4(c_)], [Qf[g], Y_])
            for st in G:
                g = st["g"]
                if not last:
                    TT(p, "dve", Vm[g].ap, st["bA"].ap, c.bd4[lv + 1].ap, ALU.mult, [st["bA"], c.bd4[lv + 1]], [Vm[g]])
                TT(p, "dve", Wm[g].ap, st["bB"].ap, c.bd4[lv + 1].ap, ALU.mult, [st["bB"], c.bd4[lv + 1]], [Wm[g]])
            for st in G:
                g = st["g"]
                X_, Y_ = XX[g][cur], YY[g][cur]
                if not last:
                    mm4(st["bC"], lambda c_: Y_.ap[:, cs4(c_)], lambda c_: Vm[g].ap[:, cs4(c_)], [Y_, Vm[g]])
                    MM(p, st["bC"].ap, c.ident.ap, X_.ap, [c.ident, X_], [st["bC"]], start=False, stop=True)
                mm4(st["bD"], lambda c_: X_.ap[:, cs4(c_)], lambda c_: Wm[g].ap[:, cs4(c_)], [X_, Wm[g]])
                MM(p, st["bD"].ap, c.ident.ap, Y_.ap, [c.ident, Y_], [st["bD"]], start=False, stop=True)
            for st in G:
                g = st["g"]
                if not last:
                    CP(p, "act", XX[g][nxt].ap, st["bC"].ap, [st["bC"]], [XX[g][nxt]])
                CP(p, "act", YY[g][nxt].ap, st["bD"].ap, [st["bD"]], [YY[g][nxt]])
        assert si % 2 == 0
        for st in G:
            d, g = st["d"], st["g"]
            TT(p, "pool", C4(xs[g].ap), ktm3[:, n0:n0 + 4, :], ts4[d][:, n0:n0 + 4, 2, h:h + 1].broadcast_to([128, 4, 128]),
               ALU.mult, [ktm, tsd[d]], [xs[g]])
        for st in G:
            d, g = st["d"], st["g"]
            Y_ = YY[g][0]
            mm4(st["bA"], lambda c_: Y_.ap[:, cs4(c_)], lambda c_: vtm3[:, n0 + c_, :], [Y_, vtm])
            mm4(st["bB"], lambda c_: xs[g].ap[:, cs4(c_)], lambda c_: Y_.ap[:, cs4(c_)], [xs[g], Y_])
        for st in G:
            d, g = st["d"], st["g"]
            TT(p, "dve", ub3[d][:, n0:n0 + 4, :], C4(st["bA"].ap), ts4[d][:, n0:n0 + 4, 0, h:h + 1].broadcast_to([128, 4, 128]),
               ALU.mult, [st["bA"], tsd[d]], [(ub[d].key, pc)])
            CP(p, "act", wT[d].ap[:, psl], st["bB"].ap, [st["bB"]], [(wT[d].key, pc)])
    p.release(m3)
    S32 = [p.sb("S32_%d" % d, 128, F32) for d in range(2)]
    Sbf = [p.sb("Sbf_%d" % d, 128, BF16) for d in range(2)]
    vn = [[p.sb("vn_%d_%d" % (d, i), 128, BF16) for i in range(2)] for d in range(2)]
    kdc = [[p.sb("kdc%d_%d" % (d, i), 128, BF16) for i in range(4)] for d in range(2)]
    for d in range(2):
        MSET(p, "pool", S32[d].ap, 0.0, [S32[d]])
        MSET(p, "pool", Sbf[d].ap, 0.0, [Sbf[d]])
    def emit_kd(step):
        for d in range(2):
            n = step if d == 0 else NT - 1 - step
            kd_ = kdc[d][step % 4]
            ACTV(p, kd_.ap, ktm3[:, n, :], AF.Copy, [ktm, tsd[d]], [kd_], scale=ts4[d][:, n, 3, h:h + 1])

    emit_kd(0)
    emit_kd(1)
    for step in range(NT):
        if step + 2 < NT:
            emit_kd(step + 2)
        ctx = []
        for d in range(2):
            n = step if d == 0 else NT - 1 - step
            ctx.append(dict(d=d, n=n, cs=slice(n * 128, (n + 1) * 128), b1=p.bank(4 * d), b2=p.bank(4 * d + 1),
                            b3=p.bank(4 * d + 2), v=vn[d][step % 2], kd=kdc[d][step % 4]))
        for x in ctx:
            d, n, cs = x["d"], x["n"], x["cs"]
            MM(p, x["b1"].ap[:, 0:128], wT[d].ap[:, cs], Sbf[d].ap, [(wT[d].key, n // 4), Sbf[d]], [x["b1"]])
            MM(p, x["b2"].ap[:, 0:128], qdec[d].ap[:, cs], Sbf[d].ap, [(qdec[d].key, n // 4), Sbf[d]], [x["b2"]],
               start=True, stop=False)
        for x in ctx:
            d, n = x["d"], x["n"]
            STT(p, "dve", x["v"].ap, x["b1"].ap[:, 0:128], ts4[d][:, n, 1, h:h + 1], ub3[d][:, n, :], ALU.mult, ALU.add,
                [x["b1"], tsd[d], (ub[d].key, n // 4)], [x["v"]])
        for x in ctx:
            d, n = x["d"], x["n"]
            MM(p, x["b3"].ap[:, 0:128], x["kd"].ap, x["v"].ap, [x["kd"], x["v"]], [x["b3"]])
            MM(p, x["b2"].ap[:, 0:128], aT3[d][:, n, :], x["v"].ap, [(aT[d].key, n // 4), x["v"]], [x["b2"]],
               start=False, stop=True)
        for x in ctx:
            d, n = x["d"], x["n"]
            STT(p, "dve", Sbf[d].ap, S32[d].ap, glB[d].ap[:, n:n + 1], x["b3"].ap[:, 0:128], ALU.mult, ALU.add,
                [S32[d], glB[d], x["b3"]], [Sbf[d]])
        for x in ctx:
            d, n = x["d"], x["n"]
            STT(p, "dve", S32[d].ap, S32[d].ap, glB[d].ap[:, n:n + 1], x["b3"].ap[:, 0:128], ALU.mult, ALU.add,
                [S32[d], glB[d], x["b3"]], [S32[d]])
            if step < NT // 2:
                CP(p, "act", of3[:, n, :], x["b2"].ap[:, 0:128], [x["b2"]], [(of.key, n)])
            else:
                TT(p, "dve", of3[:, n, :], x["b2"].ap[:, 0:128], of3[:, n, :], ALU.add, [x["b2"], (of.key, n)], [(of.key, n)])
    sqt = [p.sb("sqt%d" % i, 1024, F32) for i in range(2)]
    ss = p.sb("oss", 32, F32)
    rstd = p.sb("ors", 32, F32)
    onb = p.sb("onb", L, BF16)
    sgp = [p.sb("sgp%d" % i, 1024, BF16) for i in range(2)]
    ysp = [p.sb("ysp%d" % i, 1024, BF16) for i in range(2)]
    for r in range(4):
        sl = slice(r * 1024, (r + 1) * 1024)
        ACTV(p, sqt[r % 2].ap, of.ap[:, sl], AF.Square, [(of.key, n) for n in range(r * 8, r * 8 + 8)], [sqt[r % 2]])
        p.op("dve", lambda e, r=r: e.tensor_reduce(out=ss.ap[:, r * 8:(r + 1) * 8],
                                                    in_=sqt[r % 2].ap.rearrange("p (c i) -> p c i", i=128),
                                                    axis=AX.X, op=ALU.add), [sqt[r % 2]], [ss])
    ACTV(p, rstd.ap, ss.ap, AF.Ln, [ss], [rstd], bias=EPS, scale=1.0 / 128)
    ACTV(p, rstd.ap, rstd.ap, AF.Exp, [rstd], [rstd], scale=-0.5)
    TT(p, "dve", onb.ap.rearrange("p (t i) -> p t i", i=128), of3,
       rstd.ap.rearrange("p (t o) -> p t o", o=1).broadcast_to([128, 32, 128]), ALU.mult,
       [(of.key, n) for n in range(NT)] + [rstd], [onb])
    for r in range(4):
        sl = slice(r * 1024, (r + 1) * 1024)
        bk = p.bank(r % 8, BF16)
        p.dma("sp", sgp[r % 2].ap, s_agate[h * 128:(h + 1) * 128, sl], writes=[sgp[r % 2]])
        for j in range(8):
            t = r * 8 + j
            TR(p, bk.ap[:, j * 128:(j + 1) * 128], onb.ap[:, t * 128:(t + 1) * 128], c.ident.ap, [onb, c.ident], [bk])
        STT(p, "dve", ysp[r % 2].ap, bk.ap, onorm.ap[:, 0:1], sgp[r % 2].ap, ALU.mult, ALU.mult, [bk, onorm, sgp[r % 2]],
            [ysp[r % 2]])
        p.dma("sp", s_y[h * 128:(h + 1) * 128, sl], ysp[r % 2].ap, reads=[ysp[r % 2]])
    p.release(m)


def make_consts():
    i = np.arange(128)
    ident = np.eye(128, dtype=np.float32)
    perm = np.zeros((128, 128), np.float32)
    for d in range(128):
        hh = d // 64
        r = d % 64
        perm[hh * 64 + (r + 32) % 64, d] = 1.0
    NEG = -30000.0
    nmiT_f = np.where(i[None, :] >= i[:, None], 0.0, NEG).astype(np.float32)
    nms_f = np.where(i[:, None] > i[None, :], 0.0, NEG).astype(np.float32)
    nmiT_b = np.where(i[None, :] <= i[:, None], 0.0, NEG).astype(np.float32)
    nms_b = np.where(i[:, None] < i[None, :], 0.0, NEG).astype(np.float32)
    offd = (1.0 - ident).astype(np.float32)
    def bd(sz):
        return ((i[:, None] // sz) == (i[None, :] // sz)).astype(np.float32)
    bd16, bd32, bd64 = bd(16), bd(32), bd(64)
    extra = [bd16, bd32 - bd16, bd64 - bd32, 1.0 - bd64]
    return np.ascontiguousarray(np.concatenate([ident, perm, nmiT_f, nms_f, nmiT_b, nms_b, offd] + extra, axis=1))


def rope_tables():
    t = np.arange(L)
    row = (t // 64).astype(np.float32)
    col = (t % 64).astype(np.float32)
    nf = 32
    inv = (10000.0 ** (-np.arange(nf, dtype=np.float32) / nf)).astype(np.float32)
    ar = (row[:, None] * inv).astype(np.float32)
    ac = (col[:, None] * inv).astype(np.float32)
    C = np.zeros((128, L), np.float32)
    S = np.zeros((128, L), np.float32)
    C[0:32] = np.cos(ar).T; C[32:64] = np.cos(ar).T; C[64:96] = np.cos(ac).T; C[96:128] = np.cos(ac).T
    S[0:32] = -np.sin(ar).T; S[32:64] = np.sin(ar).T; S[64:96] = -np.sin(ac).T; S[96:128] = np.sin(ac).T
    return C, S


def build_full(upto=99, heads=range(8), dbg_out=()):
    nc = bass.Bass("TRN2", target_bir_lowering=False)
    I = lambda n, s, d=F32: nc.dram_tensor(n, s, d, kind="ExternalInput").ap()

    def SC(n, s, d):
        kind = "ExternalOutput" if n in dbg_out else "Internal"
        return nc.dram_tensor(n, s, d, kind=kind).ap()

    x = I("x", [L, D]); norm0_g = I("norm0_g", [1, D]); w_in0 = I("w_in0", [D, 6688])
    convwT = I("convwT", [3072, 5]); a_log = I("a_log0", [2, 8]); dt_bias = I("dt_bias0", [2, 8])
    onorm_g = I("a_onorm_g0", [1, 128]); qn_g = I("b_qnorm_g0", [1, 128]); kn_g = I("b_knorm_g0", [1, 128])
    w_out0 = I("w_out0", [D, D]); norm1_g = I("norm1_g", [1, D]); w_in1 = I("w_in1", [D, 6144])
    ln_g = I("c_ln_g1", [1, D]); ln_b = I("c_ln_b1", [1, D]); wsT = I("c_wsT", [16, 128, 128]); bs = I("c_bs1", [1, D])
    w_out1 = I("w_out1", [D, D]); cst = I("cst", [128, 1408]); ropec = I("rope_c", [128, L]); ropes = I("rope_s", [128, L])
    out = nc.dram_tensor("out", [L, D], F32, kind="ExternalOutput").ap()

    s_aqkv = SC("s_aqkv", [3072, L], BF16); s_agate = SC("s_agate", [1024, L], BF16); s_alog = SC("s_alog", [32, L], F32)
    s_bq = SC("s_bq", [1024, L], BF16); s_bk = SC("s_bk", [256, L], BF16); s_bv = SC("s_bv", [L, 256], BF16)
    s_bgate = SC("s_bgate", [1024, L], BF16); s_qk = SC("s_qk", [1280, L], BF16); s_y = SC("s_y", [2048, L], BF16)
    s_x1 = SC("s_x1", [L, D], F32)
    s_rows = SC("s_rows", [2, 3, 8, L], F32); s_eg = SC("s_eg", [2, 8, L], F32); s_gl = SC("s_gl", [2, 8, 32], F32)
    s_ts = SC("s_ts", [2, 128, 1024], F32)
    s_rows16 = SC("s_rows16", [2, 3, 2, 8, L], BF16)
    s_u = SC("s_u", [D, L], BF16); s_g1 = SC("s_g1", [D, L], BF16); s_gv = SC("s_gv", [L, D], BF16); s_z = SC("s_z", [D, L], BF16)

    p = Prog(nc)
    c = setup_consts(p, cst)
    convw = p.sb("convw", 120, F32)
    p.dma("sp", convw.ap.rearrange("p (b j) -> p b j", j=5), convwT.rearrange("(b p) j -> p b j", p=128), writes=[convw])
    onorm = p.sb("onorm", 1, F32)
    p.dma("sp", onorm.ap, onorm_g.rearrange("o d -> d o"), writes=[onorm])

    m = p.mark()
    hT = p.sb("hT", KC * L, BF16)
    wb0 = [p.sb("w%d" % i, KC * 512, BF16) for i in range(2)]
    w0v = w_in0.rearrange("(k p) c -> p k c", p=128)
    bctr = [0]
    sk_dir = make_fm_direct_sink(p, lambda job, sub: job["dst"][job["r0"] + sub * 128:job["r0"] + sub * 128 + 128, :])
    job0 = dict(c0=0, n=512, mode="fm", sink=sk_dir, dst=s_aqkv, r0=0)
    w30 = proj_load_w(p, wb0[0], w0v, job0)

    def pre0(tb):
        for sub in range(4):
            proj_fm_block(p, hT, wb0[0], w30, job0, sub, tb, bctr)

    stage_norm_T(p, c, x, norm0_g, hT, after_block=pre0)
    sk_plain = make_fm_sink(p, lambda job, sub: job["dst"][job["r0"] + sub * 128:job["r0"] + sub * 128 + min(128, job["n"] - sub * 128), :])
    sk_log = make_fm_sink(p, lambda job, sub: job["dst"][0:32, :], dtype=F32, nbuf=1)
    sk_tm = make_tm_sink(p, lambda job, t: job["dst"][t * 128:(t + 1) * 128, 0:job["n"]])
    jobs = [job0]
    for i in range(1, 6):
        jobs.append(dict(c0=i * 512, n=512, mode="fm", sink=sk_plain, dst=s_aqkv, r0=i * 512))
    for i in range(2):
        jobs.append(dict(c0=3072 + i * 512, n=512, mode="fm", sink=sk_plain, dst=s_agate, r0=i * 512, func=AF.Silu))
    jobs.append(dict(c0=4096, n=32, mode="fm", sink=sk_log, dst=s_alog, r0=0))
    for i in range(2):
        jobs.append(dict(c0=4128 + i * 512, n=512, mode="fm", sink=sk_plain, dst=s_bq, r0=i * 512))
    jobs.append(dict(c0=5152, n=256, mode="fm", sink=sk_plain, dst=s_bk, r0=0))
    jobs.append(dict(c0=5408, n=256, mode="tm", sink=sk_tm, dst=s_bv))
    for i in range(2):
        jobs.append(dict(c0=5664 + i * 512, n=512, mode="fm", sink=sk_plain, dst=s_bgate, r0=i * 512, func=AF.Silu))
    if upto >= 1:
        stage_proj(p, hT, w0v, jobs, bank_ctr=bctr, wb=wb0, skip_first=True)
    p.release(m)
    if upto >= 2:
        stage_qkprep(p, c, s_bq, s_bk, qn_g, kn_g, ropec, ropes, s_qk)
        stage_attn(p, c, s_qk, s_bv, s_bgate, s_y)
    if upto >= 3:
        mg = p.mark()
        setup_gdn_consts(p, c, cst)
        stage_gdn_scalars(p, c, s_alog, a_log, dt_bias, s_rows, s_eg, s_gl, s_ts, s_rows16=s_rows16)
        for h in heads:
            stage_gdn_head2(p, c, h, s_aqkv, s_agate, convw, onorm, s_rows, s_eg, s_gl, s_ts, s_y, s_rows16=s_rows16)
        p.release(mg)
    if upto >= 4:
        stage_outproj(p, s_y, w_out0, x, s_x1)
    if upto >= 5:
        m = p.mark()
        hT = p.sb("hT1", KC * L, BF16)
        wb1 = [p.sb("w1_%d" % i, KC * 512, BF16) for i in range(2)]
        w1v = w_in1.rearrange("(k p) c -> p k c", p=128)
        sk_dir1 = make_fm_direct_sink(p, lambda job, sub: job["dst"][job["r0"] + sub * 128:job["r0"] + sub * 128 + 128, :])
        job10 = dict(c0=0, n=512, mode="fm", sink=sk_dir1, dst=s_u, r0=0, func=AF.Gelu)
        w310 = proj_load_w(p, wb1[0], w1v, job10)

        def pre1(tb):
            for sub in range(4):
                proj_fm_block(p, hT, wb1[0], w310, job10, sub, tb, bctr)

        stage_norm_T(p, c, s_x1, norm1_g, hT, after_block=pre1)
        sk_gelu = make_fm_sink(p, lambda job, sub: job["dst"][job["r0"] + sub * 128:job["r0"] + sub * 128 + 128, :], func=AF.Gelu)
        sk_silu1 = sk_gelu
        sk_tmg = make_tm_sink(p, lambda job, t: job["dst"][t * 128:(t + 1) * 128, job["cc"]:job["cc"] + 512], func=AF.Gelu)
        jobs = [job10]
        for i in range(1, 4):
            jobs.append(dict(c0=i * 512, n=512, mode="fm", sink=sk_gelu, dst=s_u, r0=i * 512))
        for i in range(4):
            jobs.append(dict(c0=2048 + i * 512, n=512, mode="tm", sink=sk_tmg, dst=s_gv, cc=i * 512))
        for i in range(4):
            jobs.append(dict(c0=4096 + i * 512, n=512, mode="fm", sink=sk_silu1, dst=s_g1, r0=i * 512, func=AF.Silu))
        stage_proj(p, hT, w1v, jobs, bank_ctr=bctr, wb=wb1, skip_first=True)
        p.release(m)
    if upto >= 6:
        stage_gmlp_mix(p, c, s_gv, s_u, s_g1, ln_g, ln_b, wsT, bs, s_z)
        stage_outproj(p, s_z, w_out1, s_x1, out)
    p.emit()
    print("ops:", {e: len(p.ops[e]) for e in ENGS}, "waits", p.nwaits, flush=True)
    return nc


def make_in_maps(inputs):
    C, S = rope_tables()
    shared = dict(
        norm0_g=np.ascontiguousarray(inputs["norm0_g"][0:1]), w_in0=np.ascontiguousarray(inputs["w_in0"][0]),
        convwT=np.ascontiguousarray(inputs["conv0_w"][0].T), a_log0=np.ascontiguousarray(inputs["a_log0"][0]),
        dt_bias0=np.ascontiguousarray(inputs["dt_bias0"][0]), a_onorm_g0=np.ascontiguousarray(inputs["a_onorm_g0"][0:1]),
        b_qnorm_g0=np.ascontiguousarray(inputs["b_qnorm_g0"][0:1]), b_knorm_g0=np.ascontiguousarray(inputs["b_knorm_g0"][0:1]),
        w_out0=np.ascontiguousarray(inputs["w_out0"][0]), norm1_g=np.ascontiguousarray(inputs["norm1_g"][0:1]),
        w_in1=np.ascontiguousarray(inputs["w_in1"][0]), c_ln_g1=np.ascontiguousarray(inputs["c_ln_g1"][0:1]),
        c_ln_b1=np.ascontiguousarray(inputs["c_ln_b1"][0:1]),
        c_wsT=np.ascontiguousarray(np.transpose(inputs["c_ws1"][0], (0, 2, 1))),
        c_bs1=np.ascontiguousarray(inputs["c_bs1"][0].reshape(1, D)), w_out1=np.ascontiguousarray(inputs["w_out1"][0]),
        cst=make_consts(), rope_c=C, rope_s=S)
    maps = []
    for b in range(inputs["x"].shape[0]):
        mp = dict(shared)
        mp["x"] = np.ascontiguousarray(inputs["x"][b])
        maps.append(mp)
    return maps


def kernel(**inputs):
    from concourse.bass_utils import run_bass_kernel_spmd
    inputs = {k: np.asarray(v, dtype=np.float32) for k, v in inputs.items()}
    nc = build_full()
    maps = make_in_maps(inputs)
    res = run_bass_kernel_spmd(nc, maps, core_ids=list(range(len(maps))))
    return np.stack([np.asarray(r["out"], dtype=np.float32) for r in res.results], axis=0)
```
